# Optimizing a Trainium2 kernel written in Bass

```python
import math
import jax, jax.numpy as jnp
from jax import lax
import numpy as np

D_MODEL = 1024
BATCH = 8
SEQ = 4096
DEPTH = 2

DA_HEADS = 4
DA_HEAD_DIM = 64
DA_V_DIM = 2 * DA_HEAD_DIM
DA_WIDTH = DA_HEADS * DA_V_DIM
NSA_HEADS = 8
NSA_KV_GROUPS = 2
NSA_HPG = NSA_HEADS // NSA_KV_GROUPS
NSA_HEAD_DIM = 64
NSA_WIDTH = NSA_HEADS * NSA_HEAD_DIM
CMP_BLOCK = 32
CMP_STRIDE = 16
CMP_HIDDEN = 128
SEL_BLOCK = 64
SEL_TOPK = 8
SEL_FORCE = 1e4
WINDOW = 512
N_BUCKETS = 32
MAX_DISTANCE = 1024
TOTAL_HEADS = DA_HEADS + NSA_HEADS
D_FF = 2816
Q_BLOCK = 128
EPS = 1e-6
NEG_INF = -1e30
NSA_KV = NSA_KV_GROUPS * NSA_HEAD_DIM
PROJ_SIZES = (
    DA_HEADS * 2 * DA_HEAD_DIM,
    DA_HEADS * 2 * DA_HEAD_DIM,
    DA_WIDTH,
    NSA_WIDTH,
    NSA_KV, NSA_KV,
    NSA_KV, NSA_KV,
    NSA_KV, NSA_KV,
    3 * NSA_HEADS,
    2 * D_MODEL,
)
N_IN = sum(PROJ_SIZES)

kernel_name = "hybrid_diffattn_nsa_macaron"


def rmsnorm(x, g):
    xf = x.astype(jnp.float32)
    y = xf * lax.rsqrt(jnp.mean(xf * xf, axis=-1, keepdims=True) + EPS)
    return (y * g.astype(jnp.float32)).astype(x.dtype)


def swiglu(h, w1, w3, w2):
    return (jax.nn.silu(h @ w1) * (h @ w3)) @ w2


def masked_softmax(logits, mask):
    s = jnp.where(mask, logits.astype(jnp.float32), NEG_INF)
    p = jax.nn.softmax(s, axis=-1)
    return jnp.where(mask, p, 0.0)


def rel_bucket(dist):
    n = jnp.maximum(dist, 0)
    max_exact = N_BUCKETS // 2
    nf = jnp.maximum(n, 1).astype(jnp.float32)
    large = max_exact + (jnp.log(nf / max_exact) / math.log(MAX_DISTANCE / max_exact)
                         * (N_BUCKETS - max_exact)).astype(jnp.int32)
    large = jnp.minimum(large, N_BUCKETS - 1)
    return jnp.where(n < max_exact, n, large)


def diff_attention(q, k, v, lam, table):
    B, S = q.shape[:2]
    nb = S // Q_BLOCK
    scale = DA_HEAD_DIM ** -0.5
    kpos = jnp.arange(S)
    qb = q.reshape(B, nb, Q_BLOCK, DA_HEADS, 2, DA_HEAD_DIM).swapaxes(0, 1)

    def block(args):
        qi, i = args
        qpos = i * Q_BLOCK + jnp.arange(Q_BLOCK)
        dist = qpos[:, None] - kpos[None, :]
        bias = jnp.take(table, rel_bucket(dist), axis=0).transpose(2, 0, 1)
        s = jnp.einsum('bqhcd,bkhcd->bhcqk', qi, k).astype(jnp.float32) * scale
        s = s + bias[None, :, None].astype(jnp.float32)
        p = masked_softmax(s, dist >= 0)
        a = p[:, :, 0] - lam * p[:, :, 1]
        return jnp.einsum('bhqk,bkhe->bqhe', a.astype(v.dtype), v)

    out = lax.map(block, (qb, jnp.arange(nb)))
    return out.swapaxes(0, 1).reshape(B, S, DA_HEADS, DA_V_DIM)


def compress(x, pe, w1, w2):
    B, S, G, d = x.shape
    ratio = CMP_BLOCK // CMP_STRIDE
    n_cmp = S // CMP_STRIDE - ratio + 1
    c = x.reshape(B, S // CMP_STRIDE, CMP_STRIDE, G, d)
    blocks = jnp.concatenate([c[:, r:r + n_cmp] for r in range(ratio)], axis=2)
    blocks = blocks + pe[None, None, :, None, :]
    flat = blocks.transpose(0, 1, 3, 2, 4).reshape(B, n_cmp, G, CMP_BLOCK * d)
    return jax.nn.gelu(flat @ w1) @ w2


def nsa_attention(q, kc, vc, ks, vs, kw, vw, gates, table):
    B, S = q.shape[:2]
    G, P, d = NSA_KV_GROUPS, NSA_HPG, NSA_HEAD_DIM
    nb = S // Q_BLOCK
    n_cmp = kc.shape[1]
    n_sel = S // SEL_BLOCK
    topk = min(SEL_TOPK, n_sel)
    scale = d ** -0.5
    table_h = table.reshape(N_BUCKETS, G, P)
    cmp_start = jnp.arange(n_cmp) * CMP_STRIDE
    cmp_end = cmp_start + CMP_BLOCK - 1
    sel_j = jnp.arange(n_sel)
    overlap = ((cmp_start[:, None] < (sel_j[None, :] + 1) * SEL_BLOCK)
               & (cmp_end[:, None] >= sel_j[None, :] * SEL_BLOCK)).astype(jnp.float32)
    ks_blocks = ks.reshape(B, n_sel, SEL_BLOCK, G, d).transpose(0, 3, 1, 2, 4)
    vs_blocks = vs.reshape(B, n_sel, SEL_BLOCK, G, d).transpose(0, 3, 1, 2, 4)
    kw_pad = jnp.pad(kw, ((0, 0), (WINDOW, 0), (0, 0), (0, 0)))
    vw_pad = jnp.pad(vw, ((0, 0), (WINDOW, 0), (0, 0), (0, 0)))
    gather = jax.vmap(jax.vmap(lambda t, i: t[i]))
    group_bias = jax.vmap(lambda t, bk: t[bk], in_axes=(1, 1), out_axes=1)
    qb = q.reshape(B, nb, Q_BLOCK, G, P, d).swapaxes(0, 1)
    gb = gates.reshape(B, nb, Q_BLOCK, G, P, 3).swapaxes(0, 1)
    n_keys = topk * SEL_BLOCK

    def block(args):
        qi, gi, i = args
        qpos = i * Q_BLOCK + jnp.arange(Q_BLOCK)
        dist_c = qpos[:, None] - cmp_end[None, :]
        bias_c = table_h[rel_bucket(dist_c)].transpose(2, 3, 0, 1)
        s_c = jnp.einsum('bqgpd,bcgd->bgpqc', qi, kc).astype(jnp.float32) * scale + bias_c
        p_c = masked_softmax(s_c, dist_c >= 0)
        o_c = jnp.einsum('bgpqc,bcgd->bqgpd', p_c.astype(vc.dtype), vc)
        imp = jnp.einsum('bgpqc,cj->bgqj', p_c, overlap)
        cur = qpos // SEL_BLOCK
        forced = ((sel_j[None, :] == 0) | (sel_j[None, :] == cur[:, None])
                  | (sel_j[None, :] == cur[:, None] - 1))
        future = sel_j[None, :] > cur[:, None]
        score = jnp.where(future, -SEL_FORCE, imp + jnp.where(forced, SEL_FORCE, 0.0))
        _, idx = lax.top_k(score, topk)
        k_g = gather(ks_blocks, idx).reshape(B, G, Q_BLOCK, n_keys, d)
        v_g = gather(vs_blocks, idx).reshape(B, G, Q_BLOCK, n_keys, d)
        tpos = (idx[..., None] * SEL_BLOCK + jnp.arange(SEL_BLOCK)).reshape(B, G, Q_BLOCK, n_keys)
        dist_s = qpos[None, None, :, None] - tpos
        bias_s = group_bias(table_h, rel_bucket(dist_s)).transpose(0, 1, 4, 2, 3)
        s_s = jnp.einsum('bqgpd,bgqkd->bgpqk', qi, k_g).astype(jnp.float32) * scale + bias_s
        p_s = masked_softmax(s_s, (dist_s >= 0)[:, :, None])
        o_s = jnp.einsum('bgpqk,bgqkd->bqgpd', p_s.astype(v_g.dtype), v_g)
        kw_i = lax.dynamic_slice_in_dim(kw_pad, i * Q_BLOCK, Q_BLOCK + WINDOW, axis=1)
        vw_i = lax.dynamic_slice_in_dim(vw_pad, i * Q_BLOCK, Q_BLOCK + WINDOW, axis=1)
        kpos_w = i * Q_BLOCK - WINDOW + jnp.arange(Q_BLOCK + WINDOW)
        dist_w = qpos[:, None] - kpos_w[None, :]
        mask_w = (dist_w >= 0) & (dist_w < WINDOW) & (kpos_w[None, :] >= 0)
        bias_w = table_h[rel_bucket(dist_w)].transpose(2, 3, 0, 1)
        s_w = jnp.einsum('bqgpd,bkgd->bgpqk', qi, kw_i).astype(jnp.float32) * scale + bias_w
        p_w = masked_softmax(s_w, mask_w)
        o_w = jnp.einsum('bgpqk,bkgd->bqgpd', p_w.astype(vw_i.dtype), vw_i)
        return gi[..., 0:1] * o_c + gi[..., 1:2] * o_s + gi[..., 2:3] * o_w

    out = lax.map(block, (qb, gb, jnp.arange(nb)))
    return out.swapaxes(0, 1).reshape(B, S, NSA_WIDTH)


def setup_inputs(seed: int = 0) -> dict:
    key = jax.random.key(seed)
    ks = jax.random.split(key, 30)
    f32 = jnp.float32

    def nrm(k, shape, scale):
        return jax.random.normal(k, shape, f32) * scale

    def gain(k, shape):
        return 1.0 + 0.05 * jax.random.normal(k, shape, f32)

    d = D_MODEL
    return {
        "x": nrm(ks[0], (BATCH, SEQ, d), 1.0),
        "w_in": nrm(ks[1], (DEPTH, d, N_IN), d ** -0.5),
        "w_branch_a": nrm(ks[2], (DEPTH, DA_WIDTH, d), DA_WIDTH ** -0.5),
        "w_branch_b": nrm(ks[3], (DEPTH, NSA_WIDTH, d), NSA_WIDTH ** -0.5),
        "w_out": nrm(ks[4], (DEPTH, d, d), d ** -0.5),
        "norm_ffn1": gain(ks[5], (DEPTH, d)),
        "norm_mix": gain(ks[6], (DEPTH, d)),
        "norm_ffn2": gain(ks[7], (DEPTH, d)),
        "ffn1_w1": nrm(ks[8], (DEPTH, d, D_FF), d ** -0.5),
        "ffn1_w3": nrm(ks[9], (DEPTH, d, D_FF), d ** -0.5),
        "ffn1_w2": nrm(ks[10], (DEPTH, D_FF, d), D_FF ** -0.5),
        "ffn2_w1": nrm(ks[11], (DEPTH, d, D_FF), d ** -0.5),
        "ffn2_w3": nrm(ks[12], (DEPTH, d, D_FF), d ** -0.5),
        "ffn2_w2": nrm(ks[13], (DEPTH, D_FF, d), D_FF ** -0.5),
        "da_q_gain": gain(ks[14], (DEPTH, DA_HEAD_DIM)),
        "da_k_gain": gain(ks[15], (DEPTH, DA_HEAD_DIM)),
        "da_lambda_q1": nrm(ks[16], (DEPTH, DA_HEAD_DIM), 0.1),
        "da_lambda_k1": nrm(ks[17], (DEPTH, DA_HEAD_DIM), 0.1),
        "da_lambda_q2": nrm(ks[18], (DEPTH, DA_HEAD_DIM), 0.1),
        "da_lambda_k2": nrm(ks[19], (DEPTH, DA_HEAD_DIM), 0.1),
        "da_subln_gain": gain(ks[20], (DEPTH, DA_V_DIM)),
        "nsa_q_gain": gain(ks[21], (DEPTH, NSA_HEAD_DIM)),
        "nsa_k_gain": gain(ks[22], (DEPTH, 3, NSA_HEAD_DIM)),
        "cmp_pe_k": nrm(ks[23], (DEPTH, CMP_BLOCK, NSA_HEAD_DIM), 0.1),
        "cmp_w1_k": nrm(ks[24], (DEPTH, CMP_BLOCK * NSA_HEAD_DIM, CMP_HIDDEN), (CMP_BLOCK * NSA_HEAD_DIM) ** -0.5),
        "cmp_w2_k": nrm(ks[25], (DEPTH, CMP_HIDDEN, NSA_HEAD_DIM), CMP_HIDDEN ** -0.5),
        "cmp_pe_v": nrm(ks[26], (DEPTH, CMP_BLOCK, NSA_HEAD_DIM), 0.1),
        "cmp_w1_v": nrm(ks[27], (DEPTH, CMP_BLOCK * NSA_HEAD_DIM, CMP_HIDDEN), (CMP_BLOCK * NSA_HEAD_DIM) ** -0.5),
        "cmp_w2_v": nrm(ks[28], (DEPTH, CMP_HIDDEN, NSA_HEAD_DIM), CMP_HIDDEN ** -0.5),
        "rel_bias_table": nrm(ks[29], (N_BUCKETS, TOTAL_HEADS), 0.3),
    }


def reference(x, w_in, w_branch_a, w_branch_b, w_out, norm_ffn1, norm_mix, norm_ffn2,
              ffn1_w1, ffn1_w3, ffn1_w2, ffn2_w1, ffn2_w3, ffn2_w2,
              da_q_gain, da_k_gain, da_lambda_q1, da_lambda_k1, da_lambda_q2, da_lambda_k2,
              da_subln_gain, nsa_q_gain, nsa_k_gain,
              cmp_pe_k, cmp_w1_k, cmp_w2_k, cmp_pe_v, cmp_w1_v, cmp_w2_v,
              rel_bias_table):
    B, S, D = x.shape
    G, P, dh = NSA_KV_GROUPS, NSA_HPG, NSA_HEAD_DIM
    offsets = [int(o) for o in np.cumsum(PROJ_SIZES)[:-1]]
    table_a = rel_bias_table[:, :DA_HEADS]
    table_b = rel_bias_table[:, DA_HEADS:]
    for l in range(DEPTH):
        h = rmsnorm(x, norm_ffn1[l])
        x = x + 0.5 * swiglu(h, ffn1_w1[l], ffn1_w3[l], ffn1_w2[l])
        h = rmsnorm(x, norm_mix[l])
        z = h @ w_in[l]
        (dq, dk, dv, nq, kc, vc, ksel, vsel, kwin, vwin, ng, mg) = jnp.split(z, offsets, axis=-1)
        dq = rmsnorm(dq.reshape(B, S, DA_HEADS, 2, DA_HEAD_DIM), da_q_gain[l])
        dk = rmsnorm(dk.reshape(B, S, DA_HEADS, 2, DA_HEAD_DIM), da_k_gain[l])
        dv = dv.reshape(B, S, DA_HEADS, DA_V_DIM)
        lam_init = 0.8 - 0.6 * math.exp(-0.3 * l)
        lam = (jnp.exp(jnp.sum(da_lambda_q1[l].astype(jnp.float32) * da_lambda_k1[l].astype(jnp.float32)))
               - jnp.exp(jnp.sum(da_lambda_q2[l].astype(jnp.float32) * da_lambda_k2[l].astype(jnp.float32)))
               + lam_init)
        ya = diff_attention(dq, dk, dv, lam, table_a)
        ya = (rmsnorm(ya, da_subln_gain[l]) * (1.0 - lam_init)).reshape(B, S, DA_WIDTH)
        nq = rmsnorm(nq.reshape(B, S, G, P, dh), nsa_q_gain[l])
        kc = rmsnorm(compress(kc.reshape(B, S, G, dh), cmp_pe_k[l], cmp_w1_k[l], cmp_w2_k[l]), nsa_k_gain[l, 0])
        vc = compress(vc.reshape(B, S, G, dh), cmp_pe_v[l], cmp_w1_v[l], cmp_w2_v[l])
        ksel = rmsnorm(ksel.reshape(B, S, G, dh), nsa_k_gain[l, 1])
        vsel = vsel.reshape(B, S, G, dh)
        kwin = rmsnorm(kwin.reshape(B, S, G, dh), nsa_k_gain[l, 2])
        vwin = vwin.reshape(B, S, G, dh)
        ng = jax.nn.sigmoid(ng.reshape(B, S, G, P, 3))
        yb = nsa_attention(nq, kc, vc, ksel, vsel, kwin, vwin, ng, table_b)
        mg = jax.nn.sigmoid(mg.reshape(B, S, 2, D))
        merged = mg[:, :, 0] * (ya @ w_branch_a[l]) + mg[:, :, 1] * (yb @ w_branch_b[l])
        x = x + merged @ w_out[l]
        h = rmsnorm(x, norm_ffn2[l])
        x = x + 0.5 * swiglu(h, ffn2_w1[l], ffn2_w3[l], ffn2_w2[l])
    return x
```

```python
import numpy as np
from contextlib import ExitStack
import concourse.bass as bass
import concourse.mybir as mybir
from concourse.bass_utils import run_bass_kernel_spmd

F32 = mybir.dt.float32
BF16 = mybir.dt.bfloat16
AF = mybir.ActivationFunctionType
ALU = mybir.AluOpType
AX = mybir.AxisListType

S = 4096
D = 1024
DFF = 2816
NIN = 4888
DEPTH = 2
NCORES = 8
EPS = 1e-6
NEG = -30000.0
STRICT = True


class Buf:
    __slots__ = ("name", "writers", "readers")

    def __init__(self, name):
        self.name = name
        self.writers = []
        self.readers = []


class Op:
    __slots__ = ("eng", "fn", "deps", "needed", "semval", "stream", "dma_waits", "sidx")

    def __init__(self, eng, fn, stream=None):
        self.eng = eng
        self.fn = fn
        self.deps = []
        self.needed = False
        self.semval = None
        self.stream = stream
        self.dma_waits = {}
        self.sidx = None


class Stream:
    __slots__ = ("name", "queue", "count", "waiters", "sem")

    def __init__(self, name, queue):
        self.name = name
        self.queue = queue
        self.count = 0
        self.waiters = []
        self.sem = None


ENGS = ("pe", "act", "dve", "pool", "sp")


class Prog:
    def __init__(self, nc, es):
        self.nc = nc
        self.es = es
        self.ops = {e: [] for e in ENGS}
        self.streams = {}
        self.nops = 0

    def sb(self, name, shape, dtype):
        t = self.es.enter_context(self.nc.sbuf_tensor(name, list(shape), dtype))
        return t

    def ps(self, name, shape=(128, 512), dtype=F32):
        t = self.es.enter_context(self.nc.psum_tensor(name, list(shape), dtype))
        return t

    def _dep(self, op, other, raw=False):
        if other is None or other is op:
            return
        if other.stream is not None:
            st = other.stream
            op.dma_waits[st.name] = st.count
            if op not in st.waiters:
                st.waiters.append(op)
            return
        if other.eng == op.eng and op.stream is None and (op.eng == "pe" or not (raw or STRICT)):
            return
        other.needed = True
        op.deps.append(other)

    def op(self, eng, fn, reads=(), writes=(), stream=None, pwrites=()):
        st = None
        if stream is not None:
            st = self.streams.get(stream)
            if st is None:
                st = Stream(stream, eng)
                self.streams[stream] = st
            assert st.queue == eng, (stream, st.queue, eng)
        o = Op(eng, fn, st)
        for b in reads:
            for w in b.writers:
                self._dep(o, w, raw=True)
        for b in writes:
            for w in b.writers:
                self._dep(o, w)
            for r in b.readers:
                self._dep(o, r)
        for b in pwrites:
            for r in b.readers:
                self._dep(o, r)
        if st is not None:
            for w in st.waiters:
                if w.eng == eng:
                    continue
                self._dep(o, w)
            st.waiters = []
            st.count += 1
            o.sidx = st.count
        for b in reads:
            b.readers.append(o)
        for b in writes:
            b.writers = [o]
            b.readers = []
        for b in pwrites:
            if b.readers:
                b.writers = []
                b.readers = []
            b.writers.append(o)
        self.ops[eng].append(o)
        self.nops += 1
        return o

    def pe(self, fn, reads=(), writes=()):
        return self.op("pe", fn, reads, writes)

    def act(self, fn, reads=(), writes=()):
        return self.op("act", fn, reads, writes)

    def dve(self, fn, reads=(), writes=()):
        return self.op("dve", fn, reads, writes)

    def pool(self, fn, reads=(), writes=()):
        return self.op("pool", fn, reads, writes)

    def dma(self, queue, out, in_, reads, writes, stream, pwrites=()):
        return self.op(queue, lambda e: e.dma_start(out=out, in_=in_), reads, writes, stream, pwrites)

    def emit(self):
        nc = self.nc
        es = self.es
        esem = {e: es.enter_context(nc.semaphore("sem_" + e)) for e in ENGS}
        for st in self.streams.values():
            st.sem = es.enter_context(nc.semaphore("dq_" + st.name))
        for e in ENGS:
            c = 0
            for o in self.ops[e]:
                if o.stream is None and o.needed:
                    c += 1
                    o.semval = c
        streams = self.streams
        ops = self.ops

        def run(e, eng):
            waited = {}
            for o in ops[e]:
                w = {}
                for d in o.deps:
                    s = esem[d.eng]
                    k = id(s)
                    if k not in w or w[k][1] < d.semval:
                        w[k] = (s, d.semval)
                for sn, cnt in o.dma_waits.items():
                    s = streams[sn].sem
                    k = id(s)
                    v = cnt * 16
                    if k not in w or w[k][1] < v:
                        w[k] = (s, v)
                for k, (s, v) in w.items():
                    if waited.get(k, 0) < v:
                        eng.wait_ge(s, v)
                        waited[k] = v
                ins = o.fn(eng)
                if o.stream is not None:
                    ins.then_inc(o.stream.sem, 16)
                elif o.needed:
                    ins.then_inc(esem[e], 1)
            if e == "sp":
                for st in streams.values():
                    if st.count:
                        eng.wait_ge(st.sem, st.count * 16)

        blk = es.enter_context(nc.Block())

        @blk.tensor
        def _(eng):
            run("pe", eng)

        @blk.scalar
        def _(eng):
            run("act", eng)

        @blk.vector
        def _(eng):
            run("dve", eng)

        @blk.gpsimd
        def _(eng):
            run("pool", eng)

        @blk.sync
        def _(eng):
            run("sp", eng)


class T:
    __slots__ = ("ap", "buf")

    def __init__(self, ap, name):
        self.ap = ap
        self.buf = Buf(name)


ARENA_WORDS = 51000


class Builder:
    def __init__(self, dbg_in=(), dbg_out=()):
        self.nc = bass.Bass("TRN2", target_bir_lowering=False)
        self.es = ExitStack()
        self.p = Prog(self.nc, self.es)
        self.dbg_in = set(dbg_in)
        self.dbg_out = set(dbg_out)
        self.dram = {}
        self.dbuf = {}
        self.arena = self.p.sb("arena", [128, ARENA_WORDS], F32)
        self.aoff = 0
        self.abase = 0
        self.psum = [T(self.p.ps("psum%d" % i)[:], "psum%d" % i) for i in range(8)]
        self.ntile = 0

    def dt(self, name, shape, dtype, kind=None):
        if kind is None:
            kind = "Internal"
            if name in self.dbg_in:
                kind = "ExternalInput"
            elif name in self.dbg_out:
                kind = "ExternalOutput"
        t = self.nc.dram_tensor(name, list(shape), dtype, kind=kind).ap()
        self.dram[name] = t
        return t

    def db(self, name, n=1):
        if name not in self.dbuf:
            self.dbuf[name] = [Buf("%s_%d" % (name, i)) for i in range(n)]
        return self.dbuf[name]

    def tile(self, name, free, dtype, parts=128):
        n = int(np.prod(free))
        words = n if dtype == F32 else (n + 1) // 2
        assert self.aoff + words <= ARENA_WORDS, (name, self.aoff, words)
        ap = self.arena[0:parts, self.aoff:self.aoff + words]
        if dtype != F32:
            ap = ap.bitcast(dtype)
            if n % 2:
                ap = ap[:, 0:n]
        self.aoff += words
        self.apeak = max(getattr(self, 'apeak', 0), self.aoff)
        if len(free) == 2:
            ap = ap.rearrange("p (a b) -> p a b", a=free[0])
        elif len(free) == 3:
            ap = ap.rearrange("p (a b c) -> p a b c", a=free[0], b=free[1])
        self.ntile += 1
        return T(ap, "%s_%d" % (name, self.ntile))

    def persist(self):
        self.abase = self.aoff

    def phase_end(self):
        p = self.p
        last = {}
        for e in ENGS:
            last[e] = None
            for o in reversed(p.ops[e]):
                if o.stream is None:
                    last[e] = o
                    break
        for e in ENGS:
            o = Op(e, lambda eng: eng.nop())
            for e2 in ENGS:
                l = last[e2]
                if e2 == e or l is None:
                    continue
                if l.stream is None:
                    l.needed = True
                    o.deps.append(l)
            for st in p.streams.values():
                if st.count and not st.name.startswith("cv"):
                    o.dma_waits[st.name] = st.count
                    st.waiters.append(o)
            p.ops[e].append(o)
        self.aoff = self.abase

    PK = {}
    NPK = 0

    @classmethod
    def pk_add(cls, name, width):
        cls.PK[name] = (cls.NPK, width)
        cls.NPK += width

    def pk(self, name):
        o, w = self.PK[name]
        return self.pkt.ap[:, o:o + w]

    def mm(self, pt, out, lhsT, rhs, start, stop, rb):
        self.p.pe(lambda e: e.matmul(out, lhsT=lhsT, rhs=rhs, start=start, stop=stop), rb, [pt.buf])

    def castload(self, name, free, src, parts=128):
        t = self.tile(name, free, BF16)
        self.p.dma("pool", t.ap[0:parts], src, [], [t.buf], "cl")
        return t

    def prologue(self, wnames, full=True):
        p = self.p
        nc = self.nc
        self.x_in = self.dt("x", [S, D], F32, kind="ExternalInput")
        self.pk_in = self.dt("pk", [128, self.NPK], F32, kind="ExternalInput")
        self.cst_in = self.dt("cst", [128, NCST], F32, kind="ExternalInput")
        self.w_in = {}
        self.w_bf = {}
        for name, shape in WSHAPES.items():
            if name not in wnames:
                continue
            self.w_in[name] = self.dt(name, [DEPTH] + list(shape), F32, kind="ExternalInput")
            self.w_bf[name] = self.dt(name + "_bf", [DEPTH] + list(shape), BF16)
        self.pkt = self.tile("pkt", [self.NPK], F32)
        p.dma("sp", self.pkt.ap, self.pk_in, [], [self.pkt.buf], "misc")
        self.ident = self.tile("ident", [128], F32)
        p.dma("sp", self.ident.ap, self.cst_in[:, C_ID:C_ID + 128], [], [self.ident.buf], "misc")
        self.identb = self.castload("identb", [128], self.cst_in[:, C_ID:C_ID + 128])
        self.J = self.castload("J", [128], self.cst_in[:, C_J:C_J + 128])
        self.BO = self.castload("BO", [128], self.cst_in[:, C_BO:C_BO + 128])
        self.OZ = self.castload("OZ", [192], self.cst_in[:, C_OZ:C_OZ + 192])
        self.ones = self.tile("ones", [128], BF16)
        p.pool(lambda e: e.memset(self.ones.ap, 1.0), [], [self.ones.buf])
        self.epsc = self.tile("epsc", [1], F32)
        p.pool(lambda e: e.memset(self.epsc.ap, EPS), [], [self.epsc.buf])
        self.eps64 = self.tile("eps64", [1], F32)
        p.pool(lambda e: e.memset(self.eps64.ap, 64 * EPS), [], [self.eps64.buf])
        self.persist()
        order = ["ffn1_w1", "ffn1_w3", "ffn1_w2", "w_in", "w_branch_a", "w_branch_b", "w_out",
                 "ffn2_w1", "ffn2_w3", "ffn2_w2"]
        for l in range(DEPTH):
            for name in order:
                if name not in self.w_in:
                    continue
                src = self.w_in[name][l].rearrange("(p r) c -> p (r c)", p=128)
                dst = self.w_bf[name][l].rearrange("(p r) c -> p (r c)", p=128)
                b = self.db(name + "_bf", DEPTH)[l]
                grp = "A" if name.startswith("ffn1") else ("C" if name.startswith("ffn2") else "B")
                p.dma("pool", dst, src, [], [b], "cv%s%d" % (grp, l))
        if not full:
            return
        self.oh_in = self.dt("oh", [33, LF], F32, kind="ExternalInput")
        self.tblx_in = self.dt("tblx", [33, 12], F32, kind="ExternalInput")
        self.Fd = self.dt("Fd", [12, LF], BF16)
        fb = self.db("Fd")[0]
        tb = self.tile("tblx", [12], F32)
        p.dma("sp", tb.ap[0:33], self.tblx_in, [], [tb.buf], "misc")
        OH = [self.tile("OH", [512], F32) for _ in range(2)]
        FS = [self.tile("FS", [512], BF16) for _ in range(2)]
        for c in range(LF // 512):
            oh = OH[c % 2]
            fs = FS[c % 2]
            ps = self.psum[c % 2]
            p.dma("sp", oh.ap[0:33], self.oh_in[:, c * 512:(c + 1) * 512], [], [oh.buf], "OH%d" % (c % 2))
            self.mm(ps, ps.ap[0:12, :], tb.ap[0:33, :], oh.ap[0:33, :], True, True, [tb.buf, oh.buf])
            p.act(lambda e, fs=fs, ps=ps: e.activation(out=fs.ap[0:12], in_=ps.ap[0:12, :], func=AF.Copy),
                  [ps.buf], [fs.buf])
            p.dma("sp", self.Fd[:, c * 512:(c + 1) * 512], fs.ap[0:12], [fs.buf], [], "FS%d" % (c % 2),
                  pwrites=[fb])
        self.phase_end()

    def transpose_in(self, xT):
        p = self.p
        xb = self.db(xT.name, 8)
        A = [self.tile("A", [4, D], F32) for _ in range(2)]
        O = [self.tile("O", [8, 512], F32) for _ in range(2)]
        xv = self.x_in.rearrange("(c r p) d -> c p r d", p=128, r=4)
        ov = xT.rearrange("(kc p) t -> p kc t", p=128)
        n = 0
        for c in range(8):
            a = A[c % 2]
            o = O[c % 2]
            p.dma("sp" if c % 2 == 0 else "act", a.ap, xv[c], [], [a.buf], "A%d" % (c % 2))
            for kc in range(8):
                ps = self.psum[n % 4]
                n += 1
                for r in range(4):
                    p.pe(lambda e, ps=ps, a=a, r=r, kc=kc: e.transpose(
                        out=ps.ap[:, r * 128:(r + 1) * 128], in_=a.ap[:, r, kc * 128:(kc + 1) * 128],
                        identity=self.ident.ap), [a.buf, self.ident.buf], [ps.buf])
                if kc % 2 == 0:
                    p.act(lambda e, ps=ps, o=o, kc=kc: e.activation(out=o.ap[:, kc, :], in_=ps.ap, func=AF.Copy),
                          [ps.buf], [o.buf])
                else:
                    p.dve(lambda e, ps=ps, o=o, kc=kc: e.tensor_copy(out=o.ap[:, kc, :], in_=ps.ap),
                          [ps.buf], [o.buf])
            p.dma("sp", ov[:, :, c * 512:(c + 1) * 512], o.ap, [o.buf], [xb[c]], "O%d" % (c % 2))
        self.phase_end()

    def transpose_out(self, xT, out):
        p = self.p
        xb = self.db(xT.name, 8)
        A = [self.tile("A", [8, 512], F32) for _ in range(2)]
        O = [self.tile("O", [4, D], F32) for _ in range(2)]
        xv = xT.rearrange("(kc p) t -> p kc t", p=128)
        ov = out.rearrange("(c r p) d -> c p r d", p=128, r=4)
        n = 0
        for c in range(8):
            a = A[c % 2]
            o = O[c % 2]
            p.dma("sp" if c % 2 == 0 else "act", a.ap, xv[:, :, c * 512:(c + 1) * 512], [xb[c]], [a.buf],
                  "A%d" % (c % 2))
            for r in range(4):
                for half in range(2):
                    ps = self.psum[n % 4]
                    n += 1
                    for k4 in range(4):
                        kc = half * 4 + k4
                        p.pe(lambda e, ps=ps, a=a, r=r, kc=kc, k4=k4: e.transpose(
                            out=ps.ap[:, k4 * 128:(k4 + 1) * 128], in_=a.ap[:, kc, r * 128:(r + 1) * 128],
                            identity=self.ident.ap), [a.buf, self.ident.buf], [ps.buf])
                    if half == 0:
                        p.act(lambda e, ps=ps, o=o, r=r: e.activation(out=o.ap[:, r, 0:512], in_=ps.ap,
                                                                       func=AF.Copy), [ps.buf], [o.buf])
                    else:
                        p.dve(lambda e, ps=ps, o=o, r=r: e.tensor_copy(out=o.ap[:, r, 512:1024], in_=ps.ap),
                              [ps.buf], [o.buf])
            p.dma("sp", ov[c], o.ap, [o.buf], [], "O%d" % (c % 2), pwrites=[self.db("out")[0]])
        self.phase_end()

    def ffn_phase(self, l, which, xin, xout):
        p = self.p
        TT = 512
        NCH = S // TT
        KC = D // 128
        FC = DFF // 128
        pre = "ffn%d_" % which
        w1d = self.w_bf[pre + "w1"][l]
        w3d = self.w_bf[pre + "w3"][l]
        w2d = self.w_bf[pre + "w2"][l]
        w1b = self.db(pre + "w1_bf", DEPTH)[l]
        w3b = self.db(pre + "w3_bf", DEPTH)[l]
        w2b = self.db(pre + "w2_bf", DEPTH)[l]
        xin_b = self.db(xin.name, NCH)
        xout_b = self.db(xout.name, NCH)
        gain = self.pk("g_ffn%d_l%d" % (which, l))
        xin_v = xin.rearrange("(kc p) t -> p kc t", p=128)
        xout_v = xout.rearrange("(kc p) t -> p kc t", p=128)

        X = [self.tile("X", [KC, TT], F32) for _ in range(2)]
        H = [self.tile("H", [KC, TT], BF16) for _ in range(2)]
        U = self.tile("U", [FC, TT], BF16)
        W2 = self.tile("W2", [FC, D], BF16)
        GW = 512
        groups = [(g0, min(GW, DFF - g0)) for g0 in range(0, DFF, GW)]
        W1 = [self.tile("W1", [KC, GW], BF16) for _ in range(2)]
        W3 = [self.tile("W3", [KC, GW], BF16) for _ in range(2)]
        SQ = [self.tile("SQ", [TT], BF16) for _ in range(2)]
        SA = [self.tile("SA", [TT], F32) for _ in range(2)]
        RS = self.tile("RS", [TT], F32)
        ps_st = self.psum[0]
        ps_a = [self.psum[1], self.psum[2]]
        ps_b = [self.psum[3], self.psum[4]]
        ps_y = [self.psum[5], self.psum[6]]

        w2v = w2d.rearrange("(fc p) d -> p fc d", p=128)
        for i, (a, b) in enumerate([(0, 6), (6, 12), (12, 17), (17, 22)]):
            q = "sp" if i % 2 == 0 else "act"
            p.dma(q, W2.ap[:, a:b, :], w2v[:, a:b, :], [w2b], [], "W2_" + q, pwrites=[W2.buf])

        def load_x(c):
            t = X[c % 2]
            p.dma("sp", t.ap, xin_v[:, :, c * TT:(c + 1) * TT], [xin_b[c]], [t.buf], "X%d" % (c % 2))

        def norm(c):
            x = X[c % 2]
            h = H[c % 2]
            for kc in range(KC):
                sq = SQ[kc % 2]
                p.act(lambda e, sq=sq, kc=kc: e.activation(out=sq.ap, in_=x.ap[:, kc, :], func=AF.Square),
                      [x.buf], [sq.buf])
                p.pe(lambda e, sq=sq, kc=kc: e.matmul(ps_st.ap, lhsT=self.ones.ap, rhs=sq.ap,
                                                       start=(kc == 0), stop=(kc == KC - 1)),
                     [sq.buf, self.ones.buf], [ps_st.buf])
            p.act(lambda e: e.activation(out=RS.ap, in_=ps_st.ap, func=AF.Ln, bias=self.epsc.ap,
                                         scale=1.0 / D), [ps_st.buf, self.epsc.buf], [RS.buf])
            p.act(lambda e: e.activation(out=RS.ap, in_=RS.ap, func=AF.Exp, scale=-0.5), [RS.buf], [RS.buf])
            for kc in range(KC):
                p.dve(lambda e, kc=kc: e.scalar_tensor_tensor(
                    out=h.ap[:, kc, :], in0=x.ap[:, kc, :], scalar=gain[:, kc:kc + 1], in1=RS.ap,
                    op0=ALU.mult, op1=ALU.mult), [x.buf, RS.buf, self.pkt.buf], [h.buf])

        wcnt = [0]

        def load_w(gi):
            g0, gw = groups[gi]
            i = wcnt[0] % 2
            wcnt[0] += 1
            v1 = w1d.rearrange("(kc p) f -> p kc f", p=128)
            v3 = w3d.rearrange("(kc p) f -> p kc f", p=128)
            p.dma("sp", W1[i].ap[:, :, 0:gw], v1[:, :, g0:g0 + gw], [w1b], [W1[i].buf], "W1_%d" % i)
            p.dma("act", W3[i].ap[:, :, 0:gw], v3[:, :, g0:g0 + gw], [w3b], [W3[i].buf], "W3_%d" % i)
            return i

        load_x(0)
        norm(0)
        nxt = load_w(0)
        mmc = [0]
        for c in range(NCH):
            x = X[c % 2]
            h = H[c % 2]
            if c + 1 < NCH:
                load_x(c + 1)
            for gi, (g0, gw) in enumerate(groups):
                cur = nxt
                if not (c == NCH - 1 and gi == len(groups) - 1):
                    nxt = load_w((gi + 1) % len(groups))
                for j in range(gw // 128):
                    fc = g0 // 128 + j
                    pa = ps_a[mmc[0] % 2]
                    pb = ps_b[mmc[0] % 2]
                    sa = SA[mmc[0] % 2]
                    mmc[0] += 1
                    for kc in range(KC):
                        p.pe(lambda e, pa=pa, kc=kc, cur=cur, j=j, h=h: e.matmul(
                            pa.ap, lhsT=W1[cur].ap[:, kc, j * 128:(j + 1) * 128], rhs=h.ap[:, kc, :],
                            start=(kc == 0), stop=(kc == KC - 1)), [W1[cur].buf, h.buf], [pa.buf])
                    for kc in range(KC):
                        p.pe(lambda e, pb=pb, kc=kc, cur=cur, j=j, h=h: e.matmul(
                            pb.ap, lhsT=W3[cur].ap[:, kc, j * 128:(j + 1) * 128], rhs=h.ap[:, kc, :],
                            start=(kc == 0), stop=(kc == KC - 1)), [W3[cur].buf, h.buf], [pb.buf])
                    p.act(lambda e, pa=pa, sa=sa: e.activation(out=sa.ap, in_=pa.ap, func=AF.Silu),
                          [pa.buf], [sa.buf])
                    p.dve(lambda e, pb=pb, sa=sa, fc=fc: e.tensor_tensor(
                        out=U.ap[:, fc, :], in0=sa.ap, in1=pb.ap, op=ALU.mult),
                        [sa.buf, pb.buf], [U.buf])
                if gi == 2 and c + 1 < NCH:
                    norm(c + 1)
            for dc in range(KC):
                py = ps_y[dc % 2]
                for fc in range(FC):
                    p.pe(lambda e, py=py, fc=fc, dc=dc: e.matmul(
                        py.ap, lhsT=W2.ap[:, fc, dc * 128:(dc + 1) * 128], rhs=U.ap[:, fc, :],
                        start=(fc == 0), stop=(fc == FC - 1)), [W2.buf, U.buf], [py.buf])
                p.dve(lambda e, py=py, dc=dc, x=x: e.scalar_tensor_tensor(
                    out=x.ap[:, dc, :], in0=py.ap, scalar=0.5, in1=x.ap[:, dc, :],
                    op0=ALU.mult, op1=ALU.add), [py.buf, x.buf], [x.buf])
            p.dma("sp", xout_v[:, :, c * TT:(c + 1) * TT], x.ap, [x.buf], [xout_b[c]], "XO%d" % (c % 2))
        self.phase_end()


    def xnorm(self, x, h, gain, SQ, RS, ps_st, TT=512):
        p = self.p
        KC = 8
        for kc in range(KC):
            sq = SQ[kc % 2]
            p.act(lambda e, sq=sq, kc=kc: e.activation(out=sq.ap, in_=x.ap[:, kc, :], func=AF.Square),
                  [x.buf], [sq.buf])
            self.mm(ps_st, ps_st.ap, self.ones.ap, sq.ap, kc == 0, kc == KC - 1, [sq.buf, self.ones.buf])
        p.act(lambda e: e.activation(out=RS.ap, in_=ps_st.ap, func=AF.Ln, bias=self.epsc.ap,
                                     scale=1.0 / D), [ps_st.buf, self.epsc.buf], [RS.buf])
        p.act(lambda e: e.activation(out=RS.ap, in_=RS.ap, func=AF.Exp, scale=-0.5), [RS.buf], [RS.buf])
        for kc in range(KC):
            p.dve(lambda e, kc=kc: e.scalar_tensor_tensor(
                out=h.ap[:, kc, :], in0=x.ap[:, kc, :], scalar=gain[:, kc:kc + 1], in1=RS.ap,
                op0=ALU.mult, op1=ALU.mult), [x.buf, RS.buf, self.pkt.buf], [h.buf])

    def scratch(self):
        d = self.dt
        self.hT = d("hT", [D, S], BF16)
        self.daq = d("daq", [512, S], BF16)
        self.dak = d("dak", [512, S], BF16)
        self.dav = d("dav", [S, 512], BF16)
        self.nqT = d("nqT", [512, S], BF16)
        self.kcT = d("kcT", [128, S], BF16)
        self.vcT = d("vcT", [128, S], BF16)
        self.kselT = d("kselT", [256, S], BF16)
        self.kwinT = d("kwinT", [256, S], BF16)
        self.vsw = d("vsw", [S, 256], BF16)
        self.ngT = d("ngT", [24, S], BF16)
        self.kcn = d("kcn", [2, 128, 256], BF16)
        self.vcz = d("vcz", [2, 2, 128, 192], BF16)
        self.yaT = d("yaT", [512, S], BF16)
        self.ybT = d("ybT", [512, S], BF16)

    def proj_phase(self, l, xin):
        p = self.p
        TT, NCH, KC = 512, 8, 8
        wd = self.w_bf["w_in"][l]
        wb = self.db("w_in_bf", DEPTH)[l]
        wv = wd.rearrange("(kc p) c -> p kc c", p=128)
        xin_b = self.db(xin.name, NCH)
        xin_v = xin.rearrange("(kc p) t -> p kc t", p=128)
        gain = self.pk("g_mix_l%d" % l)
        NW = 2840
        WIN = self.tile("WIN", [KC, NW], BF16)
        for i in range(4):
            q = "sp" if i % 2 == 0 else "act"
            p.dma(q, WIN.ap[:, :, i * 710:(i + 1) * 710], wv[:, :, i * 710:(i + 1) * 710], [wb], [],
                  "WIN_" + q, pwrites=[WIN.buf])
        WD = self.tile("WD", [KC, 512], BF16)
        for i, off in enumerate([2304, 2368, 2560, 2624]):
            for hh in range(2):
                q = "sp" if hh == 0 else "act"
                p.dma(q, WD.ap[:, :, i * 128 + hh * 64:i * 128 + hh * 64 + 64], wv[:, :, off:off + 64], [wb], [],
                      "WD_" + q, pwrites=[WD.buf])
        X = [self.tile("X", [KC, TT], F32) for _ in range(2)]
        H = [self.tile("H", [KC, TT], BF16) for _ in range(2)]
        SQ = [self.tile("SQ", [TT], BF16) for _ in range(2)]
        RS = self.tile("RS", [TT], F32)
        SQX = [self.tile("SQX", [TT], BF16) for _ in range(2)]
        RB = [self.tile("RB", [TT], F32) for _ in range(2)]
        OF = [self.tile("OF", [TT], BF16) for _ in range(3)]
        OT = [self.tile("OT", [TT], BF16) for _ in range(2)]
        ps_st = self.psum[0]
        ps_z = [self.psum[1], self.psum[2]]
        ps_b = [self.psum[3], self.psum[4]]
        ps_t = [self.psum[5], self.psum[6]]
        hT_b = self.db("hT", NCH)
        hT_v = self.hT.rearrange("(kc p) t -> p kc t", p=128)

        def load_x(c):
            t = X[c % 2]
            p.dma("sp", t.ap, xin_v[:, :, c * TT:(c + 1) * TT], [xin_b[c]], [t.buf], "X%d" % (c % 2))

        specs = []
        for j in range(4):
            specs.append((self.daq, j * 128, (WIN, j * 128), "gdq", True))
        for j in range(4):
            specs.append((self.dak, j * 128, (WIN, 512 + j * 128), "gdk", False))
        for j in range(4):
            specs.append((self.nqT, j * 128, (WIN, 1536 + j * 128), "gnq", True))
        for g in range(2):
            specs.append((self.kselT, g * 128, (WD, g * 128), "gks", False))
        for g in range(2):
            specs.append((self.kwinT, g * 128, (WD, (2 + g) * 128), "gkw", False))
        cnt = [0, 0, 0]
        ps_z = [self.psum[1], self.psum[2], self.psum[3]]
        ps_b = [self.psum[4], self.psum[5]]
        ps_t = [self.psum[6], self.psum[7]]
        load_x(0)
        self.xnorm(X[0], H[0], gain, SQX, RS, ps_st)
        p.dma("act", hT_v[:, :, 0:TT], H[0].ap, [H[0].buf], [hT_b[0]], "HO0")
        for c in range(NCH):
            x = X[c % 2]
            h = H[c % 2]
            if c + 1 < NCH:
                load_x(c + 1)
            tsl = slice(c * TT, (c + 1) * TT)
            st_ = {}

            def stage1(i, h=h):
                (dst, r0, (wt, c0), gname, qt) = specs[i]
                pz = ps_z[cnt[0] % 3]
                sq = SQ[cnt[0] % 2]
                st_[i] = (pz, sq, cnt[0])
                cnt[0] += 1
                for kc in range(KC):
                    self.mm(pz, pz.ap, wt.ap[:, kc, c0:c0 + 128], h.ap[:, kc, :], kc == 0, kc == KC - 1,
                            [wt.buf, h.buf])
                p.act(lambda e: e.activation(out=sq.ap, in_=pz.ap, func=AF.Square), [pz.buf], [sq.buf])

            def stage2(i, tsl=tsl):
                (dst, r0, (wt, c0), gname, qt) = specs[i]
                pz, sq, k = st_.pop(i)
                pb = ps_b[k % 2]
                rb = RB[k % 2]
                of = OF[k % 3]
                self.mm(pb, pb.ap, self.BO.ap, sq.ap, True, True, [self.BO.buf, sq.buf])
                if qt:
                    p.act(lambda e: e.activation(out=rb.ap, in_=pb.ap, func=AF.Ln, bias=self.eps64.ap, scale=1.0),
                          [pb.buf, self.eps64.buf], [rb.buf])
                else:
                    p.act(lambda e: e.activation(out=rb.ap, in_=pb.ap, func=AF.Ln, bias=self.epsc.ap,
                                                 scale=1.0 / 64), [pb.buf, self.epsc.buf], [rb.buf])
                p.act(lambda e: e.activation(out=rb.ap, in_=rb.ap, func=AF.Exp, scale=-0.5), [rb.buf], [rb.buf])
                gcol = self.pk("%s_l%d" % (gname, l))
                p.dve(lambda e: e.scalar_tensor_tensor(out=of.ap, in0=pz.ap, scalar=gcol[:, 0:1], in1=rb.ap,
                                                       op0=ALU.mult, op1=ALU.mult),
                      [pz.buf, rb.buf, self.pkt.buf], [of.buf])
                p.dma("sp", dst[r0:r0 + 128, tsl], of.ap, [of.buf], [], "OF%d" % (k % 3),
                      pwrites=[self.db(dst.name)[0]])

            ns = len(specs)
            for i in range(ns + 1):
                if i < ns:
                    stage1(i)
                if i >= 1:
                    stage2(i - 1)
                if i == 8 and c + 1 < NCH:
                    self.xnorm(X[(c + 1) % 2], H[(c + 1) % 2], gain, SQX, RS, ps_st)
                    p.dma("act", hT_v[:, :, (c + 1) * TT:(c + 2) * TT], H[(c + 1) % 2].ap, [H[(c + 1) % 2].buf],
                          [hT_b[c + 1]], "HO%d" % ((c + 1) % 2))
            for dst, c0 in ((self.kcT, 2048), (self.vcT, 2176)):
                pz = ps_z[cnt[0] % 2]
                of = OF[cnt[0] % 3]
                cnt[0] += 1
                for kc in range(KC):
                    self.mm(pz, pz.ap, WIN.ap[:, kc, c0:c0 + 128], h.ap[:, kc, :], kc == 0, kc == KC - 1,
                            [WIN.buf, h.buf])
                p.act(lambda e, of=of, pz=pz: e.activation(out=of.ap, in_=pz.ap, func=AF.Copy),
                      [pz.buf], [of.buf])
                p.dma("sp", dst[:, tsl], of.ap, [of.buf], [], "OF%d" % ((cnt[0] - 1) % 3),
                      pwrites=[self.db(dst.name)[0]])
            pz = ps_z[cnt[0] % 2]
            of = OF[cnt[0] % 3]
            cnt[0] += 1
            for kc in range(KC):
                self.mm(pz, pz.ap[0:24, :], WIN.ap[:, kc, 2816:2840], h.ap[:, kc, :], kc == 0, kc == KC - 1,
                        [WIN.buf, h.buf])
            p.act(lambda e, of=of, pz=pz: e.activation(out=of.ap[0:24], in_=pz.ap[0:24, :], func=AF.Sigmoid),
                  [pz.buf], [of.buf])
            p.dma("sp", self.ngT[:, tsl], of.ap[0:24], [of.buf], [], "OF%d" % ((cnt[0] - 1) % 3),
                  pwrites=[self.db("ngT")[0]])
            for ts in range(4):
                pt = ps_t[cnt[1] % 2]
                ot = OT[cnt[1] % 2]
                cnt[1] += 1
                for kc in range(KC):
                    self.mm(pt, pt.ap, h.ap[:, kc, ts * 128:(ts + 1) * 128], WIN.ap[:, kc, 1024:1536],
                            kc == 0, kc == KC - 1, [WIN.buf, h.buf])
                p.dve(lambda e, ot=ot, pt=pt: e.tensor_copy(out=ot.ap, in_=pt.ap), [pt.buf], [ot.buf])
                r0 = c * TT + ts * 128
                p.dma("act", self.dav[r0:r0 + 128, :], ot.ap, [ot.buf], [], "OT%d" % ((cnt[1] - 1) % 2),
                      pwrites=[self.db("dav")[0]])
                pt = ps_t[cnt[1] % 2]
                ot = OT[cnt[1] % 2]
                cnt[1] += 1
                for i, c0 in enumerate((2432, 2688)):
                    for kc in range(KC):
                        self.mm(pt, pt.ap[:, i * 128:(i + 1) * 128], h.ap[:, kc, ts * 128:(ts + 1) * 128],
                                WIN.ap[:, kc, c0:c0 + 128], kc == 0, kc == KC - 1, [WIN.buf, h.buf])
                p.dve(lambda e, ot=ot, pt=pt: e.tensor_copy(out=ot.ap[:, 0:256], in_=pt.ap[:, 0:256]),
                      [pt.buf], [ot.buf])
                p.dma("act", self.vsw[r0:r0 + 128, :], ot.ap[:, 0:256], [ot.buf], [],
                      "OT%d" % ((cnt[1] - 1) % 2), pwrites=[self.db("vsw")[0]])
        self.phase_end()

    def cmp_phase(self, l):
        p = self.p
        self.cw = {}
        for n in ("cmp_w1_k", "cmp_w2_k", "cmp_w1_v", "cmp_w2_v"):
            if n not in self.dram:
                shp = [DEPTH, 2048, 128] if "w1" in n else [DEPTH, 128, 64]
                self.dt(n, shp, F32, kind="ExternalInput")
        NC_ = 255
        ps_b, ps_h, ps_o, ps_s = self.psum[0], self.psum[1], self.psum[2], self.psum[3]
        for t in ("k", "v"):
            w1 = self.dram["cmp_w1_" + t][l].rearrange("(l d) h -> d l h", d=64)
            w2 = self.dram["cmp_w2_" + t][l]
            W1c = self.tile("W1c", [32, 128], BF16)
            W2c = self.tile("W2c", [128], BF16)
            for hh in range(2):
                p.dma("pool", W1c.ap[hh * 64:(hh + 1) * 64], w1, [], [], "W1c", pwrites=[W1c.buf])
                p.dma("pool", W2c.ap[:, hh * 64:(hh + 1) * 64], w2, [], [], "W2c", pwrites=[W2c.buf])
            peb = self.tile("peb", [32], BF16)
            pe32 = self.pk("pe%s_l%d" % (t, l))
            p.dve(lambda e, peb=peb, pe32=pe32: e.tensor_copy(out=peb.ap, in_=pe32), [self.pkt.buf], [peb.buf])
            KR = self.tile("KR", [S], BF16)
            src = self.kcT if t == "k" else self.vcT
            p.dma("sp", KR.ap, src, [self.db(src.name)[0]], [KR.buf], "KR" + t)
            for g in range(2):
                rows = slice(g * 64, (g + 1) * 64)
                bcol = self.tile("bcol", [1], F32)
                for li in range(32):
                    self.mm(ps_b, ps_b.ap[:, 0:1], W1c.ap[rows, li, :], peb.ap[rows, li:li + 1], li == 0, li == 31,
                            [W1c.buf, peb.buf])
                p.dve(lambda e, bcol=bcol: e.tensor_copy(out=bcol.ap, in_=ps_b.ap[:, 0:1]), [ps_b.buf], [bcol.buf])
                for li in range(32):
                    self.mm(ps_h, ps_h.ap[:, 0:NC_], W1c.ap[rows, li, :], KR.ap[rows, li:li + 16 * 254 + 1:16],
                            li == 0, li == 31, [W1c.buf, KR.buf])
                TS = self.tile("TS", [256], F32)
                T2 = self.tile("T2", [256], F32)
                GL = self.tile("GL", [256], BF16)
                p.act(lambda e, TS=TS, bcol=bcol: e.activation(out=TS.ap[:, 0:NC_], in_=ps_h.ap[:, 0:NC_],
                                                               func=AF.Identity, bias=bcol.ap, scale=1.0),
                      [ps_h.buf, bcol.buf], [TS.buf])
                p.dve(lambda e, TS=TS, T2=T2: e.tensor_tensor(out=T2.ap[:, 0:NC_], in0=TS.ap[:, 0:NC_],
                                                              in1=TS.ap[:, 0:NC_], op=ALU.mult), [TS.buf], [T2.buf])
                p.dve(lambda e, T2=T2: e.tensor_scalar(out=T2.ap[:, 0:NC_], in0=T2.ap[:, 0:NC_], scalar1=0.044715,
                                                       scalar2=1.0, op0=ALU.mult, op1=ALU.add), [T2.buf], [T2.buf])
                p.dve(lambda e, TS=TS, T2=T2: e.tensor_tensor(out=T2.ap[:, 0:NC_], in0=T2.ap[:, 0:NC_],
                                                              in1=TS.ap[:, 0:NC_], op=ALU.mult), [TS.buf, T2.buf],
                      [T2.buf])
                p.act(lambda e, T2=T2: e.activation(out=T2.ap[:, 0:NC_], in_=T2.ap[:, 0:NC_], func=AF.Sigmoid,
                                                    scale=1.5957691216057308), [T2.buf], [T2.buf])
                p.dve(lambda e, TS=TS, T2=T2, GL=GL: e.tensor_tensor(out=GL.ap[:, 0:NC_], in0=T2.ap[:, 0:NC_],
                                                                     in1=TS.ap[:, 0:NC_], op=ALU.mult),
                      [TS.buf, T2.buf], [GL.buf])
                if t == "k":
                    self.mm(ps_o, ps_o.ap[:, 0:NC_], W2c.ap, GL.ap[:, 0:NC_], True, True, [W2c.buf, GL.buf])
                    SQ = self.tile("SQ", [256], BF16)
                    RB = self.tile("RB", [256], F32)
                    KO = self.tile("KO", [256], BF16)
                    p.pool(lambda e, KO=KO: e.memset(KO.ap, 0.0), [], [KO.buf])
                    p.act(lambda e, SQ=SQ: e.activation(out=SQ.ap[:, 0:NC_], in_=ps_o.ap[:, 0:NC_], func=AF.Square),
                          [ps_o.buf], [SQ.buf])
                    self.mm(ps_s, ps_s.ap[:, 0:NC_], self.BO.ap, SQ.ap[:, 0:NC_], True, True, [self.BO.buf, SQ.buf])
                    p.act(lambda e, RB=RB: e.activation(out=RB.ap[:, 0:NC_], in_=ps_s.ap[:, 0:NC_], func=AF.Sqrt,
                                                        bias=self.epsc.ap, scale=1.0 / 64),
                          [ps_s.buf, self.epsc.buf], [RB.buf])
                    p.dve(lambda e, RB=RB: e.reciprocal(out=RB.ap[:, 0:NC_], in_=RB.ap[:, 0:NC_]), [RB.buf], [RB.buf])
                    gcol = self.pk("gkc_l%d" % l)
                    p.dve(lambda e, KO=KO, RB=RB, gcol=gcol: e.scalar_tensor_tensor(
                        out=KO.ap[:, 0:NC_], in0=ps_o.ap[:, 0:NC_], scalar=gcol[:, 0:1], in1=RB.ap[:, 0:NC_],
                        op0=ALU.mult, op1=ALU.mult), [ps_o.buf, RB.buf, self.pkt.buf, KO.buf], [KO.buf])
                    p.dma("sp", self.kcn[g], KO.ap, [KO.buf], [], "KO", pwrites=[self.db("kcn")[0]])
                else:
                    for ct in range(2):
                        n = 128 if ct == 0 else 127
                        VZ = self.tile("VZ", [192], BF16)
                        p.pool(lambda e, VZ=VZ: e.memset(VZ.ap, 0.0), [], [VZ.buf])
                        self.mm(ps_o, ps_o.ap[0:n, 0:64], GL.ap[:, ct * 128:ct * 128 + n], W2c.ap[:, 0:64], True, True,
                                [W2c.buf, GL.buf])
                        p.dve(lambda e, VZ=VZ, n=n: e.tensor_copy(out=VZ.ap[0:n, 64:128], in_=ps_o.ap[0:n, 0:64]),
                              [ps_o.buf, VZ.buf], [VZ.buf])
                        p.dma("sp", self.vcz[g, ct], VZ.ap, [VZ.buf], [], "VZ", pwrites=[self.db("vcz")[0]])
        self.phase_end()

    def da_phase(self, l):
        p = self.p
        lam_init = 0.8 - 0.6 * float(np.exp(-0.3 * l))
        LB = self.pk("lam_l%d" % l)
        junk = self.tile("junk", [64], F32)
        d12 = self.tile("d12", [2], F32)
        lamneg = self.tile("lamneg", [1], F32)
        sbias = self.tile("sbias", [1], F32)
        p.pool(lambda e: e.memset(sbias.ap, EPS / (1.0 - lam_init) ** 2), [], [sbias.buf])
        for i in range(2):
            p.dve(lambda e, i=i: e.tensor_tensor(out=junk.ap, in0=LB[:, i * 128:i * 128 + 64],
                                                 in1=LB[:, i * 128 + 64:i * 128 + 128], op=ALU.mult),
                  [self.pkt.buf], [junk.buf])
            p.dve(lambda e, i=i: e.tensor_reduce(out=d12.ap[:, i:i + 1], in_=junk.ap, axis=AX.X, op=ALU.add),
                  [junk.buf], [d12.buf])
        p.act(lambda e: e.activation(out=d12.ap, in_=d12.ap, func=AF.Exp), [d12.buf], [d12.buf])
        p.dve(lambda e: e.tensor_tensor(out=lamneg.ap, in0=d12.ap[:, 1:2], in1=d12.ap[:, 0:1], op=ALU.subtract),
              [d12.buf], [lamneg.buf])
        p.dve(lambda e: e.tensor_scalar_add(out=lamneg.ap, in0=lamneg.ap, scalar1=-lam_init), [lamneg.buf],
              [lamneg.buf])
        sscale = 1.0 / (128.0 * (1.0 - lam_init) ** 2)
        gsub = self.pk("gsub_l%d" % l)
        tb31 = self.pk("tb31")
        QT = [self.tile("QT", [S], BF16) for _ in range(2)]
        KT = [[self.tile("KT", [S], BF16) for _ in range(2)] for _ in range(2)]
        for i2 in range(2):
            for comp in range(2):
                p.pool(lambda e, t=KT[i2][comp]: e.memset(t.ap, 0.0), [], [KT[i2][comp].buf])
        VV = [self.tile("VV", [32, 128], BF16) for _ in range(2)]
        ST = [self.tile("ST", [1792], BF16) for _ in range(2)]
        STR = self.tile("STR", [1792], BF16)
        PT = [self.tile("PT", [512], BF16) for _ in range(8)]
        PT0 = [self.tile("PT0", [512], BF16) for _ in range(4)]
        R = [self.tile("R", [512], F32) for _ in range(2)]
        A = [self.tile("A", [512], F32) for _ in range(2)]
        Y = self.tile("Y", [512], F32)
        SQ = self.tile("SQ", [512], BF16)
        RS = self.tile("RS", [512], F32)
        YO = [self.tile("YO", [512], BF16) for _ in range(2)]
        psS = self.psum[0:4]
        psO = self.psum[4:6]
        psL = self.psum[6:8]
        qb, kb, vb, fb = self.db("daq")[0], self.db("dak")[0], self.db("dav")[0], self.db("Fd")[0]
        ya_b = self.db("yaT")[0]
        OC = [self.tile("OC", [512], F32) for _ in range(2)]
        LC = [self.tile("LC", [512], F32) for _ in range(2)]
        psS = self.psum[0:4]
        tiles = {}
        cnt = {"s": 0, "pt": 0, "p0": 0}

        def load_head(h):
            i2 = h % 2
            qt, vv, st = QT[i2], VV[i2], ST[i2]
            p.dma("sp", qt.ap, self.daq[h * 128:(h + 1) * 128, :], [qb], [qt.buf], "QT%d" % i2)
            for comp in range(2):
                kt_ = KT[i2][comp]
                rs_ = slice(comp * 64, comp * 64 + 64)
                p.dma("act", kt_.ap[rs_], self.dak[h * 128 + comp * 64:h * 128 + comp * 64 + 64, :], [kb], [kt_.buf],
                      "KT%d%d" % (i2, comp))
            p.dma("sp", vv.ap, self.dav[:, h * 128:(h + 1) * 128].rearrange("(kt p) e -> p kt e", p=128), [vb],
                  [vv.buf], "VV%d" % i2)
            p.dma("act", STR.ap, bass.AP(self.Fd.tensor, h * LF + 3585, [[1, 128], [1, 1792]]), [fb], [STR.buf],
                  "STR")
            for c4 in range(4):
                ps = psS[cnt["s"] % 4]
                cnt["s"] += 1
                cs_ = slice(c4 * 448, (c4 + 1) * 448)
                self.mm(ps, ps.ap[:, 0:448], self.J.ap, STR.ap[:, cs_], True, True, [self.J.buf, STR.buf])
                p.act(lambda e, ps=ps, cs_=cs_: e.activation(out=st.ap[:, cs_], in_=ps.ap[:, 0:448], func=AF.Exp),
                      [ps.buf], [st.buf])

        jobs = []
        for h in range(4):
            for qc in range(8):
                nk = 4 * qc + 4
                for kt in range(nk):
                    for comp in range(2):
                        jobs.append((h, qc, kt, comp, nk))
        def front(j):
            h, qc, kt, comp, nk = jobs[j]
            i2 = h % 2
            qt, kt_, st = QT[i2], KT[i2][comp], ST[i2]
            qs = slice(qc * 512, (qc + 1) * 512)
            delta = qc * 512 - kt * 128
            near = delta <= 896
            ps = psS[cnt["s"] % 4]
            cnt["s"] += 1
            self.mm(ps, ps.ap, kt_.ap[:, kt * 128:(kt + 1) * 128], qt.ap[:, qs], True, True, [kt_.buf, qt.buf])
            pt = PT[cnt["pt"] % 8]
            cnt["pt"] += 1
            if near:
                p0 = PT0[cnt["p0"] % 4]
                cnt["p0"] += 1
                p.act(lambda e: e.activation(out=p0.ap, in_=ps.ap, func=AF.Exp), [ps.buf], [p0.buf])
                p.dve(lambda e: e.tensor_tensor(out=pt.ap, in0=p0.ap, in1=st.ap[:, delta + 384:delta + 896],
                                                op=ALU.mult), [p0.buf, st.buf], [pt.buf])
            else:
                p.act(lambda e: e.activation(out=pt.ap, in_=ps.ap, func=AF.Exp, bias=tb31[:, h:h + 1], scale=1.0),
                      [ps.buf, self.pkt.buf], [pt.buf])
            tiles[j] = pt

        steps = []

        def epi(h, qc):
            qs = slice(qc * 512, (qc + 1) * 512)
            for comp in range(2):
                p.act(lambda e, comp=comp: e.activation(out=LC[comp].ap, in_=psL[comp].ap, func=AF.Ln),
                      [psL[comp].buf], [LC[comp].buf])
                p.act(lambda e, comp=comp: e.activation(out=R[comp].ap, in_=LC[comp].ap, func=AF.Exp, scale=-1.0),
                      [LC[comp].buf], [R[comp].buf])
                p.dve(lambda e, comp=comp: e.tensor_tensor(out=A[comp].ap, in0=psO[comp].ap, in1=R[comp].ap,
                                                           op=ALU.mult), [psO[comp].buf, R[comp].buf], [A[comp].buf])
            steps.append(lambda: p.dve(lambda e: e.scalar_tensor_tensor(
                out=Y.ap, in0=A[1].ap, scalar=lamneg.ap[:, 0:1], in1=A[0].ap, op0=ALU.mult, op1=ALU.add),
                [A[0].buf, A[1].buf, lamneg.buf], [Y.buf]))
            steps.append(lambda: p.act(lambda e: e.activation(out=SQ.ap, in_=Y.ap, func=AF.Square), [Y.buf], [SQ.buf]))
            steps.append(None)
            steps.append(None)

            def stat():
                ps = psS[cnt["s"] % 4]
                cnt["s"] += 1
                self.mm(ps, ps.ap, self.ones.ap, SQ.ap, True, True, [self.ones.buf, SQ.buf])
                p.act(lambda e: e.activation(out=RS.ap, in_=ps.ap, func=AF.Ln, bias=sbias.ap, scale=sscale),
                      [ps.buf, sbias.buf], [RS.buf])
                p.act(lambda e: e.activation(out=RS.ap, in_=RS.ap, func=AF.Exp, scale=-0.5), [RS.buf], [RS.buf])
            steps.append(stat)
            steps.append(None)
            yo = YO[qc % 2]

            def fin():
                p.dve(lambda e: e.scalar_tensor_tensor(out=yo.ap, in0=Y.ap, scalar=gsub[:, 0:1], in1=RS.ap,
                                                       op0=ALU.mult, op1=ALU.mult),
                      [Y.buf, RS.buf, self.pkt.buf], [yo.buf])
                p.dma("sp", self.yaT[h * 128:(h + 1) * 128, qs], yo.ap, [yo.buf], [], "YO%d" % (qc % 2),
                      pwrites=[ya_b])
            steps.append(fin)

        def drain(n):
            while n > 0 and steps:
                f = steps.pop(0)
                if f is not None:
                    f()
                n -= 1

        def back(j):
            h, qc, kt, comp, nk = jobs[j]
            vv = VV[h % 2]
            pt = tiles.pop(j)
            self.mm(psO[comp], psO[comp].ap, vv.ap[:, kt, :], pt.ap, kt == 0, kt == nk - 1, [vv.buf, pt.buf])
            self.mm(psL[comp], psL[comp].ap, self.ones.ap, pt.ap, kt == 0, kt == nk - 1, [self.ones.buf, pt.buf])
            if kt == nk - 1 and comp == 1:
                drain(len(steps))
                epi(h, qc)
                if qc == 7 and h + 2 < 4:
                    load_head(h + 2)

        LA = 3
        load_head(0)
        load_head(1)
        n = len(jobs)
        for i in range(n + LA):
            if i < n:
                front(i)
            if i >= LA:
                back(i - LA)
            drain(2)
        drain(len(steps))
        self.phase_end()

    def nsa_phase(self, l):
        p = self.p
        for n, shp in (("ebig", [128, S]), ("gs", [24, 1536]), ("cadd", [S, 64])):
            if n not in self.dram:
                self.dt(n, shp, F32, kind="ExternalInput")
        EB = self.castload("EB", [S], self.dram["ebig"])
        GS = self.castload("GS", [1536], self.dram["gs"], parts=24)
        OV = self.castload("OV", [2, 65], self.cst_in[:, C_OV:C_OV + 130].rearrange("p (a b) -> p a b", a=2))
        CA = self.tile("CA", [32, 64], F32)
        p.dma("sp", CA.ap, self.dram["cadd"].rearrange("(qt p) j -> p qt j", p=128), [], [CA.buf], "CA")
        NG = self.tile("NG", [S], BF16)
        p.dma("act", NG.ap[0:24], self.ngT, [self.db("ngT")[0]], [NG.buf], "NG")
        tb31 = self.pk("tb31")
        fb = self.db("Fd")[0]
        tiny = 1e-30
        psS = self.psum[0:4]
        psO, psL, psG, psI = self.psum[4], self.psum[5], self.psum[6], self.psum[7]
        psTb = psI.ap.bitcast(BF16)
        PT = [self.tile("PT", [512], BF16) for _ in range(6)]
        Rt = self.tile("Rt", [512], F32)
        T1 = self.tile("T1", [512], F32)
        T2 = [self.tile("T2", [512], F32) for _ in range(2)]
        YO = [self.tile("YO", [512], BF16) for _ in range(2)]
        cnt = {"s": 0, "pt": 0, "t2": 0, "bt": 0, "yo": 0}

        def sbank():
            cnt["s"] += 1
            return psS[cnt["s"] % 3]

        psI2 = [self.psum[3], self.psum[7]]

        def ptile():
            cnt["pt"] += 1
            return PT[cnt["pt"] % 6]

        def epilogue(gi, pair, b, qs, YBt, first, guard=False):
            if guard:
                p.dve(lambda e: e.tensor_scalar_max(out=Rt.ap, in0=psL.ap, scalar1=tiny), [psL.buf], [Rt.buf])
                p.act(lambda e: e.activation(out=Rt.ap, in_=Rt.ap, func=AF.Ln), [Rt.buf], [Rt.buf])
                p.act(lambda e: e.activation(out=Rt.ap, in_=Rt.ap, func=AF.Exp, scale=-1.0), [Rt.buf], [Rt.buf])
            else:
                p.dve(lambda e: e.reciprocal(out=Rt.ap, in_=psL.ap), [psL.buf], [Rt.buf])
            p.dve(lambda e: e.tensor_tensor(out=T1.ap, in0=psO.ap, in1=Rt.ap, op=ALU.mult), [psO.buf, Rt.buf],
                  [T1.buf])
            c0 = ((gi * 2 + pair) * 3 + b) * 128
            self.mm(psG, psG.ap, GS.ap[0:24, c0:c0 + 128], NG.ap[0:24, qs], True, True, [GS.buf, NG.buf])
            if first:
                p.dve(lambda e: e.tensor_tensor(out=YBt.ap[:, qs], in0=T1.ap, in1=psG.ap, op=ALU.mult),
                      [T1.buf, psG.buf], [YBt.buf])
            else:
                cnt["t2"] += 1
                t2 = T2[cnt["t2"] % 2]
                p.dve(lambda e: e.tensor_tensor(out=t2.ap, in0=T1.ap, in1=psG.ap, op=ALU.mult),
                      [T1.buf, psG.buf], [t2.buf])
                p.pool(lambda e: e.tensor_tensor(out=YBt.ap[:, qs], in0=YBt.ap[:, qs], in1=t2.ap, op=ALU.add),
                       [t2.buf, YBt.buf], [YBt.buf])

        for g in range(2):
            base = self.aoff
            KS = [self.tile("KS", [S], BF16) for _ in range(2)]
            KW = [self.tile("KW", [S], BF16) for _ in range(2)]
            for hh in range(2):
                rs_ = slice(hh * 64, hh * 64 + 64)
                for kk, src, nm in ((KS, self.kselT, "kselT"), (KW, self.kwinT, "kwinT")):
                    p.pool(lambda e, t=kk[hh]: e.memset(t.ap, 0.0), [], [kk[hh].buf])
                    p.dma("sp" if hh == 0 else "act", kk[hh].ap[rs_], src[g * 128 + hh * 64:g * 128 + hh * 64 + 64, :],
                          [self.db(nm)[0]], [kk[hh].buf], "K%s%d" % (nm[1], hh))
            VS = self.tile("VS", [32, 192], BF16)
            VW = self.tile("VW", [32, 192], BF16)
            for i, vt in enumerate((VS, VW)):
                p.pool(lambda e, vt=vt: e.memset(vt.ap, 0.0), [], [vt.buf])
                c0 = i * 128 + g * 64
                p.dma("sp" if i == 0 else "act", vt.ap[:, :, 64:128],
                      self.vsw[:, c0:c0 + 64].rearrange("(kt p) d -> p kt d", p=128), [self.db("vsw")[0]],
                      [vt.buf], "VSW%d" % i)
            KC = [self.tile("KC", [256], BF16) for _ in range(2)]
            for hh in range(2):
                rs_ = slice(hh * 64, hh * 64 + 64)
                p.pool(lambda e, t=KC[hh]: e.memset(t.ap, 0.0), [], [KC[hh].buf])
                p.dma("sp", KC[hh].ap[rs_], self.kcn[g, hh * 64:hh * 64 + 64, :], [self.db("kcn")[0]], [KC[hh].buf],
                      "KC%d" % hh)
            VC = self.tile("VC", [2, 192], BF16)
            p.dma("act", VC.ap, self.vcz[g].rearrange("ct p c -> p ct c"), [self.db("vcz")[0]], [VC.buf], "VC")
            SELB = self.tile("SELB", [S], BF16)
            p.pool(lambda e, SELB=SELB: e.memset(SELB.ap, 0.0), [], [SELB.buf])
            YB = [self.tile("YB", [S], F32) for _ in range(2)]
            QP = [self.tile("QP", [S], BF16) for _ in range(2)]
            for pair in range(2):
                r0 = (g * 2 + pair) * 128
                p.dma("sp" if pair == 0 else "act", QP[pair].ap, self.nqT[r0:r0 + 128, :], [self.db("nqT")[0]],
                      [QP[pair].buf], "QP%d" % pair)
            PC = [[[self.tile("PC", [512], BF16) for _ in range(2)] for _ in range(2)] for _ in range(2)]
            BT = [self.tile("BT", [512], BF16) for _ in range(4)]
            LS = self.tile("LS", [4], F32)
            IMP = self.tile("IMP", [64], F32)
            M8 = self.tile("M8", [8], F32)
            SB = self.tile("SB", [64], BF16)
            stages = getattr(self, "nsa_stages", "ciws")
            for qc in range(8 if "c" in stages else 0):
                q0 = qc * 512
                qs = slice(q0, q0 + 512)
                ncts = 1 if qc < 4 else 2
                for pair in range(2):
                    for hh in range(2):
                        rows = slice(hh * 64, hh * 64 + 64)
                        head = g * 4 + pair * 2 + hh
                        for ct in range(ncts):
                            n = 128 if ct == 0 else 127
                            bt = BT[cnt["bt"] % 4]
                            q = "sp" if cnt["bt"] % 2 == 0 else "act"
                            off = (4 + head) * LF + 2033 + q0 - 16 * ct * 128
                            p.dma(q, bt.ap, bass.AP(self.Fd.tensor, off, [[16, 128], [1, 512]]), [fb], [bt.buf],
                                  "BT%d" % (cnt["bt"] % 4))
                            cnt["bt"] += 1
                            ps = sbank()
                            self.mm(ps, ps.ap[0:n, :], KC[hh].ap[:, ct * 128:ct * 128 + n], QP[pair].ap[:, qs],
                                    True, False, [KC[hh].buf, QP[pair].buf])
                            self.mm(ps, ps.ap[0:n, :], self.J.ap[:, 0:n], bt.ap, False, True, [self.J.buf, bt.buf])
                            pc = PC[pair][hh][ct]
                            p.act(lambda e, pc=pc, ps=ps, n=n: e.activation(out=pc.ap[0:n], in_=ps.ap[0:n, :],
                                                                             func=AF.Exp), [ps.buf], [pc.buf])
                    first = True
                    for hh in range(2):
                        for ct in range(ncts):
                            n = 128 if ct == 0 else 127
                            last = (hh == 1 and ct == ncts - 1)
                            pc = PC[pair][hh][ct]
                            cs = slice(64, 192) if hh == 0 else slice(0, 128)
                            self.mm(psO, psO.ap, VC.ap[0:n, ct, cs], pc.ap[0:n], first, last, [VC.buf, pc.buf])
                            self.mm(psL, psL.ap, self.OZ.ap[0:n, cs], pc.ap[0:n], first, last,
                                    [self.OZ.buf, pc.buf])
                            first = False
                    epilogue(g, pair, 0, qs, YB[pair], True, guard=True)
                for qsub in range(4 if "i" in stages else 0):
                    qt = qc * 4 + qsub
                    psI = psI2[qt % 2]
                    psTb = psI.ap.bitcast(BF16)
                    for h4 in range(4):
                        pair, hh = h4 // 2, h4 % 2
                        for ct in range(ncts):
                            n = 128 if ct == 0 else 127
                            pc = PC[pair][hh][ct]
                            self.mm(psI, psI.ap[:, h4 * 65:(h4 + 1) * 65], pc.ap[0:n, qsub * 128:(qsub + 1) * 128],
                                    OV.ap[0:n, ct, :], ct == 0, ct == ncts - 1, [pc.buf, OV.buf])
                    p.dve(lambda e, psI=psI: e.tensor_scalar_max(out=LS.ap, in0=psI.ap[:, 64:260:65], scalar1=tiny),
                          [psI.buf], [LS.buf])
                    p.dve(lambda e: e.reciprocal(out=LS.ap, in_=LS.ap), [LS.buf], [LS.buf])
                    p.dve(lambda e, psI=psI: e.tensor_scalar(out=IMP.ap, in0=psI.ap[:, 0:64], scalar1=LS.ap[:, 0:1],
                                                    scalar2=None, op0=ALU.mult), [psI.buf, LS.buf], [IMP.buf])
                    for h4 in range(1, 4):
                        p.dve(lambda e, h4=h4, psI=psI: e.scalar_tensor_tensor(
                            out=IMP.ap, in0=psI.ap[:, h4 * 65:h4 * 65 + 64], scalar=LS.ap[:, h4:h4 + 1], in1=IMP.ap,
                            op0=ALU.mult, op1=ALU.add), [psI.buf, LS.buf, IMP.buf], [IMP.buf])
                    p.dve(lambda e, qt=qt: e.tensor_tensor(out=IMP.ap, in0=IMP.ap, in1=CA.ap[:, qt, :], op=ALU.add),
                          [IMP.buf, CA.buf], [IMP.buf])
                    p.dve(lambda e: e.max(out=M8.ap, in_=IMP.ap), [IMP.buf], [M8.buf])
                    p.dve(lambda e: e.tensor_scalar(out=SB.ap, in0=IMP.ap, scalar1=M8.ap[:, 7:8], scalar2=NEG,
                                                    op0=ALU.is_lt, op1=ALU.mult), [IMP.buf, M8.buf], [SB.buf])
                    p.pe(lambda e, psTb=psTb: e.transpose(out=psTb[0:64, 640:768], in_=SB.ap, identity=self.identb.ap),
                         [SB.buf, self.identb.buf], [psI.buf])
                    p.act(lambda e, qt=qt, psTb=psTb: e.activation(out=SELB.ap[0:64, qt * 128:(qt + 1) * 128],
                                                        in_=psTb[0:64, 640:768], func=AF.Copy),
                          [psI.buf, SELB.buf], [SELB.buf])
            psS4 = [self.psum[0], self.psum[1], self.psum[2], self.psum[7]]
            psO2 = [self.psum[3], self.psum[4]]
            psL2 = [self.psum[5], self.psum[6]]
            noE = getattr(self, "nsa_noE", False)
            for pair in range(2):
                base2 = self.aoff
                STc = [self.tile("STc", [1792], BF16) for _ in range(2)]
                STw = [self.tile("STw", [1408], BF16) for _ in range(2)]
                STR = self.tile("STR", [1792], BF16)
                PT0 = [self.tile("PT0", [512], BF16) for _ in range(4)]
                for hh in range(2):
                    head = g * 4 + pair * 2 + hh
                    for (dst, off, wid, ch) in ((STc[hh], 3585, 1792, 448), (STw[hh], 8193, 1408, 352)):
                        p.dma("sp", STR.ap[:, 0:wid], bass.AP(self.Fd.tensor, (4 + head) * LF + off, [[1, 128], [1, wid]]),
                              [fb], [STR.buf], "STRn")
                        for c4 in range(4):
                            cnt["s"] += 1
                            ps = psS4[cnt["s"] % 4]
                            cs_ = slice(c4 * ch, (c4 + 1) * ch)
                            self.mm(ps, ps.ap[:, 0:ch], self.J.ap, STR.ap[:, cs_], True, True, [self.J.buf, STR.buf])
                            p.act(lambda e, ps=ps, cs_=cs_, dst=dst, ch=ch: e.activation(
                                out=dst.ap[:, cs_], in_=ps.ap[:, 0:ch], func=AF.Exp), [ps.buf], [dst.buf])
                qp = QP[pair]
                YBt = YB[pair]
                jobs = []
                ngrp = 0
                for qc in range(getattr(self, "nsa_maxqc", 8)):
                    for br in (1, 2):
                        if (br == 1 and "s" not in stages) or (br == 2 and "w" not in stages):
                            continue
                        if br == 1:
                            kts = list(range(4 * qc + 4))
                        else:
                            kts = [kt for kt in range(4 * qc - 4, 4 * qc + 4) if kt >= 0]
                        for ki, kt in enumerate(kts):
                            for hh in range(2):
                                jobs.append((qc, br, kt, hh, ki == 0 and hh == 0, ki == len(kts) - 1 and hh == 1,
                                             ngrp % 2))
                        ngrp += 1
                tl = {}

                def front(j):
                    qc, br, kt, hh, fst, lst, gp = jobs[j]
                    q0 = qc * 512
                    qs = slice(q0, q0 + 512)
                    KK, STs = (KS, STc) if br == 1 else (KW, STw)
                    delta = q0 - kt * 128
                    near = (delta <= 896) or br == 2
                    useE = (br == 1 and not noE)
                    rows = slice(hh * 64, hh * 64 + 64)
                    head = g * 4 + pair * 2 + hh
                    cnt["s"] += 1
                    ps = psS4[cnt["s"] % 4]
                    self.mm(ps, ps.ap, KK[hh].ap[:, kt * 128:(kt + 1) * 128], qp.ap[:, qs], True, not useE,
                            [KK[hh].buf, qp.buf])
                    if useE:
                        self.mm(ps, ps.ap, EB.ap[:, kt * 128:(kt + 1) * 128], SELB.ap[:, qs], False, True,
                                [EB.buf, SELB.buf])
                    pt = ptile()
                    if near:
                        cnt["p0"] = cnt.get("p0", 0) + 1
                        p0 = PT0[cnt["p0"] % 4]
                        p.act(lambda e: e.activation(out=p0.ap, in_=ps.ap, func=AF.Exp), [ps.buf], [p0.buf])
                        p.dve(lambda e: e.tensor_tensor(out=pt.ap, in0=p0.ap, in1=STs[hh].ap[:, delta + 384:delta + 896],
                                                        op=ALU.mult), [p0.buf, STs[hh].buf], [pt.buf])
                    else:
                        p.act(lambda e: e.activation(out=pt.ap, in_=ps.ap, func=AF.Exp,
                                                     bias=tb31[:, 4 + head:5 + head], scale=1.0),
                              [ps.buf, self.pkt.buf], [pt.buf])
                    tl[j] = pt

                def back(j):
                    qc, br, kt, hh, fst, lst, gp = jobs[j]
                    qs = slice(qc * 512, qc * 512 + 512)
                    VVt = VS if br == 1 else VW
                    pt = tl.pop(j)
                    ybt = YBt
                    po, pl = psO2[gp], psL2[gp]
                    cs = slice(64, 192) if hh == 0 else slice(0, 128)
                    self.mm(po, po.ap, VVt.ap[:, kt, cs], pt.ap, fst, lst, [VVt.buf, pt.buf])
                    self.mm(pl, pl.ap, self.OZ.ap[:, cs], pt.ap, fst, lst, [self.OZ.buf, pt.buf])
                    if not lst:
                        return
                    c0 = ((g * 2 + pair) * 3 + br) * 128
                    cnt["s"] += 1
                    psG2 = psS4[cnt["s"] % 4]
                    self.mm(psG2, psG2.ap, GS.ap[0:24, c0:c0 + 128], NG.ap[0:24, qs], True, True, [GS.buf, NG.buf])
                    p.act(lambda e: e.activation(out=Rt.ap, in_=pl.ap, func=AF.Ln), [pl.buf], [Rt.buf])
                    p.act(lambda e: e.activation(out=Rt.ap, in_=Rt.ap, func=AF.Exp, scale=-1.0), [Rt.buf], [Rt.buf])
                    p.dve(lambda e: e.tensor_tensor(out=T1.ap, in0=po.ap, in1=Rt.ap, op=ALU.mult), [po.buf, Rt.buf],
                          [T1.buf])
                    cnt["t2"] += 1
                    t2 = T2[cnt["t2"] % 2]
                    p.dve(lambda e: e.tensor_tensor(out=t2.ap, in0=T1.ap, in1=psG2.ap, op=ALU.mult),
                          [T1.buf, psG2.buf], [t2.buf])
                    p.pool(lambda e: e.tensor_tensor(out=ybt.ap[:, qs], in0=ybt.ap[:, qs], in1=t2.ap, op=ALU.add),
                           [t2.buf, ybt.buf], [ybt.buf])
                    if br == 2 or "w" not in stages:
                        yo = YO[cnt["yo"] % 2]
                        p.pool(lambda e: e.tensor_copy(out=yo.ap, in_=ybt.ap[:, qs]), [ybt.buf], [yo.buf])
                        r0 = (g * 2 + pair) * 128
                        p.dma("sp", self.ybT[r0:r0 + 128, qs], yo.ap, [yo.buf], [], "YBO%d" % (cnt["yo"] % 2),
                              pwrites=[self.db("ybT")[0]])
                        cnt["yo"] += 1

                LA = 3
                n = len(jobs)
                for i in range(n + LA):
                    if i < n:
                        front(i)
                    if i >= LA:
                        back(i - LA)
                self.phase_end()
                self.aoff = base2
            self.aoff = base
        self.aoff = self.abase

    def merge_phase(self, l, xin, xout):
        p = self.p
        TT, NCH, KC = 512, 8, 8
        wv = self.w_bf["w_in"][l].rearrange("(kc p) c -> p kc c", p=128)
        wb = self.db("w_in_bf", DEPTH)[l]
        WMG = self.tile("WMG", [KC, 2048], BF16)
        for i in range(4):
            q = "sp" if i % 2 == 0 else "act"
            p.dma(q, WMG.ap[:, :, i * 512:(i + 1) * 512], wv[:, :, 2840 + i * 512:2840 + (i + 1) * 512], [wb], [],
                  "WMG_" + q, pwrites=[WMG.buf])
        WA = self.tile("WA", [4, D], BF16)
        WB = self.tile("WB", [4, D], BF16)
        WO = self.tile("WO", [KC, D], BF16)
        p.dma("sp", WA.ap, self.w_bf["w_branch_a"][l].rearrange("(kc p) c -> p kc c", p=128),
              [self.db("w_branch_a_bf", DEPTH)[l]], [WA.buf], "WA")
        p.dma("act", WB.ap, self.w_bf["w_branch_b"][l].rearrange("(kc p) c -> p kc c", p=128),
              [self.db("w_branch_b_bf", DEPTH)[l]], [WB.buf], "WB")
        p.dma("sp", WO.ap, self.w_bf["w_out"][l].rearrange("(kc p) c -> p kc c", p=128),
              [self.db("w_out_bf", DEPTH)[l]], [WO.buf], "WO")
        xin_b = self.db(xin.name, NCH)
        xout_b = self.db(xout.name, NCH)
        xin_v = xin.rearrange("(kc p) t -> p kc t", p=128)
        xout_v = xout.rearrange("(kc p) t -> p kc t", p=128)
        hT_v = self.hT.rearrange("(kc p) t -> p kc t", p=128)
        ya_v = self.yaT.rearrange("(kc p) t -> p kc t", p=128)
        yb_v = self.ybT.rearrange("(kc p) t -> p kc t", p=128)
        X = [self.tile("X", [KC, TT], F32) for _ in range(2)]
        H = [self.tile("H", [KC, TT], BF16) for _ in range(2)]
        YA = [self.tile("YA", [4, TT], BF16) for _ in range(2)]
        YBt = [self.tile("YBt", [4, TT], BF16) for _ in range(2)]
        MER = self.tile("MER", [KC, TT], BF16)
        SG = [self.tile("SG", [TT], F32) for _ in range(2)]
        TA = self.tile("TA", [TT], F32)
        TB = self.tile("TB", [TT], F32)
        psA, psB, psMA, psMB = self.psum[0], self.psum[1], self.psum[2], self.psum[3]
        psY = [self.psum[4], self.psum[5]]
        hb, yab, ybb = self.db("hT", NCH), self.db("yaT")[0], self.db("ybT")[0]

        def load(c):
            i = c % 2
            ts = slice(c * TT, (c + 1) * TT)
            p.dma("sp", X[i].ap, xin_v[:, :, ts], [xin_b[c]], [X[i].buf], "X%d" % i)
            p.dma("act", H[i].ap, hT_v[:, :, ts], [hb[c]], [H[i].buf], "H%d" % i)
            p.dma("sp", YA[i].ap, ya_v[:, :, ts], [yab], [YA[i].buf], "YA%d" % i)
            p.dma("act", YBt[i].ap, yb_v[:, :, ts], [ybb], [YBt[i].buf], "YB%d" % i)

        load(0)
        for c in range(NCH):
            i = c % 2
            x, h, ya, yb = X[i], H[i], YA[i], YBt[i]
            if c + 1 < NCH:
                load(c + 1)
            for dc in range(KC):
                ds_ = slice(dc * 128, (dc + 1) * 128)
                for kc in range(KC):
                    self.mm(psMA, psMA.ap, WMG.ap[:, kc, dc * 128:(dc + 1) * 128], h.ap[:, kc, :], kc == 0,
                            kc == KC - 1, [WMG.buf, h.buf])
                p.act(lambda e: e.activation(out=SG[0].ap, in_=psMA.ap, func=AF.Sigmoid), [psMA.buf], [SG[0].buf])
                for kc in range(KC):
                    self.mm(psMB, psMB.ap, WMG.ap[:, kc, 1024 + dc * 128:1024 + (dc + 1) * 128], h.ap[:, kc, :],
                            kc == 0, kc == KC - 1, [WMG.buf, h.buf])
                p.act(lambda e: e.activation(out=SG[1].ap, in_=psMB.ap, func=AF.Sigmoid), [psMB.buf], [SG[1].buf])
                for k in range(4):
                    self.mm(psA, psA.ap, WA.ap[:, k, ds_], ya.ap[:, k, :], k == 0, k == 3, [WA.buf, ya.buf])
                for k in range(4):
                    self.mm(psB, psB.ap, WB.ap[:, k, ds_], yb.ap[:, k, :], k == 0, k == 3, [WB.buf, yb.buf])
                p.dve(lambda e: e.tensor_tensor(out=TA.ap, in0=SG[0].ap, in1=psA.ap, op=ALU.mult),
                      [SG[0].buf, psA.buf], [TA.buf])
                p.dve(lambda e: e.tensor_tensor(out=TB.ap, in0=SG[1].ap, in1=psB.ap, op=ALU.mult),
                      [SG[1].buf, psB.buf], [TB.buf])
                p.pool(lambda e, dc=dc: e.tensor_tensor(out=MER.ap[:, dc, :], in0=TA.ap, in1=TB.ap, op=ALU.add),
                       [TA.buf, TB.buf], [MER.buf])
            for dc in range(KC):
                py = psY[dc % 2]
                for kc in range(KC):
                    self.mm(py, py.ap, WO.ap[:, kc, dc * 128:(dc + 1) * 128], MER.ap[:, kc, :], kc == 0, kc == KC - 1,
                            [WO.buf, MER.buf])
                p.dve(lambda e, py=py, dc=dc, x=x: e.tensor_tensor(out=x.ap[:, dc, :], in0=py.ap, in1=x.ap[:, dc, :],
                                                                   op=ALU.add), [py.buf, x.buf], [x.buf])
            p.dma("sp", xout_v[:, :, c * TT:(c + 1) * TT], x.ap, [x.buf], [xout_b[c]], "XO%d" % i)
        self.phase_end()

C_ID, C_J, C_BO, C_OZ, C_OV = 0, 128, 256, 384, 576
NCST = 576 + 130
LF = 8192 + 2048


WSHAPES = {
    "w_in": (D, NIN),
    "w_branch_a": (512, D),
    "w_branch_b": (512, D),
    "w_out": (D, D),
    "ffn1_w1": (D, DFF),
    "ffn1_w3": (D, DFF),
    "ffn1_w2": (DFF, D),
    "ffn2_w1": (D, DFF),
    "ffn2_w3": (D, DFF),
    "ffn2_w2": (DFF, D),
}

for _l in range(DEPTH):
    for _n in ("ffn1", "mix", "ffn2"):
        Builder.pk_add("g_%s_l%d" % (_n, _l), 8)
    for _n in ("gdq", "gdk", "gnq", "gks", "gkw", "gkc", "gsub"):
        Builder.pk_add("%s_l%d" % (_n, _l), 1)
    Builder.pk_add("lam_l%d" % _l, 256)
    Builder.pk_add("pek_l%d" % _l, 32)
    Builder.pk_add("pev_l%d" % _l, 32)
Builder.pk_add("tb31", 12)

BUCKET_STARTS = [0, 1, 2, 3, 4, 5, 6, 7, 8, 9, 10, 11, 12, 13, 14, 15, 16, 21, 27, 35, 46, 59, 77, 99, 128, 166,
                 216, 280, 363, 470, 609, 790]


def pack_params(inp):
    f = lambda a: np.asarray(a, np.float32)
    cols = []
    bc = lambda v: np.broadcast_to(f(v).reshape(1, -1), (128, f(v).size))
    for l in range(DEPTH):
        for n in ("norm_ffn1", "norm_mix", "norm_ffn2"):
            cols.append(f(inp[n][l]).reshape(8, 128).T)
        t2 = lambda v: np.tile(f(v), 2).reshape(128, 1)
        cols.append(t2(inp["da_q_gain"][l]))
        cols.append(t2(inp["da_k_gain"][l]))
        cols.append(t2(inp["nsa_q_gain"][l]))
        cols.append(t2(inp["nsa_k_gain"][l, 1]))
        cols.append(t2(inp["nsa_k_gain"][l, 2]))
        cols.append(t2(inp["nsa_k_gain"][l, 0]))
        cols.append(f(inp["da_subln_gain"][l]).reshape(128, 1))
        for n in ("da_lambda_q1", "da_lambda_k1", "da_lambda_q2", "da_lambda_k2"):
            cols.append(bc(inp[n][l]))
        for n in ("cmp_pe_k", "cmp_pe_v"):
            cols.append(np.tile(f(inp[n][l]).T, (2, 1)))
    cols.append(bc(inp["rel_bias_table"][31]))
    pk = np.concatenate(cols, axis=1)
    assert pk.shape == (128, Builder.NPK), pk.shape
    return np.ascontiguousarray(pk)


def make_consts(inp):
    cst = np.zeros((128, NCST), np.float32)
    idx = np.arange(128)
    cst[idx, C_ID + idx] = 1.0
    cst[idx, C_J + 127 - idx] = 1.0
    cst[:64, C_BO:C_BO + 64] = 1.0
    cst[64:, C_BO + 64:C_BO + 128] = 1.0
    cst[:, C_OZ + 64:C_OZ + 128] = 1.0
    for ct in range(2):
        for pp in range(128):
            c = ct * 128 + pp
            if c >= 255:
                continue
            for j in range(64):
                if 16 * c < 64 * (j + 1) and 16 * c + 31 >= 64 * j:
                    cst[pp, C_OV + ct * 65 + j] = 1.0
            cst[pp, C_OV + ct * 65 + 64] = 1.0
    bs = np.asarray(BUCKET_STARTS)
    oh = np.zeros((33, LF), np.float32)
    i = np.arange(8192)
    dl = i - 4096
    bk = np.searchsorted(bs, np.maximum(dl, 0), side="right") - 1
    oh[np.where(dl < 0, 32, bk), i] = 1.0
    i = np.arange(2048)
    dl = i - 512
    ok = (dl >= 0) & (dl < 512)
    bk = np.searchsorted(bs, np.clip(dl, 0, None), side="right") - 1
    oh[np.where(ok, bk, 32), 8192 + i] = 1.0
    tblx = np.concatenate([np.asarray(inp["rel_bias_table"], np.float32), np.full((1, 12), NEG, np.float32)], 0)
    ebig = np.zeros((128, S), np.float32)
    ebig[np.arange(S) // 64, np.arange(S)] = 1.0
    gs = np.zeros((24, 1536), np.float32)
    for j in range(4):
        for b in range(3):
            for m in range(128):
                gs[(2 * j + m // 64) * 3 + b, (j * 3 + b) * 128 + m] = 1.0
    q = np.arange(S)
    cur = (q // 64)[:, None]
    jj = np.arange(64)[None, :]
    cadd = np.zeros((S, 64), np.float32)
    cadd[(jj == 0) | (jj == cur) | (jj == cur - 1)] = 1e4
    cadd[jj > cur] = -1e4
    return {"cst": cst, "oh": oh, "tblx": np.ascontiguousarray(tblx), "ebig": ebig, "gs": gs, "cadd": cadd}


_CACHE = {}


def build_full():
    b = Builder()
    b.prologue(list(WSHAPES))
    b.scratch()
    xT = [b.dt("xT%d" % i, [D, S], F32) for i in range(3)]
    out = b.dt("out", [S, D], F32, kind="ExternalOutput")
    b.transpose_in(xT[0])
    for l in range(DEPTH):
        b.ffn_phase(l, 1, xT[0], xT[1])
        b.proj_phase(l, xT[1])
        b.cmp_phase(l)
        b.da_phase(l)
        b.nsa_phase(l)
        b.merge_phase(l, xT[1], xT[2])
        b.ffn_phase(l, 2, xT[2], xT[0])
    b.transpose_out(xT[0], out)
    b.p.emit()
    return b


def kernel(**inputs):
    inp = {k: np.asarray(v) for k, v in inputs.items()}
    if "b" not in _CACHE:
        _CACHE["b"] = build_full()
    b = _CACHE["b"]
    shared = {"pk": pack_params(inp)}
    shared.update(make_consts(inp))
    for n in WSHAPES:
        shared[n] = np.ascontiguousarray(inp[n], dtype=np.float32)
    for n in ("cmp_w1_k", "cmp_w2_k", "cmp_w1_v", "cmp_w2_v"):
        shared[n] = np.ascontiguousarray(inp[n], dtype=np.float32)
    x = np.asarray(inp["x"], np.float32)
    maps = []
    for c in range(NCORES):
        m = dict(shared)
        m["x"] = np.ascontiguousarray(x[c])
        maps.append(m)
    res = run_bass_kernel_spmd(b.nc, maps, core_ids=list(range(NCORES)))
    return np.stack([np.asarray(res.results[c]["out"], np.float32) for c in range(NCORES)], 0)
```

```python
import numpy as np
from contextlib import ExitStack
import concourse.bass as bass
import concourse.mybir as mybir
from concourse.bass_utils import run_bass_kernel_spmd

F32 = mybir.dt.float32
BF16 = mybir.dt.bfloat16
AF = mybir.ActivationFunctionType
ALU = mybir.AluOpType
AX = mybir.AxisListType

S = 4096
D = 1024
DFF = 2816
NIN = 4888
DEPTH = 2
NCORES = 8
EPS = 1e-6
NEG = -30000.0
STRICT = True


class Buf:
    __slots__ = ("name", "writers", "readers")

    def __init__(self, name):
        self.name = name
        self.writers = []
        self.readers = []


class Op:
    __slots__ = ("eng", "fn", "deps", "needed", "semval", "stream", "dma_waits", "sidx")

    def __init__(self, eng, fn, stream=None):
        self.eng = eng
        self.fn = fn
        self.deps = []
        self.needed = False
        self.semval = None
        self.stream = stream
        self.dma_waits = {}
        self.sidx = None


class Stream:
    __slots__ = ("name", "queue", "count", "waiters", "sem")

    def __init__(self, name, queue):
        self.name = name
        self.queue = queue
        self.count = 0
        self.waiters = []
        self.sem = None


ENGS = ("pe", "act", "dve", "pool", "sp")


class Prog:
    def __init__(self, nc, es):
        self.nc = nc
        self.es = es
        self.ops = {e: [] for e in ENGS}
        self.streams = {}
        self.nops = 0

    def sb(self, name, shape, dtype):
        t = self.es.enter_context(self.nc.sbuf_tensor(name, list(shape), dtype))
        return t

    def ps(self, name, shape=(128, 512), dtype=F32):
        t = self.es.enter_context(self.nc.psum_tensor(name, list(shape), dtype))
        return t

    def _dep(self, op, other, raw=False):
        if other is None or other is op:
            return
        if other.stream is not None:
            st = other.stream
            op.dma_waits[st.name] = st.count
            if op not in st.waiters:
                st.waiters.append(op)
            return
        if other.eng == op.eng and op.stream is None and (op.eng == "pe" or not (raw or STRICT)):
            return
        other.needed = True
        op.deps.append(other)

    def op(self, eng, fn, reads=(), writes=(), stream=None, pwrites=()):
        st = None
        if stream is not None:
            st = self.streams.get(stream)
            if st is None:
                st = Stream(stream, eng)
                self.streams[stream] = st
            assert st.queue == eng, (stream, st.queue, eng)
        o = Op(eng, fn, st)
        for b in reads:
            for w in b.writers:
                self._dep(o, w, raw=True)
        for b in writes:
            for w in b.writers:
                self._dep(o, w)
            for r in b.readers:
                self._dep(o, r)
        for b in pwrites:
            for r in b.readers:
                self._dep(o, r)
        if st is not None:
            for w in st.waiters:
                if w.eng == eng:
                    continue
                self._dep(o, w)
            st.waiters = []
            st.count += 1
            o.sidx = st.count
        for b in reads:
            b.readers.append(o)
        for b in writes:
            b.writers = [o]
            b.readers = []
        for b in pwrites:
            if b.readers:
                b.writers = []
                b.readers = []
            b.writers.append(o)
        self.ops[eng].append(o)
        self.nops += 1
        return o

    def pe(self, fn, reads=(), writes=()):
        return self.op("pe", fn, reads, writes)

    def act(self, fn, reads=(), writes=()):
        return self.op("act", fn, reads, writes)

    def dve(self, fn, reads=(), writes=()):
        return self.op("dve", fn, reads, writes)

    def pool(self, fn, reads=(), writes=()):
        return self.op("pool", fn, reads, writes)

    def dma(self, queue, out, in_, reads, writes, stream, pwrites=()):
        return self.op(queue, lambda e: e.dma_start(out=out, in_=in_), reads, writes, stream, pwrites)

    def emit(self):
        nc = self.nc
        es = self.es
        esem = {e: es.enter_context(nc.semaphore("sem_" + e)) for e in ENGS}
        for st in self.streams.values():
            st.sem = es.enter_context(nc.semaphore("dq_" + st.name))
        for e in ENGS:
            c = 0
            for o in self.ops[e]:
                if o.stream is None and o.needed:
                    c += 1
                    o.semval = c
        streams = self.streams
        ops = self.ops

        def run(e, eng):
            waited = {}
            for o in ops[e]:
                w = {}
                for d in o.deps:
                    s = esem[d.eng]
                    k = id(s)
                    if k not in w or w[k][1] < d.semval:
                        w[k] = (s, d.semval)
                for sn, cnt in o.dma_waits.items():
                    s = streams[sn].sem
                    k = id(s)
                    v = cnt * 16
                    if k not in w or w[k][1] < v:
                        w[k] = (s, v)
                for k, (s, v) in w.items():
                    if waited.get(k, 0) < v:
                        eng.wait_ge(s, v)
                        waited[k] = v
                ins = o.fn(eng)
                if o.stream is not None:
                    ins.then_inc(o.stream.sem, 16)
                elif o.needed:
                    ins.then_inc(esem[e], 1)
            if e == "sp":
                for st in streams.values():
                    if st.count:
                        eng.wait_ge(st.sem, st.count * 16)

        blk = es.enter_context(nc.Block())

        @blk.tensor
        def _(eng):
            run("pe", eng)

        @blk.scalar
        def _(eng):
            run("act", eng)

        @blk.vector
        def _(eng):
            run("dve", eng)

        @blk.gpsimd
        def _(eng):
            run("pool", eng)

        @blk.sync
        def _(eng):
            run("sp", eng)


class T:
    __slots__ = ("ap", "buf")

    def __init__(self, ap, name):
        self.ap = ap
        self.buf = Buf(name)


ARENA_WORDS = 51000


class Builder:
    def __init__(self, dbg_in=(), dbg_out=()):
        self.nc = bass.Bass("TRN2", target_bir_lowering=False)
        self.es = ExitStack()
        self.p = Prog(self.nc, self.es)
        self.dbg_in = set(dbg_in)
        self.dbg_out = set(dbg_out)
        self.dram = {}
        self.dbuf = {}
        self.arena = self.p.sb("arena", [128, ARENA_WORDS], F32)
        self.aoff = 0
        self.abase = 0
        self.psum = [T(self.p.ps("psum%d" % i)[:], "psum%d" % i) for i in range(8)]
        self.ntile = 0

    def dt(self, name, shape, dtype, kind=None):
        if kind is None:
            kind = "Internal"
            if name in self.dbg_in:
                kind = "ExternalInput"
            elif name in self.dbg_out:
                kind = "ExternalOutput"
        t = self.nc.dram_tensor(name, list(shape), dtype, kind=kind).ap()
        self.dram[name] = t
        return t

    def db(self, name, n=1):
        if name not in self.dbuf:
            self.dbuf[name] = [Buf("%s_%d" % (name, i)) for i in range(n)]
        return self.dbuf[name]

    def tile(self, name, free, dtype, parts=128):
        n = int(np.prod(free))
        words = n if dtype == F32 else (n + 1) // 2
        assert self.aoff + words <= ARENA_WORDS, (name, self.aoff, words)
        ap = self.arena[0:parts, self.aoff:self.aoff + words]
        if dtype != F32:
            ap = ap.bitcast(dtype)
            if n % 2:
                ap = ap[:, 0:n]
        self.aoff += words
        self.apeak = max(getattr(self, 'apeak', 0), self.aoff)
        if len(free) == 2:
            ap = ap.rearrange("p (a b) -> p a b", a=free[0])
        elif len(free) == 3:
            ap = ap.rearrange("p (a b c) -> p a b c", a=free[0], b=free[1])
        self.ntile += 1
        return T(ap, "%s_%d" % (name, self.ntile))

    def persist(self):
        self.abase = self.aoff

    def phase_end(self):
        p = self.p
        last = {}
        for e in ENGS:
            last[e] = None
            for o in reversed(p.ops[e]):
                if o.stream is None:
                    last[e] = o
                    break
        for e in ENGS:
            o = Op(e, lambda eng: eng.nop())
            for e2 in ENGS:
                l = last[e2]
                if e2 == e or l is None:
                    continue
                if l.stream is None:
                    l.needed = True
                    o.deps.append(l)
            for st in p.streams.values():
                if st.count and not st.name.startswith("cv"):
                    o.dma_waits[st.name] = st.count
                    st.waiters.append(o)
            p.ops[e].append(o)
        self.aoff = self.abase

    PK = {}
    NPK = 0

    @classmethod
    def pk_add(cls, name, width):
        cls.PK[name] = (cls.NPK, width)
        cls.NPK += width

    def pk(self, name):
        o, w = self.PK[name]
        return self.pkt.ap[:, o:o + w]

    def mm(self, pt, out, lhsT, rhs, start, stop, rb):
        self.p.pe(lambda e: e.matmul(out, lhsT=lhsT, rhs=rhs, start=start, stop=stop), rb, [pt.buf])

    def castload(self, name, free, src, parts=128):
        t = self.tile(name, free, BF16)
        self.p.dma("pool", t.ap[0:parts], src, [], [t.buf], "cl")
        return t

    def prologue(self, wnames, full=True):
        p = self.p
        nc = self.nc
        self.x_in = self.dt("x", [S, D], F32, kind="ExternalInput")
        self.pk_in = self.dt("pk", [128, self.NPK], F32, kind="ExternalInput")
        self.cst_in = self.dt("cst", [128, NCST], F32, kind="ExternalInput")
        self.w_in = {}
        self.w_bf = {}
        for name, shape in WSHAPES.items():
            if name not in wnames:
                continue
            self.w_in[name] = self.dt(name, [DEPTH] + list(shape), F32, kind="ExternalInput")
            self.w_bf[name] = self.dt(name + "_bf", [DEPTH] + list(shape), BF16)
        self.pkt = self.tile("pkt", [self.NPK], F32)
        p.dma("sp", self.pkt.ap, self.pk_in, [], [self.pkt.buf], "misc")
        self.ident = self.tile("ident", [128], F32)
        p.dma("sp", self.ident.ap, self.cst_in[:, C_ID:C_ID + 128], [], [self.ident.buf], "misc")
        self.identb = self.castload("identb", [128], self.cst_in[:, C_ID:C_ID + 128])
        self.J = self.castload("J", [128], self.cst_in[:, C_J:C_J + 128])
        self.BO = self.castload("BO", [128], self.cst_in[:, C_BO:C_BO + 128])
        self.OZ = self.castload("OZ", [192], self.cst_in[:, C_OZ:C_OZ + 192])
        self.ones = self.tile("ones", [128], BF16)
        p.pool(lambda e: e.memset(self.ones.ap, 1.0), [], [self.ones.buf])
        self.epsc = self.tile("epsc", [1], F32)
        p.pool(lambda e: e.memset(self.epsc.ap, EPS), [], [self.epsc.buf])
        self.eps64 = self.tile("eps64", [1], F32)
        p.pool(lambda e: e.memset(self.eps64.ap, 64 * EPS), [], [self.eps64.buf])
        self.persist()
        if not full:
            order = ["ffn1_w1", "ffn1_w3", "ffn1_w2", "w_in", "w_branch_a", "w_branch_b", "w_out",
                     "ffn2_w1", "ffn2_w3", "ffn2_w2"]
            for l in range(DEPTH):
                for name in order:
                    if name not in self.w_in:
                        continue
                    src = self.w_in[name][l].rearrange("(p r) c -> p (r c)", p=128)
                    dst = self.w_bf[name][l].rearrange("(p r) c -> p (r c)", p=128)
                    b = self.db(name + "_bf", DEPTH)[l]
                    grp = "A" if name.startswith("ffn1") else ("C" if name.startswith("ffn2") else "B")
                    p.dma("pool", dst, src, [], [b], "cv%s%d" % (grp, l))
            return
        self.oh_in = self.dt("oh", [33, LF], F32, kind="ExternalInput")
        self.tblx_in = self.dt("tblx", [33, 12], F32, kind="ExternalInput")
        self.Fd = self.dt("Fd", [12, LF], BF16)
        fb = self.db("Fd")[0]
        tb = self.tile("tblx", [12], F32)
        p.dma("sp", tb.ap[0:33], self.tblx_in, [], [tb.buf], "misc")
        OHt = self.tile("OHt", [LF], F32)
        FSt = self.tile("FSt", [LF], BF16)
        p.dma("sp", OHt.ap[0:33], self.oh_in, [], [OHt.buf], "OH0")
        for c in range(LF // 512):
            ps = self.psum[c % 4]
            cs_ = slice(c * 512, (c + 1) * 512)
            self.mm(ps, ps.ap[0:12, :], tb.ap[0:33, :], OHt.ap[0:33, cs_], True, True, [tb.buf, OHt.buf])
            if c % 2 == 0:
                p.act(lambda e, ps=ps, cs_=cs_: e.activation(out=FSt.ap[0:12, cs_], in_=ps.ap[0:12, :], func=AF.Copy),
                      [ps.buf], [FSt.buf])
            else:
                p.dve(lambda e, ps=ps, cs_=cs_: e.tensor_copy(out=FSt.ap[0:12, cs_], in_=ps.ap[0:12, :]),
                      [ps.buf], [FSt.buf])
        p.dma("sp", self.Fd, FSt.ap[0:12], [FSt.buf], [fb], "FS0")
        order = ["ffn1_w1", "ffn1_w3", "ffn1_w2", "w_in", "w_branch_a", "w_branch_b", "w_out",
                 "ffn2_w1", "ffn2_w3", "ffn2_w2"]
        for l in range(DEPTH):
            for name in order:
                if name not in self.w_in:
                    continue
                src = self.w_in[name][l].rearrange("(p r) c -> p (r c)", p=128)
                dst = self.w_bf[name][l].rearrange("(p r) c -> p (r c)", p=128)
                b = self.db(name + "_bf", DEPTH)[l]
                grp = "A" if name.startswith("ffn1") else ("C" if name.startswith("ffn2") else "B")
                p.dma("pool", dst, src, [], [b], "cv%s%d" % (grp, l))
        self.phase_end()

    def transpose_in(self, xT):
        p = self.p
        xb = self.db(xT.name, 8)
        A = [self.tile("A", [4, D], F32) for _ in range(2)]
        O = [self.tile("O", [8, 512], F32) for _ in range(2)]
        xv = self.x_in.rearrange("(c r p) d -> c p r d", p=128, r=4)
        ov = xT.rearrange("(kc p) t -> p kc t", p=128)
        n = 0
        for c in range(8):
            a = A[c % 2]
            o = O[c % 2]
            p.dma("sp" if c % 2 == 0 else "act", a.ap, xv[c], [], [a.buf], "A%d" % (c % 2))
            for kc in range(8):
                ps = self.psum[n % 4]
                n += 1
                for r in range(4):
                    p.pe(lambda e, ps=ps, a=a, r=r, kc=kc: e.transpose(
                        out=ps.ap[:, r * 128:(r + 1) * 128], in_=a.ap[:, r, kc * 128:(kc + 1) * 128],
                        identity=self.ident.ap), [a.buf, self.ident.buf], [ps.buf])
                if kc % 2 == 0:
                    p.act(lambda e, ps=ps, o=o, kc=kc: e.activation(out=o.ap[:, kc, :], in_=ps.ap, func=AF.Copy),
                          [ps.buf], [o.buf])
                else:
                    p.dve(lambda e, ps=ps, o=o, kc=kc: e.tensor_copy(out=o.ap[:, kc, :], in_=ps.ap),
                          [ps.buf], [o.buf])
            p.dma("sp", ov[:, :, c * 512:(c + 1) * 512], o.ap, [o.buf], [xb[c]], "O%d" % (c % 2))
        self.phase_end()

    def transpose_out(self, xT, out):
        p = self.p
        xb = self.db(xT.name, 8)
        A = [self.tile("A", [8, 512], F32) for _ in range(2)]
        O = [self.tile("O", [4, D], F32) for _ in range(2)]
        xv = xT.rearrange("(kc p) t -> p kc t", p=128)
        ov = out.rearrange("(c r p) d -> c p r d", p=128, r=4)
        n = 0
        for c in range(8):
            a = A[c % 2]
            o = O[c % 2]
            p.dma("sp" if c % 2 == 0 else "act", a.ap, xv[:, :, c * 512:(c + 1) * 512], [xb[c]], [a.buf],
                  "A%d" % (c % 2))
            for r in range(4):
                for half in range(2):
                    ps = self.psum[n % 4]
                    n += 1
                    for k4 in range(4):
                        kc = half * 4 + k4
                        p.pe(lambda e, ps=ps, a=a, r=r, kc=kc, k4=k4: e.transpose(
                            out=ps.ap[:, k4 * 128:(k4 + 1) * 128], in_=a.ap[:, kc, r * 128:(r + 1) * 128],
                            identity=self.ident.ap), [a.buf, self.ident.buf], [ps.buf])
                    if half == 0:
                        p.act(lambda e, ps=ps, o=o, r=r: e.activation(out=o.ap[:, r, 0:512], in_=ps.ap,
                                                                       func=AF.Copy), [ps.buf], [o.buf])
                    else:
                        p.dve(lambda e, ps=ps, o=o, r=r: e.tensor_copy(out=o.ap[:, r, 512:1024], in_=ps.ap),
                              [ps.buf], [o.buf])
            p.dma("sp", ov[c], o.ap, [o.buf], [], "O%d" % (c % 2), pwrites=[self.db("out")[0]])
        self.phase_end()

    def ffn_phase(self, l, which, xin, xout):
        p = self.p
        TT = 512
        NCH = S // TT
        KC = D // 128
        FC = DFF // 128
        pre = "ffn%d_" % which
        w1d = self.w_bf[pre + "w1"][l]
        w3d = self.w_bf[pre + "w3"][l]
        w2d = self.w_bf[pre + "w2"][l]
        w1b = self.db(pre + "w1_bf", DEPTH)[l]
        w3b = self.db(pre + "w3_bf", DEPTH)[l]
        w2b = self.db(pre + "w2_bf", DEPTH)[l]
        xin_b = self.db(xin.name, NCH)
        xout_b = self.db(xout.name, NCH)
        gain = self.pk("g_ffn%d_l%d" % (which, l))
        xin_v = xin.rearrange("(kc p) t -> p kc t", p=128)
        xout_v = xout.rearrange("(kc p) t -> p kc t", p=128)

        X = [self.tile("X", [KC, TT], F32) for _ in range(2)]
        H = [self.tile("H", [KC, TT], BF16) for _ in range(2)]
        U = self.tile("U", [FC, TT], BF16)
        W2 = self.tile("W2", [FC, D], BF16)
        GW = 512
        groups = [(g0, min(GW, DFF - g0)) for g0 in range(0, DFF, GW)]
        W1 = [self.tile("W1", [KC, GW], BF16) for _ in range(2)]
        W3 = [self.tile("W3", [KC, GW], BF16) for _ in range(2)]
        SQ = [self.tile("SQ", [TT], BF16) for _ in range(2)]
        SA = [self.tile("SA", [TT], F32) for _ in range(2)]
        RS = self.tile("RS", [TT], F32)
        ps_st = self.psum[0]
        ps_a = [self.psum[1], self.psum[2]]
        ps_b = [self.psum[3], self.psum[4]]
        ps_y = [self.psum[5], self.psum[6]]

        w2v = w2d.rearrange("(fc p) d -> p fc d", p=128)
        for i, (a, b) in enumerate([(0, 6), (6, 12), (12, 17), (17, 22)]):
            q = "sp" if i % 2 == 0 else "act"
            p.dma(q, W2.ap[:, a:b, :], w2v[:, a:b, :], [w2b], [], "W2_" + q, pwrites=[W2.buf])

        def load_x(c):
            t = X[c % 2]
            p.dma("sp", t.ap, xin_v[:, :, c * TT:(c + 1) * TT], [xin_b[c]], [t.buf], "X%d" % (c % 2))

        def norm(c):
            x = X[c % 2]
            h = H[c % 2]
            for kc in range(KC):
                sq = SQ[kc % 2]
                p.act(lambda e, sq=sq, kc=kc: e.activation(out=sq.ap, in_=x.ap[:, kc, :], func=AF.Square),
                      [x.buf], [sq.buf])
                p.pe(lambda e, sq=sq, kc=kc: e.matmul(ps_st.ap, lhsT=self.ones.ap, rhs=sq.ap,
                                                       start=(kc == 0), stop=(kc == KC - 1)),
                     [sq.buf, self.ones.buf], [ps_st.buf])
            p.act(lambda e: e.activation(out=RS.ap, in_=ps_st.ap, func=AF.Ln, bias=self.epsc.ap,
                                         scale=1.0 / D), [ps_st.buf, self.epsc.buf], [RS.buf])
            p.act(lambda e: e.activation(out=RS.ap, in_=RS.ap, func=AF.Exp, scale=-0.5), [RS.buf], [RS.buf])
            for kc in range(KC):
                p.dve(lambda e, kc=kc: e.scalar_tensor_tensor(
                    out=h.ap[:, kc, :], in0=x.ap[:, kc, :], scalar=gain[:, kc:kc + 1], in1=RS.ap,
                    op0=ALU.mult, op1=ALU.mult), [x.buf, RS.buf, self.pkt.buf], [h.buf])

        wcnt = [0]

        def load_w(gi):
            g0, gw = groups[gi]
            i = wcnt[0] % 2
            wcnt[0] += 1
            v1 = w1d.rearrange("(kc p) f -> p kc f", p=128)
            v3 = w3d.rearrange("(kc p) f -> p kc f", p=128)
            p.dma("sp", W1[i].ap[:, :, 0:gw], v1[:, :, g0:g0 + gw], [w1b], [W1[i].buf], "W1_%d" % i)
            p.dma("act", W3[i].ap[:, :, 0:gw], v3[:, :, g0:g0 + gw], [w3b], [W3[i].buf], "W3_%d" % i)
            return i

        load_x(0)
        norm(0)
        nxt = load_w(0)
        mmc = [0]
        for c in range(NCH):
            x = X[c % 2]
            h = H[c % 2]
            if c + 1 < NCH:
                load_x(c + 1)
            for gi, (g0, gw) in enumerate(groups):
                cur = nxt
                if not (c == NCH - 1 and gi == len(groups) - 1):
                    nxt = load_w((gi + 1) % len(groups))
                for j in range(gw // 128):
                    fc = g0 // 128 + j
                    pa = ps_a[mmc[0] % 2]
                    pb = ps_b[mmc[0] % 2]
                    sa = SA[mmc[0] % 2]
                    mmc[0] += 1
                    for kc in range(KC):
                        p.pe(lambda e, pa=pa, kc=kc, cur=cur, j=j, h=h: e.matmul(
                            pa.ap, lhsT=W1[cur].ap[:, kc, j * 128:(j + 1) * 128], rhs=h.ap[:, kc, :],
                            start=(kc == 0), stop=(kc == KC - 1)), [W1[cur].buf, h.buf], [pa.buf])
                    for kc in range(KC):
                        p.pe(lambda e, pb=pb, kc=kc, cur=cur, j=j, h=h: e.matmul(
                            pb.ap, lhsT=W3[cur].ap[:, kc, j * 128:(j + 1) * 128], rhs=h.ap[:, kc, :],
                            start=(kc == 0), stop=(kc == KC - 1)), [W3[cur].buf, h.buf], [pb.buf])
                    p.act(lambda e, pa=pa, sa=sa: e.activation(out=sa.ap, in_=pa.ap, func=AF.Silu),
                          [pa.buf], [sa.buf])
                    p.dve(lambda e, pb=pb, sa=sa, fc=fc: e.tensor_tensor(
                        out=U.ap[:, fc, :], in0=sa.ap, in1=pb.ap, op=ALU.mult),
                        [sa.buf, pb.buf], [U.buf])
                if gi == 2 and c + 1 < NCH:
                    norm(c + 1)
            for dc in range(KC):
                py = ps_y[dc % 2]
                for fc in range(FC):
                    p.pe(lambda e, py=py, fc=fc, dc=dc: e.matmul(
                        py.ap, lhsT=W2.ap[:, fc, dc * 128:(dc + 1) * 128], rhs=U.ap[:, fc, :],
                        start=(fc == 0), stop=(fc == FC - 1)), [W2.buf, U.buf], [py.buf])
                p.dve(lambda e, py=py, dc=dc, x=x: e.scalar_tensor_tensor(
                    out=x.ap[:, dc, :], in0=py.ap, scalar=0.5, in1=x.ap[:, dc, :],
                    op0=ALU.mult, op1=ALU.add), [py.buf, x.buf], [x.buf])
            p.dma("sp", xout_v[:, :, c * TT:(c + 1) * TT], x.ap, [x.buf], [xout_b[c]], "XO%d" % (c % 2))
        self.phase_end()


    def xnorm(self, x, h, gain, SQ, RS, ps_st, TT=512):
        p = self.p
        KC = 8
        for kc in range(KC):
            sq = SQ[kc % 2]
            p.act(lambda e, sq=sq, kc=kc: e.activation(out=sq.ap, in_=x.ap[:, kc, :], func=AF.Square),
                  [x.buf], [sq.buf])
            self.mm(ps_st, ps_st.ap, self.ones.ap, sq.ap, kc == 0, kc == KC - 1, [sq.buf, self.ones.buf])
        p.act(lambda e: e.activation(out=RS.ap, in_=ps_st.ap, func=AF.Ln, bias=self.epsc.ap,
                                     scale=1.0 / D), [ps_st.buf, self.epsc.buf], [RS.buf])
        p.act(lambda e: e.activation(out=RS.ap, in_=RS.ap, func=AF.Exp, scale=-0.5), [RS.buf], [RS.buf])
        for kc in range(KC):
            p.dve(lambda e, kc=kc: e.scalar_tensor_tensor(
                out=h.ap[:, kc, :], in0=x.ap[:, kc, :], scalar=gain[:, kc:kc + 1], in1=RS.ap,
                op0=ALU.mult, op1=ALU.mult), [x.buf, RS.buf, self.pkt.buf], [h.buf])

    def scratch(self):
        d = self.dt
        self.hT = d("hT", [D, S], BF16)
        self.daq = d("daq", [512, S], BF16)
        self.dak = d("dak", [512, S], BF16)
        self.dav = d("dav", [S, 512], BF16)
        self.nqT = d("nqT", [512, S], BF16)
        self.kcT = d("kcT", [128, S], BF16)
        self.vcT = d("vcT", [128, S], BF16)
        self.kselT = d("kselT", [256, S], BF16)
        self.kwinT = d("kwinT", [256, S], BF16)
        self.vsw = d("vsw", [S, 256], BF16)
        self.ngT = d("ngT", [24, S], BF16)
        self.kcn = d("kcn", [2, 128, 256], BF16)
        self.vcz = d("vcz", [2, 2, 128, 192], BF16)
        self.yaT = d("yaT", [512, S], BF16)
        self.ybT = d("ybT", [512, S], BF16)

    def proj_phase(self, l, xin):
        p = self.p
        TT, NCH, KC = 512, 8, 8
        wd = self.w_bf["w_in"][l]
        wb = self.db("w_in_bf", DEPTH)[l]
        wv = wd.rearrange("(kc p) c -> p kc c", p=128)
        xin_b = self.db(xin.name, NCH)
        xin_v = xin.rearrange("(kc p) t -> p kc t", p=128)
        gain = self.pk("g_mix_l%d" % l)
        NW = 2840
        WIN = self.tile("WIN", [KC, NW], BF16)
        for i in range(4):
            q = "sp" if i % 2 == 0 else "act"
            p.dma(q, WIN.ap[:, :, i * 710:(i + 1) * 710], wv[:, :, i * 710:(i + 1) * 710], [wb], [],
                  "WIN_" + q, pwrites=[WIN.buf])
        WD = self.tile("WD", [KC, 512], BF16)
        for i, off in enumerate([2304, 2368, 2560, 2624]):
            for hh in range(2):
                q = "sp" if hh == 0 else "act"
                p.dma(q, WD.ap[:, :, i * 128 + hh * 64:i * 128 + hh * 64 + 64], wv[:, :, off:off + 64], [wb], [],
                      "WD_" + q, pwrites=[WD.buf])
        X = [self.tile("X", [KC, TT], F32) for _ in range(2)]
        H = [self.tile("H", [KC, TT], BF16) for _ in range(2)]
        SQ = [self.tile("SQ", [TT], BF16) for _ in range(2)]
        RS = self.tile("RS", [TT], F32)
        SQX = [self.tile("SQX", [TT], BF16) for _ in range(2)]
        RB = [self.tile("RB", [TT], F32) for _ in range(2)]
        OF = [self.tile("OF", [TT], BF16) for _ in range(3)]
        OT = [self.tile("OT", [TT], BF16) for _ in range(2)]
        ps_st = self.psum[0]
        ps_z = [self.psum[1], self.psum[2]]
        ps_b = [self.psum[3], self.psum[4]]
        ps_t = [self.psum[5], self.psum[6]]
        hT_b = self.db("hT", NCH)
        hT_v = self.hT.rearrange("(kc p) t -> p kc t", p=128)

        def load_x(c):
            t = X[c % 2]
            p.dma("sp", t.ap, xin_v[:, :, c * TT:(c + 1) * TT], [xin_b[c]], [t.buf], "X%d" % (c % 2))

        specs = []
        for j in range(4):
            specs.append((self.daq, j * 128, (WIN, j * 128), "gdq", True))
        for j in range(4):
            specs.append((self.dak, j * 128, (WIN, 512 + j * 128), "gdk", False))
        for j in range(4):
            specs.append((self.nqT, j * 128, (WIN, 1536 + j * 128), "gnq", True))
        for g in range(2):
            specs.append((self.kselT, g * 128, (WD, g * 128), "gks", False))
        for g in range(2):
            specs.append((self.kwinT, g * 128, (WD, (2 + g) * 128), "gkw", False))
        cnt = [0, 0, 0]
        ps_z = [self.psum[1], self.psum[2], self.psum[3]]
        ps_b = [self.psum[4], self.psum[5]]
        ps_t = [self.psum[6], self.psum[7]]
        load_x(0)
        self.xnorm(X[0], H[0], gain, SQX, RS, ps_st)
        p.dma("act", hT_v[:, :, 0:TT], H[0].ap, [H[0].buf], [hT_b[0]], "HO0")
        for c in range(NCH):
            x = X[c % 2]
            h = H[c % 2]
            if c + 1 < NCH:
                load_x(c + 1)
            tsl = slice(c * TT, (c + 1) * TT)
            st_ = {}

            def stage1(i, h=h):
                (dst, r0, (wt, c0), gname, qt) = specs[i]
                pz = ps_z[cnt[0] % 3]
                sq = SQ[cnt[0] % 2]
                st_[i] = (pz, sq, cnt[0])
                cnt[0] += 1
                for kc in range(KC):
                    self.mm(pz, pz.ap, wt.ap[:, kc, c0:c0 + 128], h.ap[:, kc, :], kc == 0, kc == KC - 1,
                            [wt.buf, h.buf])
                p.act(lambda e: e.activation(out=sq.ap, in_=pz.ap, func=AF.Square), [pz.buf], [sq.buf])

            def stage2(i, tsl=tsl):
                (dst, r0, (wt, c0), gname, qt) = specs[i]
                pz, sq, k = st_.pop(i)
                pb = ps_b[k % 2]
                rb = RB[k % 2]
                of = OF[k % 3]
                self.mm(pb, pb.ap, self.BO.ap, sq.ap, True, True, [self.BO.buf, sq.buf])
                if qt:
                    p.act(lambda e: e.activation(out=rb.ap, in_=pb.ap, func=AF.Ln, bias=self.eps64.ap, scale=1.0),
                          [pb.buf, self.eps64.buf], [rb.buf])
                else:
                    p.act(lambda e: e.activation(out=rb.ap, in_=pb.ap, func=AF.Ln, bias=self.epsc.ap,
                                                 scale=1.0 / 64), [pb.buf, self.epsc.buf], [rb.buf])
                p.act(lambda e: e.activation(out=rb.ap, in_=rb.ap, func=AF.Exp, scale=-0.5), [rb.buf], [rb.buf])
                gcol = self.pk("%s_l%d" % (gname, l))
                p.dve(lambda e: e.scalar_tensor_tensor(out=of.ap, in0=pz.ap, scalar=gcol[:, 0:1], in1=rb.ap,
                                                       op0=ALU.mult, op1=ALU.mult),
                      [pz.buf, rb.buf, self.pkt.buf], [of.buf])
                p.dma("sp", dst[r0:r0 + 128, tsl], of.ap, [of.buf], [], "OF%d" % (k % 3),
                      pwrites=[self.db(dst.name)[0]])

            ns = len(specs)
            for i in range(ns + 1):
                if i < ns:
                    stage1(i)
                if i >= 1:
                    stage2(i - 1)
                if i == 8 and c + 1 < NCH:
                    self.xnorm(X[(c + 1) % 2], H[(c + 1) % 2], gain, SQX, RS, ps_st)
                    p.dma("act", hT_v[:, :, (c + 1) * TT:(c + 2) * TT], H[(c + 1) % 2].ap, [H[(c + 1) % 2].buf],
                          [hT_b[c + 1]], "HO%d" % ((c + 1) % 2))
            for dst, c0 in ((self.kcT, 2048), (self.vcT, 2176)):
                pz = ps_z[cnt[0] % 2]
                of = OF[cnt[0] % 3]
                cnt[0] += 1
                for kc in range(KC):
                    self.mm(pz, pz.ap, WIN.ap[:, kc, c0:c0 + 128], h.ap[:, kc, :], kc == 0, kc == KC - 1,
                            [WIN.buf, h.buf])
                p.act(lambda e, of=of, pz=pz: e.activation(out=of.ap, in_=pz.ap, func=AF.Copy),
                      [pz.buf], [of.buf])
                p.dma("sp", dst[:, tsl], of.ap, [of.buf], [], "OF%d" % ((cnt[0] - 1) % 3),
                      pwrites=[self.db(dst.name)[0]])
            pz = ps_z[cnt[0] % 2]
            of = OF[cnt[0] % 3]
            cnt[0] += 1
            for kc in range(KC):
                self.mm(pz, pz.ap[0:24, :], WIN.ap[:, kc, 2816:2840], h.ap[:, kc, :], kc == 0, kc == KC - 1,
                        [WIN.buf, h.buf])
            p.act(lambda e, of=of, pz=pz: e.activation(out=of.ap[0:24], in_=pz.ap[0:24, :], func=AF.Sigmoid),
                  [pz.buf], [of.buf])
            p.dma("sp", self.ngT[:, tsl], of.ap[0:24], [of.buf], [], "OF%d" % ((cnt[0] - 1) % 3),
                  pwrites=[self.db("ngT")[0]])
            for ts in range(4):
                pt = ps_t[cnt[1] % 2]
                ot = OT[cnt[1] % 2]
                cnt[1] += 1
                for kc in range(KC):
                    self.mm(pt, pt.ap, h.ap[:, kc, ts * 128:(ts + 1) * 128], WIN.ap[:, kc, 1024:1536],
                            kc == 0, kc == KC - 1, [WIN.buf, h.buf])
                p.dve(lambda e, ot=ot, pt=pt: e.tensor_copy(out=ot.ap, in_=pt.ap), [pt.buf], [ot.buf])
                r0 = c * TT + ts * 128
                p.dma("act", self.dav[r0:r0 + 128, :], ot.ap, [ot.buf], [], "OT%d" % ((cnt[1] - 1) % 2),
                      pwrites=[self.db("dav")[0]])
                pt = ps_t[cnt[1] % 2]
                ot = OT[cnt[1] % 2]
                cnt[1] += 1
                for i, c0 in enumerate((2432, 2688)):
                    for kc in range(KC):
                        self.mm(pt, pt.ap[:, i * 128:(i + 1) * 128], h.ap[:, kc, ts * 128:(ts + 1) * 128],
                                WIN.ap[:, kc, c0:c0 + 128], kc == 0, kc == KC - 1, [WIN.buf, h.buf])
                p.dve(lambda e, ot=ot, pt=pt: e.tensor_copy(out=ot.ap[:, 0:256], in_=pt.ap[:, 0:256]),
                      [pt.buf], [ot.buf])
                p.dma("act", self.vsw[r0:r0 + 128, :], ot.ap[:, 0:256], [ot.buf], [],
                      "OT%d" % ((cnt[1] - 1) % 2), pwrites=[self.db("vsw")[0]])
        self.phase_end()

    def cmp_phase(self, l):
        p = self.p
        self.cw = {}
        for n in ("cmp_w1_k", "cmp_w2_k", "cmp_w1_v", "cmp_w2_v"):
            if n not in self.dram:
                shp = [DEPTH, 2048, 128] if "w1" in n else [DEPTH, 128, 64]
                self.dt(n, shp, F32, kind="ExternalInput")
        NC_ = 255
        ps_b, ps_h, ps_o, ps_s = self.psum[0], self.psum[1], self.psum[2], self.psum[3]
        for t in ("k", "v"):
            w1 = self.dram["cmp_w1_" + t][l].rearrange("(l d) h -> d l h", d=64)
            w2 = self.dram["cmp_w2_" + t][l]
            W1c = self.tile("W1c", [32, 128], BF16)
            W2c = self.tile("W2c", [128], BF16)
            for hh in range(2):
                p.dma("pool", W1c.ap[hh * 64:(hh + 1) * 64], w1, [], [], "W1c", pwrites=[W1c.buf])
                p.dma("pool", W2c.ap[:, hh * 64:(hh + 1) * 64], w2, [], [], "W2c", pwrites=[W2c.buf])
            peb = self.tile("peb", [32], BF16)
            pe32 = self.pk("pe%s_l%d" % (t, l))
            p.dve(lambda e, peb=peb, pe32=pe32: e.tensor_copy(out=peb.ap, in_=pe32), [self.pkt.buf], [peb.buf])
            KR = self.tile("KR", [S], BF16)
            src = self.kcT if t == "k" else self.vcT
            p.dma("sp", KR.ap, src, [self.db(src.name)[0]], [KR.buf], "KR" + t)
            for g in range(2):
                rows = slice(g * 64, (g + 1) * 64)
                bcol = self.tile("bcol", [1], F32)
                for li in range(32):
                    self.mm(ps_b, ps_b.ap[:, 0:1], W1c.ap[rows, li, :], peb.ap[rows, li:li + 1], li == 0, li == 31,
                            [W1c.buf, peb.buf])
                p.dve(lambda e, bcol=bcol: e.tensor_copy(out=bcol.ap, in_=ps_b.ap[:, 0:1]), [ps_b.buf], [bcol.buf])
                for li in range(32):
                    self.mm(ps_h, ps_h.ap[:, 0:NC_], W1c.ap[rows, li, :], KR.ap[rows, li:li + 16 * 254 + 1:16],
                            li == 0, li == 31, [W1c.buf, KR.buf])
                TS = self.tile("TS", [256], F32)
                T2 = self.tile("T2", [256], F32)
                GL = self.tile("GL", [256], BF16)
                p.act(lambda e, TS=TS, bcol=bcol: e.activation(out=TS.ap[:, 0:NC_], in_=ps_h.ap[:, 0:NC_],
                                                               func=AF.Identity, bias=bcol.ap, scale=1.0),
                      [ps_h.buf, bcol.buf], [TS.buf])
                p.dve(lambda e, TS=TS, T2=T2: e.tensor_tensor(out=T2.ap[:, 0:NC_], in0=TS.ap[:, 0:NC_],
                                                              in1=TS.ap[:, 0:NC_], op=ALU.mult), [TS.buf], [T2.buf])
                p.dve(lambda e, T2=T2: e.tensor_scalar(out=T2.ap[:, 0:NC_], in0=T2.ap[:, 0:NC_], scalar1=0.044715,
                                                       scalar2=1.0, op0=ALU.mult, op1=ALU.add), [T2.buf], [T2.buf])
                p.dve(lambda e, TS=TS, T2=T2: e.tensor_tensor(out=T2.ap[:, 0:NC_], in0=T2.ap[:, 0:NC_],
                                                              in1=TS.ap[:, 0:NC_], op=ALU.mult), [TS.buf, T2.buf],
                      [T2.buf])
                p.act(lambda e, T2=T2: e.activation(out=T2.ap[:, 0:NC_], in_=T2.ap[:, 0:NC_], func=AF.Sigmoid,
                                                    scale=1.5957691216057308), [T2.buf], [T2.buf])
                p.dve(lambda e, TS=TS, T2=T2, GL=GL: e.tensor_tensor(out=GL.ap[:, 0:NC_], in0=T2.ap[:, 0:NC_],
                                                                     in1=TS.ap[:, 0:NC_], op=ALU.mult),
                      [TS.buf, T2.buf], [GL.buf])
                if t == "k":
                    self.mm(ps_o, ps_o.ap[:, 0:NC_], W2c.ap, GL.ap[:, 0:NC_], True, True, [W2c.buf, GL.buf])
                    SQ = self.tile("SQ", [256], BF16)
                    RB = self.tile("RB", [256], F32)
                    KO = self.tile("KO", [256], BF16)
                    p.pool(lambda e, KO=KO: e.memset(KO.ap, 0.0), [], [KO.buf])
                    p.act(lambda e, SQ=SQ: e.activation(out=SQ.ap[:, 0:NC_], in_=ps_o.ap[:, 0:NC_], func=AF.Square),
                          [ps_o.buf], [SQ.buf])
                    self.mm(ps_s, ps_s.ap[:, 0:NC_], self.BO.ap, SQ.ap[:, 0:NC_], True, True, [self.BO.buf, SQ.buf])
                    p.act(lambda e, RB=RB: e.activation(out=RB.ap[:, 0:NC_], in_=ps_s.ap[:, 0:NC_], func=AF.Sqrt,
                                                        bias=self.epsc.ap, scale=1.0 / 64),
                          [ps_s.buf, self.epsc.buf], [RB.buf])
                    p.dve(lambda e, RB=RB: e.reciprocal(out=RB.ap[:, 0:NC_], in_=RB.ap[:, 0:NC_]), [RB.buf], [RB.buf])
                    gcol = self.pk("gkc_l%d" % l)
                    p.dve(lambda e, KO=KO, RB=RB, gcol=gcol: e.scalar_tensor_tensor(
                        out=KO.ap[:, 0:NC_], in0=ps_o.ap[:, 0:NC_], scalar=gcol[:, 0:1], in1=RB.ap[:, 0:NC_],
                        op0=ALU.mult, op1=ALU.mult), [ps_o.buf, RB.buf, self.pkt.buf, KO.buf], [KO.buf])
                    p.dma("sp", self.kcn[g], KO.ap, [KO.buf], [], "KO", pwrites=[self.db("kcn")[0]])
                else:
                    for ct in range(2):
                        n = 128 if ct == 0 else 127
                        VZ = self.tile("VZ", [192], BF16)
                        p.pool(lambda e, VZ=VZ: e.memset(VZ.ap, 0.0), [], [VZ.buf])
                        self.mm(ps_o, ps_o.ap[0:n, 0:64], GL.ap[:, ct * 128:ct * 128 + n], W2c.ap[:, 0:64], True, True,
                                [W2c.buf, GL.buf])
                        p.dve(lambda e, VZ=VZ, n=n: e.tensor_copy(out=VZ.ap[0:n, 64:128], in_=ps_o.ap[0:n, 0:64]),
                              [ps_o.buf, VZ.buf], [VZ.buf])
                        p.dma("sp", self.vcz[g, ct], VZ.ap, [VZ.buf], [], "VZ", pwrites=[self.db("vcz")[0]])
        self.phase_end()

    def da_phase(self, l):
        p = self.p
        lam_init = 0.8 - 0.6 * float(np.exp(-0.3 * l))
        LB = self.pk("lam_l%d" % l)
        junk = self.tile("junk", [64], F32)
        d12 = self.tile("d12", [2], F32)
        lamneg = self.tile("lamneg", [1], F32)
        sbias = self.tile("sbias", [1], F32)
        p.pool(lambda e: e.memset(sbias.ap, EPS / (1.0 - lam_init) ** 2), [], [sbias.buf])
        for i in range(2):
            p.dve(lambda e, i=i: e.tensor_tensor(out=junk.ap, in0=LB[:, i * 128:i * 128 + 64],
                                                 in1=LB[:, i * 128 + 64:i * 128 + 128], op=ALU.mult),
                  [self.pkt.buf], [junk.buf])
            p.dve(lambda e, i=i: e.tensor_reduce(out=d12.ap[:, i:i + 1], in_=junk.ap, axis=AX.X, op=ALU.add),
                  [junk.buf], [d12.buf])
        p.act(lambda e: e.activation(out=d12.ap, in_=d12.ap, func=AF.Exp), [d12.buf], [d12.buf])
        p.dve(lambda e: e.tensor_tensor(out=lamneg.ap, in0=d12.ap[:, 1:2], in1=d12.ap[:, 0:1], op=ALU.subtract),
              [d12.buf], [lamneg.buf])
        p.dve(lambda e: e.tensor_scalar_add(out=lamneg.ap, in0=lamneg.ap, scalar1=-lam_init), [lamneg.buf],
              [lamneg.buf])
        sscale = 1.0 / (128.0 * (1.0 - lam_init) ** 2)
        gsub = self.pk("gsub_l%d" % l)
        tb31 = self.pk("tb31")
        QT = [self.tile("QT", [S], BF16) for _ in range(2)]
        KT = [[self.tile("KT", [S], BF16) for _ in range(2)] for _ in range(2)]
        for i2 in range(2):
            for comp in range(2):
                p.pool(lambda e, t=KT[i2][comp]: e.memset(t.ap, 0.0), [], [KT[i2][comp].buf])
        VV = [self.tile("VV", [32, 128], BF16) for _ in range(2)]
        ST = [self.tile("ST", [1792], BF16) for _ in range(2)]
        STR = self.tile("STR", [1792], BF16)
        PT = [self.tile("PT", [512], BF16) for _ in range(8)]
        PT0 = [self.tile("PT0", [512], BF16) for _ in range(4)]
        R = [self.tile("R", [512], F32) for _ in range(2)]
        A = [self.tile("A", [512], F32) for _ in range(2)]
        Y = self.tile("Y", [512], F32)
        SQ = self.tile("SQ", [512], BF16)
        RS = self.tile("RS", [512], F32)
        YO = [self.tile("YO", [512], BF16) for _ in range(2)]
        psS = self.psum[0:4]
        psO = self.psum[4:6]
        psL = self.psum[6:8]
        qb, kb, vb, fb = self.db("daq")[0], self.db("dak")[0], self.db("dav")[0], self.db("Fd")[0]
        ya_b = self.db("yaT")[0]
        OC = [self.tile("OC", [512], F32) for _ in range(2)]
        LC = [self.tile("LC", [512], F32) for _ in range(2)]
        psS = self.psum[0:4]
        tiles = {}
        cnt = {"s": 0, "pt": 0, "p0": 0}

        def load_head(h):
            i2 = h % 2
            qt, vv, st = QT[i2], VV[i2], ST[i2]
            p.dma("sp", qt.ap, self.daq[h * 128:(h + 1) * 128, :], [qb], [qt.buf], "QT%d" % i2)
            for comp in range(2):
                kt_ = KT[i2][comp]
                rs_ = slice(comp * 64, comp * 64 + 64)
                p.dma("act", kt_.ap[rs_], self.dak[h * 128 + comp * 64:h * 128 + comp * 64 + 64, :], [kb], [kt_.buf],
                      "KT%d%d" % (i2, comp))
            p.dma("sp", vv.ap, self.dav[:, h * 128:(h + 1) * 128].rearrange("(kt p) e -> p kt e", p=128), [vb],
                  [vv.buf], "VV%d" % i2)
            p.dma("act", STR.ap, bass.AP(self.Fd.tensor, h * LF + 3585, [[1, 128], [1, 1792]]), [fb], [STR.buf],
                  "STR")
            for c4 in range(4):
                ps = psS[cnt["s"] % 4]
                cnt["s"] += 1
                cs_ = slice(c4 * 448, (c4 + 1) * 448)
                self.mm(ps, ps.ap[:, 0:448], self.J.ap, STR.ap[:, cs_], True, True, [self.J.buf, STR.buf])
                p.act(lambda e, ps=ps, cs_=cs_: e.activation(out=st.ap[:, cs_], in_=ps.ap[:, 0:448], func=AF.Exp),
                      [ps.buf], [st.buf])

        jobs = []
        for h in range(4):
            for qc in range(8):
                nk = 4 * qc + 4
                for kt in range(nk):
                    for comp in range(2):
                        jobs.append((h, qc, kt, comp, nk))
        def front(j):
            h, qc, kt, comp, nk = jobs[j]
            i2 = h % 2
            qt, kt_, st = QT[i2], KT[i2][comp], ST[i2]
            qs = slice(qc * 512, (qc + 1) * 512)
            delta = qc * 512 - kt * 128
            near = delta <= 896
            ps = psS[cnt["s"] % 4]
            cnt["s"] += 1
            self.mm(ps, ps.ap, kt_.ap[:, kt * 128:(kt + 1) * 128], qt.ap[:, qs], True, True, [kt_.buf, qt.buf])
            pt = PT[cnt["pt"] % 8]
            cnt["pt"] += 1
            if near:
                p0 = PT0[cnt["p0"] % 4]
                cnt["p0"] += 1
                p.act(lambda e: e.activation(out=p0.ap, in_=ps.ap, func=AF.Exp), [ps.buf], [p0.buf])
                p.dve(lambda e: e.tensor_tensor(out=pt.ap, in0=p0.ap, in1=st.ap[:, delta + 384:delta + 896],
                                                op=ALU.mult), [p0.buf, st.buf], [pt.buf])
            else:
                p.act(lambda e: e.activation(out=pt.ap, in_=ps.ap, func=AF.Exp, bias=tb31[:, h:h + 1], scale=1.0),
                      [ps.buf, self.pkt.buf], [pt.buf])
            tiles[j] = pt

        steps = []

        def epi(h, qc):
            qs = slice(qc * 512, (qc + 1) * 512)
            for comp in range(2):
                p.act(lambda e, comp=comp: e.activation(out=LC[comp].ap, in_=psL[comp].ap, func=AF.Ln),
                      [psL[comp].buf], [LC[comp].buf])
                p.act(lambda e, comp=comp: e.activation(out=R[comp].ap, in_=LC[comp].ap, func=AF.Exp, scale=-1.0),
                      [LC[comp].buf], [R[comp].buf])
                p.dve(lambda e, comp=comp: e.tensor_tensor(out=A[comp].ap, in0=psO[comp].ap, in1=R[comp].ap,
                                                           op=ALU.mult), [psO[comp].buf, R[comp].buf], [A[comp].buf])
            steps.append(lambda: p.dve(lambda e: e.scalar_tensor_tensor(
                out=Y.ap, in0=A[1].ap, scalar=lamneg.ap[:, 0:1], in1=A[0].ap, op0=ALU.mult, op1=ALU.add),
                [A[0].buf, A[1].buf, lamneg.buf], [Y.buf]))
            steps.append(lambda: p.act(lambda e: e.activation(out=SQ.ap, in_=Y.ap, func=AF.Square), [Y.buf], [SQ.buf]))
            steps.append(None)
            steps.append(None)

            def stat():
                ps = psS[cnt["s"] % 4]
                cnt["s"] += 1
                self.mm(ps, ps.ap, self.ones.ap, SQ.ap, True, True, [self.ones.buf, SQ.buf])
                p.act(lambda e: e.activation(out=RS.ap, in_=ps.ap, func=AF.Ln, bias=sbias.ap, scale=sscale),
                      [ps.buf, sbias.buf], [RS.buf])
                p.act(lambda e: e.activation(out=RS.ap, in_=RS.ap, func=AF.Exp, scale=-0.5), [RS.buf], [RS.buf])
            steps.append(stat)
            steps.append(None)
            yo = YO[qc % 2]

            def fin():
                p.dve(lambda e: e.scalar_tensor_tensor(out=yo.ap, in0=Y.ap, scalar=gsub[:, 0:1], in1=RS.ap,
                                                       op0=ALU.mult, op1=ALU.mult),
                      [Y.buf, RS.buf, self.pkt.buf], [yo.buf])
                p.dma("sp", self.yaT[h * 128:(h + 1) * 128, qs], yo.ap, [yo.buf], [], "YO%d" % (qc % 2),
                      pwrites=[ya_b])
            steps.append(fin)

        def drain(n):
            while n > 0 and steps:
                f = steps.pop(0)
                if f is not None:
                    f()
                n -= 1

        def back(j):
            h, qc, kt, comp, nk = jobs[j]
            vv = VV[h % 2]
            pt = tiles.pop(j)
            self.mm(psO[comp], psO[comp].ap, vv.ap[:, kt, :], pt.ap, kt == 0, kt == nk - 1, [vv.buf, pt.buf])
            self.mm(psL[comp], psL[comp].ap, self.ones.ap, pt.ap, kt == 0, kt == nk - 1, [self.ones.buf, pt.buf])
            if kt == nk - 1 and comp == 1:
                drain(len(steps))
                epi(h, qc)
                if qc == 7 and h + 2 < 4:
                    load_head(h + 2)

        LA = 3
        load_head(0)
        load_head(1)
        n = len(jobs)
        for i in range(n + LA):
            if i < n:
                front(i)
            if i >= LA:
                back(i - LA)
            drain(2)
        drain(len(steps))
        self.phase_end()

    def nsa_phase(self, l):
        p = self.p
        for n, shp in (("ebig", [128, S]), ("gs", [24, 1536]), ("cadd", [S, 64])):
            if n not in self.dram:
                self.dt(n, shp, F32, kind="ExternalInput")
        EB = self.castload("EB", [S], self.dram["ebig"])
        GS = self.castload("GS", [1536], self.dram["gs"], parts=24)
        OV = self.castload("OV", [2, 65], self.cst_in[:, C_OV:C_OV + 130].rearrange("p (a b) -> p a b", a=2))
        CA = self.tile("CA", [32, 64], F32)
        p.dma("sp", CA.ap, self.dram["cadd"].rearrange("(qt p) j -> p qt j", p=128), [], [CA.buf], "CA")
        NG = self.tile("NG", [S], BF16)
        p.dma("act", NG.ap[0:24], self.ngT, [self.db("ngT")[0]], [NG.buf], "NG")
        tb31 = self.pk("tb31")
        fb = self.db("Fd")[0]
        tiny = 1e-30
        psS = self.psum[0:4]
        psO, psL, psG, psI = self.psum[4], self.psum[5], self.psum[6], self.psum[7]
        psTb = psI.ap.bitcast(BF16)
        PT = [self.tile("PT", [512], BF16) for _ in range(6)]
        Rt = self.tile("Rt", [512], F32)
        T1 = self.tile("T1", [512], F32)
        T2 = [self.tile("T2", [512], F32) for _ in range(2)]
        YO = [self.tile("YO", [512], BF16) for _ in range(2)]
        cnt = {"s": 0, "pt": 0, "t2": 0, "bt": 0, "yo": 0}

        def sbank():
            cnt["s"] += 1
            return psS[cnt["s"] % 3]

        psI2 = [self.psum[3], self.psum[7]]

        def ptile():
            cnt["pt"] += 1
            return PT[cnt["pt"] % 6]

        def epilogue(gi, pair, b, qs, YBt, first, guard=False):
            if guard:
                p.dve(lambda e: e.tensor_scalar_max(out=Rt.ap, in0=psL.ap, scalar1=tiny), [psL.buf], [Rt.buf])
                p.act(lambda e: e.activation(out=Rt.ap, in_=Rt.ap, func=AF.Ln), [Rt.buf], [Rt.buf])
                p.act(lambda e: e.activation(out=Rt.ap, in_=Rt.ap, func=AF.Exp, scale=-1.0), [Rt.buf], [Rt.buf])
            else:
                p.dve(lambda e: e.reciprocal(out=Rt.ap, in_=psL.ap), [psL.buf], [Rt.buf])
            p.dve(lambda e: e.tensor_tensor(out=T1.ap, in0=psO.ap, in1=Rt.ap, op=ALU.mult), [psO.buf, Rt.buf],
                  [T1.buf])
            c0 = ((gi * 2 + pair) * 3 + b) * 128
            self.mm(psG, psG.ap, GS.ap[0:24, c0:c0 + 128], NG.ap[0:24, qs], True, True, [GS.buf, NG.buf])
            if first:
                p.dve(lambda e: e.tensor_tensor(out=YBt.ap[:, qs], in0=T1.ap, in1=psG.ap, op=ALU.mult),
                      [T1.buf, psG.buf], [YBt.buf])
            else:
                cnt["t2"] += 1
                t2 = T2[cnt["t2"] % 2]
                p.dve(lambda e: e.tensor_tensor(out=t2.ap, in0=T1.ap, in1=psG.ap, op=ALU.mult),
                      [T1.buf, psG.buf], [t2.buf])
                p.pool(lambda e: e.tensor_tensor(out=YBt.ap[:, qs], in0=YBt.ap[:, qs], in1=t2.ap, op=ALU.add),
                       [t2.buf, YBt.buf], [YBt.buf])

        for g in range(2):
            base = self.aoff
            KS = [self.tile("KS", [S], BF16) for _ in range(2)]
            KW = [self.tile("KW", [S], BF16) for _ in range(2)]
            for hh in range(2):
                rs_ = slice(hh * 64, hh * 64 + 64)
                for kk, src, nm in ((KS, self.kselT, "kselT"), (KW, self.kwinT, "kwinT")):
                    p.pool(lambda e, t=kk[hh]: e.memset(t.ap, 0.0), [], [kk[hh].buf])
                    p.dma("sp" if hh == 0 else "act", kk[hh].ap[rs_], src[g * 128 + hh * 64:g * 128 + hh * 64 + 64, :],
                          [self.db(nm)[0]], [kk[hh].buf], "K%s%d" % (nm[1], hh))
            VS = self.tile("VS", [32, 192], BF16)
            VW = self.tile("VW", [32, 192], BF16)
            for i, vt in enumerate((VS, VW)):
                p.pool(lambda e, vt=vt: e.memset(vt.ap, 0.0), [], [vt.buf])
                c0 = i * 128 + g * 64
                p.dma("sp" if i == 0 else "act", vt.ap[:, :, 64:128],
                      self.vsw[:, c0:c0 + 64].rearrange("(kt p) d -> p kt d", p=128), [self.db("vsw")[0]],
                      [vt.buf], "VSW%d" % i)
            KC = [self.tile("KC", [256], BF16) for _ in range(2)]
            for hh in range(2):
                rs_ = slice(hh * 64, hh * 64 + 64)
                p.pool(lambda e, t=KC[hh]: e.memset(t.ap, 0.0), [], [KC[hh].buf])
                p.dma("sp", KC[hh].ap[rs_], self.kcn[g, hh * 64:hh * 64 + 64, :], [self.db("kcn")[0]], [KC[hh].buf],
                      "KC%d" % hh)
            VC = self.tile("VC", [2, 192], BF16)
            p.dma("act", VC.ap, self.vcz[g].rearrange("ct p c -> p ct c"), [self.db("vcz")[0]], [VC.buf], "VC")
            SELB = self.tile("SELB", [S], BF16)
            p.pool(lambda e, SELB=SELB: e.memset(SELB.ap, 0.0), [], [SELB.buf])
            YB = [self.tile("YB", [S], F32) for _ in range(2)]
            QP = [self.tile("QP", [S], BF16) for _ in range(2)]
            for pair in range(2):
                r0 = (g * 2 + pair) * 128
                p.dma("sp" if pair == 0 else "act", QP[pair].ap, self.nqT[r0:r0 + 128, :], [self.db("nqT")[0]],
                      [QP[pair].buf], "QP%d" % pair)
            PC = [[[self.tile("PC", [512], BF16) for _ in range(2)] for _ in range(2)] for _ in range(2)]
            BT = [self.tile("BT", [512], BF16) for _ in range(4)]
            LS = self.tile("LS", [4], F32)
            IMP = self.tile("IMP", [64], F32)
            M8 = self.tile("M8", [8], F32)
            SB = self.tile("SB", [64], BF16)
            stages = getattr(self, "nsa_stages", "ciws")
            for qc in range(8 if "c" in stages else 0):
                q0 = qc * 512
                qs = slice(q0, q0 + 512)
                ncts = 1 if qc < 4 else 2
                for pair in range(2):
                    for hh in range(2):
                        rows = slice(hh * 64, hh * 64 + 64)
                        head = g * 4 + pair * 2 + hh
                        for ct in range(ncts):
                            n = 128 if ct == 0 else 127
                            bt = BT[cnt["bt"] % 4]
                            q = "sp" if cnt["bt"] % 2 == 0 else "act"
                            off = (4 + head) * LF + 2033 + q0 - 16 * ct * 128
                            p.dma(q, bt.ap, bass.AP(self.Fd.tensor, off, [[16, 128], [1, 512]]), [fb], [bt.buf],
                                  "BT%d" % (cnt["bt"] % 4))
                            cnt["bt"] += 1
                            ps = sbank()
                            self.mm(ps, ps.ap[0:n, :], KC[hh].ap[:, ct * 128:ct * 128 + n], QP[pair].ap[:, qs],
                                    True, False, [KC[hh].buf, QP[pair].buf])
                            self.mm(ps, ps.ap[0:n, :], self.J.ap[:, 0:n], bt.ap, False, True, [self.J.buf, bt.buf])
                            pc = PC[pair][hh][ct]
                            p.act(lambda e, pc=pc, ps=ps, n=n: e.activation(out=pc.ap[0:n], in_=ps.ap[0:n, :],
                                                                             func=AF.Exp), [ps.buf], [pc.buf])
                    first = True
                    for hh in range(2):
                        for ct in range(ncts):
                            n = 128 if ct == 0 else 127
                            last = (hh == 1 and ct == ncts - 1)
                            pc = PC[pair][hh][ct]
                            cs = slice(64, 192) if hh == 0 else slice(0, 128)
                            self.mm(psO, psO.ap, VC.ap[0:n, ct, cs], pc.ap[0:n], first, last, [VC.buf, pc.buf])
                            self.mm(psL, psL.ap, self.OZ.ap[0:n, cs], pc.ap[0:n], first, last,
                                    [self.OZ.buf, pc.buf])
                            first = False
                    epilogue(g, pair, 0, qs, YB[pair], True, guard=True)
                for qsub in range(4 if "i" in stages else 0):
                    qt = qc * 4 + qsub
                    psI = psI2[qt % 2]
                    psTb = psI.ap.bitcast(BF16)
                    for h4 in range(4):
                        pair, hh = h4 // 2, h4 % 2
                        for ct in range(ncts):
                            n = 128 if ct == 0 else 127
                            pc = PC[pair][hh][ct]
                            self.mm(psI, psI.ap[:, h4 * 65:(h4 + 1) * 65], pc.ap[0:n, qsub * 128:(qsub + 1) * 128],
                                    OV.ap[0:n, ct, :], ct == 0, ct == ncts - 1, [pc.buf, OV.buf])
                    p.dve(lambda e, psI=psI: e.tensor_scalar_max(out=LS.ap, in0=psI.ap[:, 64:260:65], scalar1=tiny),
                          [psI.buf], [LS.buf])
                    p.dve(lambda e: e.reciprocal(out=LS.ap, in_=LS.ap), [LS.buf], [LS.buf])
                    p.dve(lambda e, psI=psI: e.tensor_scalar(out=IMP.ap, in0=psI.ap[:, 0:64], scalar1=LS.ap[:, 0:1],
                                                    scalar2=None, op0=ALU.mult), [psI.buf, LS.buf], [IMP.buf])
                    for h4 in range(1, 4):
                        p.dve(lambda e, h4=h4, psI=psI: e.scalar_tensor_tensor(
                            out=IMP.ap, in0=psI.ap[:, h4 * 65:h4 * 65 + 64], scalar=LS.ap[:, h4:h4 + 1], in1=IMP.ap,
                            op0=ALU.mult, op1=ALU.add), [psI.buf, LS.buf, IMP.buf], [IMP.buf])
                    p.dve(lambda e, qt=qt: e.tensor_tensor(out=IMP.ap, in0=IMP.ap, in1=CA.ap[:, qt, :], op=ALU.add),
                          [IMP.buf, CA.buf], [IMP.buf])
                    p.dve(lambda e: e.max(out=M8.ap, in_=IMP.ap), [IMP.buf], [M8.buf])
                    p.dve(lambda e: e.tensor_scalar(out=SB.ap, in0=IMP.ap, scalar1=M8.ap[:, 7:8], scalar2=NEG,
                                                    op0=ALU.is_lt, op1=ALU.mult), [IMP.buf, M8.buf], [SB.buf])
                    p.pe(lambda e, psTb=psTb: e.transpose(out=psTb[0:64, 640:768], in_=SB.ap, identity=self.identb.ap),
                         [SB.buf, self.identb.buf], [psI.buf])
                    p.act(lambda e, qt=qt, psTb=psTb: e.activation(out=SELB.ap[0:64, qt * 128:(qt + 1) * 128],
                                                        in_=psTb[0:64, 640:768], func=AF.Copy),
                          [psI.buf, SELB.buf], [SELB.buf])
            psS4 = [self.psum[0], self.psum[1], self.psum[2], self.psum[7]]
            psO2 = [self.psum[3], self.psum[4]]
            psL2 = [self.psum[5], self.psum[6]]
            noE = getattr(self, "nsa_noE", False)
            for pair in range(2):
                base2 = self.aoff
                STc = [self.tile("STc", [1792], BF16) for _ in range(2)]
                STw = [self.tile("STw", [1408], BF16) for _ in range(2)]
                STR = self.tile("STR", [1792], BF16)
                PT0 = [self.tile("PT0", [512], BF16) for _ in range(4)]
                for hh in range(2):
                    head = g * 4 + pair * 2 + hh
                    for (dst, off, wid, ch) in ((STc[hh], 3585, 1792, 448), (STw[hh], 8193, 1408, 352)):
                        p.dma("sp", STR.ap[:, 0:wid], bass.AP(self.Fd.tensor, (4 + head) * LF + off, [[1, 128], [1, wid]]),
                              [fb], [STR.buf], "STRn")
                        for c4 in range(4):
                            cnt["s"] += 1
                            ps = psS4[cnt["s"] % 4]
                            cs_ = slice(c4 * ch, (c4 + 1) * ch)
                            self.mm(ps, ps.ap[:, 0:ch], self.J.ap, STR.ap[:, cs_], True, True, [self.J.buf, STR.buf])
                            p.act(lambda e, ps=ps, cs_=cs_, dst=dst, ch=ch: e.activation(
                                out=dst.ap[:, cs_], in_=ps.ap[:, 0:ch], func=AF.Exp), [ps.buf], [dst.buf])
                qp = QP[pair]
                YBt = YB[pair]
                jobs = []
                ngrp = 0
                for qc in range(getattr(self, "nsa_maxqc", 8)):
                    for br in (1, 2):
                        if (br == 1 and "s" not in stages) or (br == 2 and "w" not in stages):
                            continue
                        if br == 1:
                            kts = list(range(4 * qc + 4))
                        else:
                            kts = [kt for kt in range(4 * qc - 4, 4 * qc + 4) if kt >= 0]
                        for ki, kt in enumerate(kts):
                            for hh in range(2):
                                jobs.append((qc, br, kt, hh, ki == 0 and hh == 0, ki == len(kts) - 1 and hh == 1,
                                             ngrp % 2))
                        ngrp += 1
                tl = {}

                def front(j):
                    qc, br, kt, hh, fst, lst, gp = jobs[j]
                    q0 = qc * 512
                    qs = slice(q0, q0 + 512)
                    KK, STs = (KS, STc) if br == 1 else (KW, STw)
                    delta = q0 - kt * 128
                    near = (delta <= 896) or br == 2
                    useE = (br == 1 and not noE)
                    rows = slice(hh * 64, hh * 64 + 64)
                    head = g * 4 + pair * 2 + hh
                    cnt["s"] += 1
                    ps = psS4[cnt["s"] % 4]
                    self.mm(ps, ps.ap, KK[hh].ap[:, kt * 128:(kt + 1) * 128], qp.ap[:, qs], True, not useE,
                            [KK[hh].buf, qp.buf])
                    if useE:
                        self.mm(ps, ps.ap, EB.ap[:, kt * 128:(kt + 1) * 128], SELB.ap[:, qs], False, True,
                                [EB.buf, SELB.buf])
                    pt = ptile()
                    if near:
                        cnt["p0"] = cnt.get("p0", 0) + 1
                        p0 = PT0[cnt["p0"] % 4]
                        p.act(lambda e: e.activation(out=p0.ap, in_=ps.ap, func=AF.Exp), [ps.buf], [p0.buf])
                        p.dve(lambda e: e.tensor_tensor(out=pt.ap, in0=p0.ap, in1=STs[hh].ap[:, delta + 384:delta + 896],
                                                        op=ALU.mult), [p0.buf, STs[hh].buf], [pt.buf])
                    else:
                        p.act(lambda e: e.activation(out=pt.ap, in_=ps.ap, func=AF.Exp,
                                                     bias=tb31[:, 4 + head:5 + head], scale=1.0),
                              [ps.buf, self.pkt.buf], [pt.buf])
                    tl[j] = pt

                def back(j):
                    qc, br, kt, hh, fst, lst, gp = jobs[j]
                    qs = slice(qc * 512, qc * 512 + 512)
                    VVt = VS if br == 1 else VW
                    pt = tl.pop(j)
                    ybt = YBt
                    po, pl = psO2[gp], psL2[gp]
                    cs = slice(64, 192) if hh == 0 else slice(0, 128)
                    self.mm(po, po.ap, VVt.ap[:, kt, cs], pt.ap, fst, lst, [VVt.buf, pt.buf])
                    self.mm(pl, pl.ap, self.OZ.ap[:, cs], pt.ap, fst, lst, [self.OZ.buf, pt.buf])
                    if not lst:
                        return
                    c0 = ((g * 2 + pair) * 3 + br) * 128
                    cnt["s"] += 1
                    psG2 = psS4[cnt["s"] % 4]
                    self.mm(psG2, psG2.ap, GS.ap[0:24, c0:c0 + 128], NG.ap[0:24, qs], True, True, [GS.buf, NG.buf])
                    p.act(lambda e: e.activation(out=Rt.ap, in_=pl.ap, func=AF.Ln), [pl.buf], [Rt.buf])
                    p.act(lambda e: e.activation(out=Rt.ap, in_=Rt.ap, func=AF.Exp, scale=-1.0), [Rt.buf], [Rt.buf])
                    p.dve(lambda e: e.tensor_tensor(out=T1.ap, in0=po.ap, in1=Rt.ap, op=ALU.mult), [po.buf, Rt.buf],
                          [T1.buf])
                    cnt["t2"] += 1
                    t2 = T2[cnt["t2"] % 2]
                    p.dve(lambda e: e.tensor_tensor(out=t2.ap, in0=T1.ap, in1=psG2.ap, op=ALU.mult),
                          [T1.buf, psG2.buf], [t2.buf])
                    p.pool(lambda e: e.tensor_tensor(out=ybt.ap[:, qs], in0=ybt.ap[:, qs], in1=t2.ap, op=ALU.add),
                           [t2.buf, ybt.buf], [ybt.buf])
                    if br == 2 or "w" not in stages:
                        yo = YO[cnt["yo"] % 2]
                        p.pool(lambda e: e.tensor_copy(out=yo.ap, in_=ybt.ap[:, qs]), [ybt.buf], [yo.buf])
                        r0 = (g * 2 + pair) * 128
                        p.dma("sp", self.ybT[r0:r0 + 128, qs], yo.ap, [yo.buf], [], "YBO%d" % (cnt["yo"] % 2),
                              pwrites=[self.db("ybT")[0]])
                        cnt["yo"] += 1

                LA = 3
                n = len(jobs)
                for i in range(n + LA):
                    if i < n:
                        front(i)
                    if i >= LA:
                        back(i - LA)
                self.phase_end()
                self.aoff = base2
            self.aoff = base
        self.aoff = self.abase

    def merge_phase(self, l, xin, xout):
        p = self.p
        TT, NCH, KC = 512, 8, 8
        wv = self.w_bf["w_in"][l].rearrange("(kc p) c -> p kc c", p=128)
        wb = self.db("w_in_bf", DEPTH)[l]
        WMG = self.tile("WMG", [KC, 2048], BF16)
        for i in range(4):
            q = "sp" if i % 2 == 0 else "act"
            p.dma(q, WMG.ap[:, :, i * 512:(i + 1) * 512], wv[:, :, 2840 + i * 512:2840 + (i + 1) * 512], [wb], [],
                  "WMG_" + q, pwrites=[WMG.buf])
        WA = self.tile("WA", [4, D], BF16)
        WB = self.tile("WB", [4, D], BF16)
        WO = self.tile("WO", [KC, D], BF16)
        p.dma("sp", WA.ap, self.w_bf["w_branch_a"][l].rearrange("(kc p) c -> p kc c", p=128),
              [self.db("w_branch_a_bf", DEPTH)[l]], [WA.buf], "WA")
        p.dma("act", WB.ap, self.w_bf["w_branch_b"][l].rearrange("(kc p) c -> p kc c", p=128),
              [self.db("w_branch_b_bf", DEPTH)[l]], [WB.buf], "WB")
        p.dma("sp", WO.ap, self.w_bf["w_out"][l].rearrange("(kc p) c -> p kc c", p=128),
              [self.db("w_out_bf", DEPTH)[l]], [WO.buf], "WO")
        xin_b = self.db(xin.name, NCH)
        xout_b = self.db(xout.name, NCH)
        xin_v = xin.rearrange("(kc p) t -> p kc t", p=128)
        xout_v = xout.rearrange("(kc p) t -> p kc t", p=128)
        hT_v = self.hT.rearrange("(kc p) t -> p kc t", p=128)
        ya_v = self.yaT.rearrange("(kc p) t -> p kc t", p=128)
        yb_v = self.ybT.rearrange("(kc p) t -> p kc t", p=128)
        X = [self.tile("X", [KC, TT], F32) for _ in range(2)]
        H = [self.tile("H", [KC, TT], BF16) for _ in range(2)]
        YA = [self.tile("YA", [4, TT], BF16) for _ in range(2)]
        YBt = [self.tile("YBt", [4, TT], BF16) for _ in range(2)]
        MER = self.tile("MER", [KC, TT], BF16)
        SG = [self.tile("SG", [TT], F32) for _ in range(2)]
        TA = self.tile("TA", [TT], F32)
        TB = self.tile("TB", [TT], F32)
        psA, psB, psMA, psMB = self.psum[0], self.psum[1], self.psum[2], self.psum[3]
        psY = [self.psum[4], self.psum[5]]
        hb, yab, ybb = self.db("hT", NCH), self.db("yaT")[0], self.db("ybT")[0]

        def load(c):
            i = c % 2
            ts = slice(c * TT, (c + 1) * TT)
            p.dma("sp", X[i].ap, xin_v[:, :, ts], [xin_b[c]], [X[i].buf], "X%d" % i)
            p.dma("act", H[i].ap, hT_v[:, :, ts], [hb[c]], [H[i].buf], "H%d" % i)
            p.dma("sp", YA[i].ap, ya_v[:, :, ts], [yab], [YA[i].buf], "YA%d" % i)
            p.dma("act", YBt[i].ap, yb_v[:, :, ts], [ybb], [YBt[i].buf], "YB%d" % i)

        load(0)
        for c in range(NCH):
            i = c % 2
            x, h, ya, yb = X[i], H[i], YA[i], YBt[i]
            if c + 1 < NCH:
                load(c + 1)
            for dc in range(KC):
                ds_ = slice(dc * 128, (dc + 1) * 128)
                for kc in range(KC):
                    self.mm(psMA, psMA.ap, WMG.ap[:, kc, dc * 128:(dc + 1) * 128], h.ap[:, kc, :], kc == 0,
                            kc == KC - 1, [WMG.buf, h.buf])
                p.act(lambda e: e.activation(out=SG[0].ap, in_=psMA.ap, func=AF.Sigmoid), [psMA.buf], [SG[0].buf])
                for kc in range(KC):
                    self.mm(psMB, psMB.ap, WMG.ap[:, kc, 1024 + dc * 128:1024 + (dc + 1) * 128], h.ap[:, kc, :],
                            kc == 0, kc == KC - 1, [WMG.buf, h.buf])
                p.act(lambda e: e.activation(out=SG[1].ap, in_=psMB.ap, func=AF.Sigmoid), [psMB.buf], [SG[1].buf])
                for k in range(4):
                    self.mm(psA, psA.ap, WA.ap[:, k, ds_], ya.ap[:, k, :], k == 0, k == 3, [WA.buf, ya.buf])
                for k in range(4):
                    self.mm(psB, psB.ap, WB.ap[:, k, ds_], yb.ap[:, k, :], k == 0, k == 3, [WB.buf, yb.buf])
                p.dve(lambda e: e.tensor_tensor(out=TA.ap, in0=SG[0].ap, in1=psA.ap, op=ALU.mult),
                      [SG[0].buf, psA.buf], [TA.buf])
                p.dve(lambda e: e.tensor_tensor(out=TB.ap, in0=SG[1].ap, in1=psB.ap, op=ALU.mult),
                      [SG[1].buf, psB.buf], [TB.buf])
                p.pool(lambda e, dc=dc: e.tensor_tensor(out=MER.ap[:, dc, :], in0=TA.ap, in1=TB.ap, op=ALU.add),
                       [TA.buf, TB.buf], [MER.buf])
            for dc in range(KC):
                py = psY[dc % 2]
                for kc in range(KC):
                    self.mm(py, py.ap, WO.ap[:, kc, dc * 128:(dc + 1) * 128], MER.ap[:, kc, :], kc == 0, kc == KC - 1,
                            [WO.buf, MER.buf])
                p.dve(lambda e, py=py, dc=dc, x=x: e.tensor_tensor(out=x.ap[:, dc, :], in0=py.ap, in1=x.ap[:, dc, :],
                                                                   op=ALU.add), [py.buf, x.buf], [x.buf])
            p.dma("sp", xout_v[:, :, c * TT:(c + 1) * TT], x.ap, [x.buf], [xout_b[c]], "XO%d" % i)
        self.phase_end()

C_ID, C_J, C_BO, C_OZ, C_OV = 0, 128, 256, 384, 576
NCST = 576 + 130
LF = 8192 + 2048


WSHAPES = {
    "w_in": (D, NIN),
    "w_branch_a": (512, D),
    "w_branch_b": (512, D),
    "w_out": (D, D),
    "ffn1_w1": (D, DFF),
    "ffn1_w3": (D, DFF),
    "ffn1_w2": (DFF, D),
    "ffn2_w1": (D, DFF),
    "ffn2_w3": (D, DFF),
    "ffn2_w2": (DFF, D),
}

for _l in range(DEPTH):
    for _n in ("ffn1", "mix", "ffn2"):
        Builder.pk_add("g_%s_l%d" % (_n, _l), 8)
    for _n in ("gdq", "gdk", "gnq", "gks", "gkw", "gkc", "gsub"):
        Builder.pk_add("%s_l%d" % (_n, _l), 1)
    Builder.pk_add("lam_l%d" % _l, 256)
    Builder.pk_add("pek_l%d" % _l, 32)
    Builder.pk_add("pev_l%d" % _l, 32)
Builder.pk_add("tb31", 12)

BUCKET_STARTS = [0, 1, 2, 3, 4, 5, 6, 7, 8, 9, 10, 11, 12, 13, 14, 15, 16, 21, 27, 35, 46, 59, 77, 99, 128, 166,
                 216, 280, 363, 470, 609, 790]


def pack_params(inp):
    f = lambda a: np.asarray(a, np.float32)
    cols = []
    bc = lambda v: np.broadcast_to(f(v).reshape(1, -1), (128, f(v).size))
    for l in range(DEPTH):
        for n in ("norm_ffn1", "norm_mix", "norm_ffn2"):
            cols.append(f(inp[n][l]).reshape(8, 128).T)
        t2 = lambda v: np.tile(f(v), 2).reshape(128, 1)
        cols.append(t2(inp["da_q_gain"][l]))
        cols.append(t2(inp["da_k_gain"][l]))
        cols.append(t2(inp["nsa_q_gain"][l]))
        cols.append(t2(inp["nsa_k_gain"][l, 1]))
        cols.append(t2(inp["nsa_k_gain"][l, 2]))
        cols.append(t2(inp["nsa_k_gain"][l, 0]))
        cols.append(f(inp["da_subln_gain"][l]).reshape(128, 1))
        for n in ("da_lambda_q1", "da_lambda_k1", "da_lambda_q2", "da_lambda_k2"):
            cols.append(bc(inp[n][l]))
        for n in ("cmp_pe_k", "cmp_pe_v"):
            cols.append(np.tile(f(inp[n][l]).T, (2, 1)))
    cols.append(bc(inp["rel_bias_table"][31]))
    pk = np.concatenate(cols, axis=1)
    assert pk.shape == (128, Builder.NPK), pk.shape
    return np.ascontiguousarray(pk)


def make_consts(inp):
    cst = np.zeros((128, NCST), np.float32)
    idx = np.arange(128)
    cst[idx, C_ID + idx] = 1.0
    cst[idx, C_J + 127 - idx] = 1.0
    cst[:64, C_BO:C_BO + 64] = 1.0
    cst[64:, C_BO + 64:C_BO + 128] = 1.0
    cst[:, C_OZ + 64:C_OZ + 128] = 1.0
    for ct in range(2):
        for pp in range(128):
            c = ct * 128 + pp
            if c >= 255:
                continue
            for j in range(64):
                if 16 * c < 64 * (j + 1) and 16 * c + 31 >= 64 * j:
                    cst[pp, C_OV + ct * 65 + j] = 1.0
            cst[pp, C_OV + ct * 65 + 64] = 1.0
    bs = np.asarray(BUCKET_STARTS)
    oh = np.zeros((33, LF), np.float32)
    i = np.arange(8192)
    dl = i - 4096
    bk = np.searchsorted(bs, np.maximum(dl, 0), side="right") - 1
    oh[np.where(dl < 0, 32, bk), i] = 1.0
    i = np.arange(2048)
    dl = i - 512
    ok = (dl >= 0) & (dl < 512)
    bk = np.searchsorted(bs, np.clip(dl, 0, None), side="right") - 1
    oh[np.where(ok, bk, 32), 8192 + i] = 1.0
    tblx = np.concatenate([np.asarray(inp["rel_bias_table"], np.float32), np.full((1, 12), NEG, np.float32)], 0)
    ebig = np.zeros((128, S), np.float32)
    ebig[np.arange(S) // 64, np.arange(S)] = 1.0
    gs = np.zeros((24, 1536), np.float32)
    for j in range(4):
        for b in range(3):
            for m in range(128):
                gs[(2 * j + m // 64) * 3 + b, (j * 3 + b) * 128 + m] = 1.0
    q = np.arange(S)
    cur = (q // 64)[:, None]
    jj = np.arange(64)[None, :]
    cadd = np.zeros((S, 64), np.float32)
    cadd[(jj == 0) | (jj == cur) | (jj == cur - 1)] = 1e4
    cadd[jj > cur] = -1e4
    return {"cst": cst, "oh": oh, "tblx": np.ascontiguousarray(tblx), "ebig": ebig, "gs": gs, "cadd": cadd}


_CACHE = {}


def build_full():
    b = Builder()
    b.prologue(list(WSHAPES))
    b.scratch()
    xT = [b.dt("xT%d" % i, [D, S], F32) for i in range(3)]
    out = b.dt("out", [S, D], F32, kind="ExternalOutput")
    b.transpose_in(xT[0])
    for l in range(DEPTH):
        b.ffn_phase(l, 1, xT[0], xT[1])
        b.proj_phase(l, xT[1])
        b.cmp_phase(l)
        b.da_phase(l)
        b.nsa_phase(l)
        b.merge_phase(l, xT[1], xT[2])
        b.ffn_phase(l, 2, xT[2], xT[0])
    b.transpose_out(xT[0], out)
    b.p.emit()
    return b


def kernel(**inputs):
    inp = {k: np.asarray(v) for k, v in inputs.items()}
    if "b" not in _CACHE:
        _CACHE["b"] = build_full()
    b = _CACHE["b"]
    shared = {"pk": pack_params(inp)}
    shared.update(make_consts(inp))
    for n in WSHAPES:
        shared[n] = np.ascontiguousarray(inp[n], dtype=np.float32)
    for n in ("cmp_w1_k", "cmp_w2_k", "cmp_w1_v", "cmp_w2_v"):
        shared[n] = np.ascontiguousarray(inp[n], dtype=np.float32)
    x = np.asarray(inp["x"], np.float32)
    maps = []
    for c in range(NCORES):
        m = dict(shared)
        m["x"] = np.ascontiguousarray(x[c])
        maps.append(m)
    res = run_bass_kernel_spmd(b.nc, maps, core_ids=list(range(NCORES)))
    return np.stack([np.asarray(res.results[c]["out"], np.float32) for c in range(NCORES)], 0)
```

```python
import numpy as np
from contextlib import ExitStack
import concourse.bass as bass
import concourse.mybir as mybir
from concourse.bass_utils import run_bass_kernel_spmd

F32 = mybir.dt.float32
BF16 = mybir.dt.bfloat16
AF = mybir.ActivationFunctionType
ALU = mybir.AluOpType
AX = mybir.AxisListType

S = 4096
D = 1024
DFF = 2816
NIN = 4888
DEPTH = 2
NCORES = 8
EPS = 1e-6
NEG = -30000.0
STRICT = True


class Buf:
    __slots__ = ("name", "writers", "readers")

    def __init__(self, name):
        self.name = name
        self.writers = []
        self.readers = []


class Op:
    __slots__ = ("eng", "fn", "deps", "needed", "semval", "stream", "dma_waits", "sidx")

    def __init__(self, eng, fn, stream=None):
        self.eng = eng
        self.fn = fn
        self.deps = []
        self.needed = False
        self.semval = None
        self.stream = stream
        self.dma_waits = {}
        self.sidx = None


class Stream:
    __slots__ = ("name", "queue", "count", "waiters", "sem")

    def __init__(self, name, queue):
        self.name = name
        self.queue = queue
        self.count = 0
        self.waiters = []
        self.sem = None


ENGS = ("pe", "act", "dve", "pool", "sp")


class Prog:
    def __init__(self, nc, es):
        self.nc = nc
        self.es = es
        self.ops = {e: [] for e in ENGS}
        self.streams = {}
        self.nops = 0

    def sb(self, name, shape, dtype):
        t = self.es.enter_context(self.nc.sbuf_tensor(name, list(shape), dtype))
        return t

    def ps(self, name, shape=(128, 512), dtype=F32):
        t = self.es.enter_context(self.nc.psum_tensor(name, list(shape), dtype))
        return t

    def _dep(self, op, other, raw=False):
        if other is None or other is op:
            return
        if other.stream is not None:
            st = other.stream
            op.dma_waits[st.name] = st.count
            if op not in st.waiters:
                st.waiters.append(op)
            return
        if other.eng == op.eng and op.stream is None and (op.eng == "pe" or not (raw or STRICT)):
            return
        other.needed = True
        op.deps.append(other)

    def op(self, eng, fn, reads=(), writes=(), stream=None, pwrites=()):
        st = None
        if stream is not None:
            st = self.streams.get(stream)
            if st is None:
                st = Stream(stream, eng)
                self.streams[stream] = st
            assert st.queue == eng, (stream, st.queue, eng)
        o = Op(eng, fn, st)
        for b in reads:
            for w in b.writers:
                self._dep(o, w, raw=True)
        for b in writes:
            for w in b.writers:
                self._dep(o, w)
            for r in b.readers:
                self._dep(o, r)
        for b in pwrites:
            for r in b.readers:
                self._dep(o, r)
        if st is not None:
            for w in st.waiters:
                if w.eng == eng:
                    continue
                self._dep(o, w)
            st.waiters = []
            st.count += 1
            o.sidx = st.count
        for b in reads:
            b.readers.append(o)
        for b in writes:
            b.writers = [o]
            b.readers = []
        for b in pwrites:
            if b.readers:
                b.writers = []
                b.readers = []
            b.writers.append(o)
        self.ops[eng].append(o)
        self.nops += 1
        return o

    def pe(self, fn, reads=(), writes=()):
        return self.op("pe", fn, reads, writes)

    def act(self, fn, reads=(), writes=()):
        return self.op("act", fn, reads, writes)

    def dve(self, fn, reads=(), writes=()):
        return self.op("dve", fn, reads, writes)

    def pool(self, fn, reads=(), writes=()):
        return self.op("pool", fn, reads, writes)

    def dma(self, queue, out, in_, reads, writes, stream, pwrites=()):
        return self.op(queue, lambda e: e.dma_start(out=out, in_=in_), reads, writes, stream, pwrites)

    def emit(self):
        nc = self.nc
        es = self.es
        esem = {e: es.enter_context(nc.semaphore("sem_" + e)) for e in ENGS}
        for st in self.streams.values():
            st.sem = es.enter_context(nc.semaphore("dq_" + st.name))
        for e in ENGS:
            c = 0
            for o in self.ops[e]:
                if o.stream is None and o.needed:
                    c += 1
                    o.semval = c
        streams = self.streams
        ops = self.ops

        def run(e, eng):
            waited = {}
            for o in ops[e]:
                w = {}
                for d in o.deps:
                    s = esem[d.eng]
                    k = id(s)
                    if k not in w or w[k][1] < d.semval:
                        w[k] = (s, d.semval)
                for sn, cnt in o.dma_waits.items():
                    s = streams[sn].sem
                    k = id(s)
                    v = cnt * 16
                    if k not in w or w[k][1] < v:
                        w[k] = (s, v)
                for k, (s, v) in w.items():
                    if waited.get(k, 0) < v:
                        eng.wait_ge(s, v)
                        waited[k] = v
                ins = o.fn(eng)
                if o.stream is not None:
                    ins.then_inc(o.stream.sem, 16)
                elif o.needed:
                    ins.then_inc(esem[e], 1)
            if e == "sp":
                for st in streams.values():
                    if st.count:
                        eng.wait_ge(st.sem, st.count * 16)

        blk = es.enter_context(nc.Block())

        @blk.tensor
        def _(eng):
            run("pe", eng)

        @blk.scalar
        def _(eng):
            run("act", eng)

        @blk.vector
        def _(eng):
            run("dve", eng)

        @blk.gpsimd
        def _(eng):
            run("pool", eng)

        @blk.sync
        def _(eng):
            run("sp", eng)


class T:
    __slots__ = ("ap", "buf")

    def __init__(self, ap, name):
        self.ap = ap
        self.buf = Buf(name)


ARENA_WORDS = 51000


class Builder:
    def __init__(self, dbg_in=(), dbg_out=()):
        self.nc = bass.Bass("TRN2", target_bir_lowering=False)
        self.es = ExitStack()
        self.p = Prog(self.nc, self.es)
        self.dbg_in = set(dbg_in)
        self.dbg_out = set(dbg_out)
        self.dram = {}
        self.dbuf = {}
        self.arena = self.p.sb("arena", [128, ARENA_WORDS], F32)
        self.aoff = 0
        self.abase = 0
        self.psum = [T(self.p.ps("psum%d" % i)[:], "psum%d" % i) for i in range(8)]
        self.ntile = 0

    def dt(self, name, shape, dtype, kind=None):
        if kind is None:
            kind = "Internal"
            if name in self.dbg_in:
                kind = "ExternalInput"
            elif name in self.dbg_out:
                kind = "ExternalOutput"
        t = self.nc.dram_tensor(name, list(shape), dtype, kind=kind).ap()
        self.dram[name] = t
        return t

    def db(self, name, n=1):
        if name not in self.dbuf:
            self.dbuf[name] = [Buf("%s_%d" % (name, i)) for i in range(n)]
        return self.dbuf[name]

    def tile(self, name, free, dtype, parts=128):
        n = int(np.prod(free))
        words = n if dtype == F32 else (n + 1) // 2
        assert self.aoff + words <= ARENA_WORDS, (name, self.aoff, words)
        ap = self.arena[0:parts, self.aoff:self.aoff + words]
        if dtype != F32:
            ap = ap.bitcast(dtype)
            if n % 2:
                ap = ap[:, 0:n]
        self.aoff += words
        self.apeak = max(getattr(self, 'apeak', 0), self.aoff)
        if len(free) == 2:
            ap = ap.rearrange("p (a b) -> p a b", a=free[0])
        elif len(free) == 3:
            ap = ap.rearrange("p (a b c) -> p a b c", a=free[0], b=free[1])
        self.ntile += 1
        return T(ap, "%s_%d" % (name, self.ntile))

    def persist(self):
        self.abase = self.aoff

    def phase_end(self):
        p = self.p
        last = {}
        for e in ENGS:
            last[e] = None
            for o in reversed(p.ops[e]):
                if o.stream is None:
                    last[e] = o
                    break
        for e in ENGS:
            o = Op(e, lambda eng: eng.nop())
            for e2 in ENGS:
                l = last[e2]
                if e2 == e or l is None:
                    continue
                if l.stream is None:
                    l.needed = True
                    o.deps.append(l)
            for st in p.streams.values():
                if st.count and not st.name.startswith("cv"):
                    o.dma_waits[st.name] = st.count
                    st.waiters.append(o)
            p.ops[e].append(o)
        self.aoff = self.abase

    PK = {}
    NPK = 0

    @classmethod
    def pk_add(cls, name, width):
        cls.PK[name] = (cls.NPK, width)
        cls.NPK += width

    def pk(self, name):
        o, w = self.PK[name]
        return self.pkt.ap[:, o:o + w]

    def mm(self, pt, out, lhsT, rhs, start, stop, rb):
        self.p.pe(lambda e: e.matmul(out, lhsT=lhsT, rhs=rhs, start=start, stop=stop), rb, [pt.buf])

    def castload(self, name, free, src, parts=128):
        t = self.tile(name, free, BF16)
        self.p.dma("pool", t.ap[0:parts], src, [], [t.buf], "cl")
        return t

    def prologue(self, wnames, full=True):
        p = self.p
        nc = self.nc
        self.x_in = self.dt("x", [S, D], F32, kind="ExternalInput")
        self.pk_in = self.dt("pk", [128, self.NPK], F32, kind="ExternalInput")
        self.cst_in = self.dt("cst", [128, NCST], F32, kind="ExternalInput")
        self.w_in = {}
        self.w_bf = {}
        for name, shape in WSHAPES.items():
            if name not in wnames:
                continue
            self.w_in[name] = self.dt(name, [DEPTH] + list(shape), F32, kind="ExternalInput")
            self.w_bf[name] = self.dt(name + "_bf", [DEPTH] + list(shape), BF16)
        self.pkt = self.tile("pkt", [self.NPK], F32)
        p.dma("sp", self.pkt.ap, self.pk_in, [], [self.pkt.buf], "misc")
        self.ident = self.tile("ident", [128], F32)
        p.dma("sp", self.ident.ap, self.cst_in[:, C_ID:C_ID + 128], [], [self.ident.buf], "misc")
        self.identb = self.castload("identb", [128], self.cst_in[:, C_ID:C_ID + 128])
        self.J = self.castload("J", [128], self.cst_in[:, C_J:C_J + 128])
        self.BO = self.castload("BO", [128], self.cst_in[:, C_BO:C_BO + 128])
        self.OZ = self.castload("OZ", [192], self.cst_in[:, C_OZ:C_OZ + 192])
        self.ones = self.tile("ones", [128], BF16)
        p.pool(lambda e: e.memset(self.ones.ap, 1.0), [], [self.ones.buf])
        self.epsc = self.tile("epsc", [1], F32)
        p.pool(lambda e: e.memset(self.epsc.ap, EPS), [], [self.epsc.buf])
        self.eps64 = self.tile("eps64", [1], F32)
        p.pool(lambda e: e.memset(self.eps64.ap, 64 * EPS), [], [self.eps64.buf])
        self.persist()
        if not full:
            order = ["ffn1_w1", "ffn1_w3", "ffn1_w2", "w_in", "w_branch_a", "w_branch_b", "w_out",
                     "ffn2_w1", "ffn2_w3", "ffn2_w2"]
            for l in range(DEPTH):
                for name in order:
                    if name not in self.w_in:
                        continue
                    grp = "A" if name.startswith("ffn1") else ("C" if name.startswith("ffn2") else "B")
                    if name.endswith("_w3"):
                        continue
                    if name.endswith("_w1"):
                        for half, (c0, c1) in enumerate(((0, 1536), (1536, DFF))):
                            for nm in (name, name[:-1] + "3"):
                                src = self.w_in[nm][l][:, c0:c1].rearrange("(p r) c -> p r c", p=128)
                                dst = self.w_bf[nm][l][:, c0:c1].rearrange("(p r) c -> p r c", p=128)
                                b = self.db(nm + "_bf", 2 * DEPTH)[2 * l + half]
                                sfx = "ab"[half] if grp == "A" else ""
                                p.dma("pool", dst, src, [], [b], "cv%s%d%s" % (grp, l, sfx))
                        continue
                    src = self.w_in[name][l].rearrange("(p r) c -> p (r c)", p=128)
                    dst = self.w_bf[name][l].rearrange("(p r) c -> p (r c)", p=128)
                    b = self.db(name + "_bf", DEPTH)[l]
                    sfx = "c" if name == "ffn1_w2" else ""
                    p.dma("pool", dst, src, [], [b], "cv%s%d%s" % (grp, l, sfx))
            return
        self.oh_in = self.dt("oh", [33, LF], F32, kind="ExternalInput")
        self.tblx_in = self.dt("tblx", [33, 12], F32, kind="ExternalInput")
        self.Fd = self.dt("Fd", [12, LF], BF16)
        fb = self.db("Fd")[0]
        tb = self.tile("tblx", [12], F32)
        p.dma("sp", tb.ap[0:33], self.tblx_in, [], [tb.buf], "misc")
        OHt = self.tile("OHt", [LF], F32)
        FSt = self.tile("FSt", [LF], BF16)
        p.dma("sp", OHt.ap[0:33], self.oh_in, [], [OHt.buf], "OH0")
        for c in range(LF // 512):
            ps = self.psum[c % 4]
            cs_ = slice(c * 512, (c + 1) * 512)
            self.mm(ps, ps.ap[0:12, :], tb.ap[0:33, :], OHt.ap[0:33, cs_], True, True, [tb.buf, OHt.buf])
            if c % 2 == 0:
                p.act(lambda e, ps=ps, cs_=cs_: e.activation(out=FSt.ap[0:12, cs_], in_=ps.ap[0:12, :], func=AF.Copy),
                      [ps.buf], [FSt.buf])
            else:
                p.dve(lambda e, ps=ps, cs_=cs_: e.tensor_copy(out=FSt.ap[0:12, cs_], in_=ps.ap[0:12, :]),
                      [ps.buf], [FSt.buf])
        p.dma("sp", self.Fd, FSt.ap[0:12], [FSt.buf], [fb], "FS0")
        order = ["ffn1_w1", "ffn1_w3", "ffn1_w2", "w_in", "w_branch_a", "w_branch_b", "w_out",
                 "ffn2_w1", "ffn2_w3", "ffn2_w2"]
        for l in range(DEPTH):
            for name in order:
                if name not in self.w_in:
                    continue
                grp = "A" if name.startswith("ffn1") else ("C" if name.startswith("ffn2") else "B")
                if name.endswith("_w3"):
                    continue
                if name.endswith("_w1"):
                    for half, (c0, c1) in enumerate(((0, 1536), (1536, DFF))):
                        for nm in (name, name[:-1] + "3"):
                            src = self.w_in[nm][l][:, c0:c1].rearrange("(p r) c -> p r c", p=128)
                            dst = self.w_bf[nm][l][:, c0:c1].rearrange("(p r) c -> p r c", p=128)
                            b = self.db(nm + "_bf", 2 * DEPTH)[2 * l + half]
                            sfx = "ab"[half] if grp == "A" else ""
                            p.dma("pool", dst, src, [], [b], "cv%s%d%s" % (grp, l, sfx))
                    continue
                src = self.w_in[name][l].rearrange("(p r) c -> p (r c)", p=128)
                dst = self.w_bf[name][l].rearrange("(p r) c -> p (r c)", p=128)
                b = self.db(name + "_bf", DEPTH)[l]
                sfx = "c" if name == "ffn1_w2" else ""
                p.dma("pool", dst, src, [], [b], "cv%s%d%s" % (grp, l, sfx))
        self.phase_end()

    def transpose_in(self, xT):
        p = self.p
        xb = self.db(xT.name, 8)
        A = [self.tile("A", [4, D], F32) for _ in range(2)]
        O = [self.tile("O", [8, 512], F32) for _ in range(2)]
        xv = self.x_in.rearrange("(c r p) d -> c p r d", p=128, r=4)
        ov = xT.rearrange("(kc p) t -> p kc t", p=128)
        n = 0
        for c in range(8):
            a = A[c % 2]
            o = O[c % 2]
            p.dma("sp" if c % 2 == 0 else "act", a.ap, xv[c], [], [a.buf], "A%d" % (c % 2))
            for kc in range(8):
                ps = self.psum[n % 4]
                n += 1
                for r in range(4):
                    p.pe(lambda e, ps=ps, a=a, r=r, kc=kc: e.transpose(
                        out=ps.ap[:, r * 128:(r + 1) * 128], in_=a.ap[:, r, kc * 128:(kc + 1) * 128],
                        identity=self.ident.ap), [a.buf, self.ident.buf], [ps.buf])
                if kc % 2 == 0:
                    p.act(lambda e, ps=ps, o=o, kc=kc: e.activation(out=o.ap[:, kc, :], in_=ps.ap, func=AF.Copy),
                          [ps.buf], [o.buf])
                else:
                    p.dve(lambda e, ps=ps, o=o, kc=kc: e.tensor_copy(out=o.ap[:, kc, :], in_=ps.ap),
                          [ps.buf], [o.buf])
            p.dma("sp", ov[:, :, c * 512:(c + 1) * 512], o.ap, [o.buf], [xb[c]], "O%d" % (c % 2))
        self.phase_end()

    def transpose_out(self, xT, out):
        p = self.p
        xb = self.db(xT.name, 8)
        A = [self.tile("A", [8, 512], F32) for _ in range(2)]
        O = [self.tile("O", [4, D], F32) for _ in range(2)]
        xv = xT.rearrange("(kc p) t -> p kc t", p=128)
        ov = out.rearrange("(c r p) d -> c p r d", p=128, r=4)
        n = 0
        for c in range(8):
            a = A[c % 2]
            o = O[c % 2]
            p.dma("sp" if c % 2 == 0 else "act", a.ap, xv[:, :, c * 512:(c + 1) * 512], [xb[c]], [a.buf],
                  "A%d" % (c % 2))
            for r in range(4):
                for half in range(2):
                    ps = self.psum[n % 4]
                    n += 1
                    for k4 in range(4):
                        kc = half * 4 + k4
                        p.pe(lambda e, ps=ps, a=a, r=r, kc=kc, k4=k4: e.transpose(
                            out=ps.ap[:, k4 * 128:(k4 + 1) * 128], in_=a.ap[:, kc, r * 128:(r + 1) * 128],
                            identity=self.ident.ap), [a.buf, self.ident.buf], [ps.buf])
                    if half == 0:
                        p.act(lambda e, ps=ps, o=o, r=r: e.activation(out=o.ap[:, r, 0:512], in_=ps.ap,
                                                                       func=AF.Copy), [ps.buf], [o.buf])
                    else:
                        p.dve(lambda e, ps=ps, o=o, r=r: e.tensor_copy(out=o.ap[:, r, 512:1024], in_=ps.ap),
                              [ps.buf], [o.buf])
            p.dma("sp", ov[c], o.ap, [o.buf], [], "O%d" % (c % 2), pwrites=[self.db("out")[0]])
        self.phase_end()

    def ffn_phase(self, l, which, xin, xout):
        p = self.p
        TT = 512
        NCH = S // TT
        KC = D // 128
        FC = DFF // 128
        pre = "ffn%d_" % which
        w1d = self.w_bf[pre + "w1"][l]
        w3d = self.w_bf[pre + "w3"][l]
        w2d = self.w_bf[pre + "w2"][l]
        w1bs = self.db(pre + "w1_bf", 2 * DEPTH)[2 * l:2 * l + 2]
        w3bs = self.db(pre + "w3_bf", 2 * DEPTH)[2 * l:2 * l + 2]
        w2b = self.db(pre + "w2_bf", DEPTH)[l]
        xin_b = self.db(xin.name, NCH)
        xout_b = self.db(xout.name, NCH)
        gain = self.pk("g_ffn%d_l%d" % (which, l))
        xin_v = xin.rearrange("(kc p) t -> p kc t", p=128)
        xout_v = xout.rearrange("(kc p) t -> p kc t", p=128)

        X = [self.tile("X", [KC, TT], F32) for _ in range(2)]
        H = [self.tile("H", [KC, TT], BF16) for _ in range(2)]
        U = self.tile("U", [FC, TT], BF16)
        W2 = self.tile("W2", [FC, D], BF16)
        GW = 512
        groups = [(g0, min(GW, DFF - g0)) for g0 in range(0, DFF, GW)]
        W1 = [self.tile("W1", [KC, GW], BF16) for _ in range(2)]
        W3 = [self.tile("W3", [KC, GW], BF16) for _ in range(2)]
        SQ = [self.tile("SQ", [TT], BF16) for _ in range(2)]
        SA = [self.tile("SA", [TT], F32) for _ in range(2)]
        RS = self.tile("RS", [TT], F32)
        ps_st = self.psum[0]
        ps_a = [self.psum[1], self.psum[2]]
        ps_b = [self.psum[3], self.psum[4]]
        ps_y = [self.psum[5], self.psum[6]]

        w2v = w2d.rearrange("(fc p) d -> p fc d", p=128)
        for i, (a, b) in enumerate([(0, 6), (6, 12), (12, 17), (17, 22)]):
            q = "sp" if i % 2 == 0 else "act"
            p.dma(q, W2.ap[:, a:b, :], w2v[:, a:b, :], [w2b], [], "W2_" + q, pwrites=[W2.buf])

        def load_x(c):
            t = X[c % 2]
            p.dma("sp", t.ap, xin_v[:, :, c * TT:(c + 1) * TT], [xin_b[c]], [t.buf], "X%d" % (c % 2))

        def norm(c):
            x = X[c % 2]
            h = H[c % 2]
            for kc in range(KC):
                sq = SQ[kc % 2]
                p.act(lambda e, sq=sq, kc=kc: e.activation(out=sq.ap, in_=x.ap[:, kc, :], func=AF.Square),
                      [x.buf], [sq.buf])
                p.pe(lambda e, sq=sq, kc=kc: e.matmul(ps_st.ap, lhsT=self.ones.ap, rhs=sq.ap,
                                                       start=(kc == 0), stop=(kc == KC - 1)),
                     [sq.buf, self.ones.buf], [ps_st.buf])
            p.act(lambda e: e.activation(out=RS.ap, in_=ps_st.ap, func=AF.Ln, bias=self.epsc.ap,
                                         scale=1.0 / D), [ps_st.buf, self.epsc.buf], [RS.buf])
            p.act(lambda e: e.activation(out=RS.ap, in_=RS.ap, func=AF.Exp, scale=-0.5), [RS.buf], [RS.buf])
            for kc in range(KC):
                p.dve(lambda e, kc=kc: e.scalar_tensor_tensor(
                    out=h.ap[:, kc, :], in0=x.ap[:, kc, :], scalar=gain[:, kc:kc + 1], in1=RS.ap,
                    op0=ALU.mult, op1=ALU.mult), [x.buf, RS.buf, self.pkt.buf], [h.buf])

        wcnt = [0]

        def load_w(gi):
            g0, gw = groups[gi]
            i = wcnt[0] % 2
            wcnt[0] += 1
            v1 = w1d.rearrange("(kc p) f -> p kc f", p=128)
            v3 = w3d.rearrange("(kc p) f -> p kc f", p=128)
            hf = 0 if g0 < 1536 else 1
            p.dma("sp", W1[i].ap[:, :, 0:gw], v1[:, :, g0:g0 + gw], [w1bs[hf]], [W1[i].buf], "W1_%d" % i)
            p.dma("act", W3[i].ap[:, :, 0:gw], v3[:, :, g0:g0 + gw], [w3bs[hf]], [W3[i].buf], "W3_%d" % i)
            return i

        load_x(0)
        norm(0)
        nxt = load_w(0)
        mmc = [0]
        for c in range(NCH):
            x = X[c % 2]
            h = H[c % 2]
            if c + 1 < NCH:
                load_x(c + 1)
            for gi, (g0, gw) in enumerate(groups):
                cur = nxt
                if not (c == NCH - 1 and gi == len(groups) - 1):
                    nxt = load_w((gi + 1) % len(groups))
                for j in range(gw // 128):
                    fc = g0 // 128 + j
                    pa = ps_a[mmc[0] % 2]
                    pb = ps_b[mmc[0] % 2]
                    sa = SA[mmc[0] % 2]
                    mmc[0] += 1
                    for kc in range(KC):
                        p.pe(lambda e, pa=pa, kc=kc, cur=cur, j=j, h=h: e.matmul(
                            pa.ap, lhsT=W1[cur].ap[:, kc, j * 128:(j + 1) * 128], rhs=h.ap[:, kc, :],
                            start=(kc == 0), stop=(kc == KC - 1)), [W1[cur].buf, h.buf], [pa.buf])
                    for kc in range(KC):
                        p.pe(lambda e, pb=pb, kc=kc, cur=cur, j=j, h=h: e.matmul(
                            pb.ap, lhsT=W3[cur].ap[:, kc, j * 128:(j + 1) * 128], rhs=h.ap[:, kc, :],
                            start=(kc == 0), stop=(kc == KC - 1)), [W3[cur].buf, h.buf], [pb.buf])
                    p.act(lambda e, pa=pa, sa=sa: e.activation(out=sa.ap, in_=pa.ap, func=AF.Silu),
                          [pa.buf], [sa.buf])
                    p.dve(lambda e, pb=pb, sa=sa, fc=fc: e.tensor_tensor(
                        out=U.ap[:, fc, :], in0=sa.ap, in1=pb.ap, op=ALU.mult),
                        [sa.buf, pb.buf], [U.buf])
                if gi == 2 and c + 1 < NCH:
                    norm(c + 1)
            for dc in range(KC):
                py = ps_y[dc % 2]
                for fc in range(FC):
                    p.pe(lambda e, py=py, fc=fc, dc=dc: e.matmul(
                        py.ap, lhsT=W2.ap[:, fc, dc * 128:(dc + 1) * 128], rhs=U.ap[:, fc, :],
                        start=(fc == 0), stop=(fc == FC - 1)), [W2.buf, U.buf], [py.buf])
                p.dve(lambda e, py=py, dc=dc, x=x: e.scalar_tensor_tensor(
                    out=x.ap[:, dc, :], in0=py.ap, scalar=0.5, in1=x.ap[:, dc, :],
                    op0=ALU.mult, op1=ALU.add), [py.buf, x.buf], [x.buf])
            p.dma("sp", xout_v[:, :, c * TT:(c + 1) * TT], x.ap, [x.buf], [xout_b[c]], "XO%d" % (c % 2))
        self.phase_end()


    def xnorm(self, x, h, gain, SQ, RS, ps_st, TT=512):
        p = self.p
        KC = 8
        for kc in range(KC):
            sq = SQ[kc % 2]
            p.act(lambda e, sq=sq, kc=kc: e.activation(out=sq.ap, in_=x.ap[:, kc, :], func=AF.Square),
                  [x.buf], [sq.buf])
            self.mm(ps_st, ps_st.ap, self.ones.ap, sq.ap, kc == 0, kc == KC - 1, [sq.buf, self.ones.buf])
        p.act(lambda e: e.activation(out=RS.ap, in_=ps_st.ap, func=AF.Ln, bias=self.epsc.ap,
                                     scale=1.0 / D), [ps_st.buf, self.epsc.buf], [RS.buf])
        p.act(lambda e: e.activation(out=RS.ap, in_=RS.ap, func=AF.Exp, scale=-0.5), [RS.buf], [RS.buf])
        for kc in range(KC):
            p.dve(lambda e, kc=kc: e.scalar_tensor_tensor(
                out=h.ap[:, kc, :], in0=x.ap[:, kc, :], scalar=gain[:, kc:kc + 1], in1=RS.ap,
                op0=ALU.mult, op1=ALU.mult), [x.buf, RS.buf, self.pkt.buf], [h.buf])

    def scratch(self):
        d = self.dt
        self.hT = d("hT", [D, S], BF16)
        self.daq = d("daq", [512, S], BF16)
        self.dak = d("dak", [512, S], BF16)
        self.dav = d("dav", [S, 512], BF16)
        self.nqT = d("nqT", [512, S], BF16)
        self.kcT = d("kcT", [128, S], BF16)
        self.vcT = d("vcT", [128, S], BF16)
        self.kselT = d("kselT", [256, S], BF16)
        self.kwinT = d("kwinT", [256, S], BF16)
        self.vsw = d("vsw", [S, 256], BF16)
        self.ngT = d("ngT", [24, S], BF16)
        self.kcn = d("kcn", [2, 128, 256], BF16)
        self.vcz = d("vcz", [2, 2, 128, 192], BF16)
        self.yaT = d("yaT", [512, S], BF16)
        self.ybT = d("ybT", [512, S], BF16)

    def proj_phase(self, l, xin):
        p = self.p
        TT, NCH, KC = 512, 8, 8
        wd = self.w_bf["w_in"][l]
        wb = self.db("w_in_bf", DEPTH)[l]
        wv = wd.rearrange("(kc p) c -> p kc c", p=128)
        xin_b = self.db(xin.name, NCH)
        xin_v = xin.rearrange("(kc p) t -> p kc t", p=128)
        gain = self.pk("g_mix_l%d" % l)
        NW = 2840
        WIN = self.tile("WIN", [KC, NW], BF16)
        for i in range(4):
            q = "sp" if i % 2 == 0 else "act"
            p.dma(q, WIN.ap[:, :, i * 710:(i + 1) * 710], wv[:, :, i * 710:(i + 1) * 710], [wb], [],
                  "WIN_" + q, pwrites=[WIN.buf])
        WD = self.tile("WD", [KC, 512], BF16)
        for i, off in enumerate([2304, 2368, 2560, 2624]):
            for hh in range(2):
                q = "sp" if hh == 0 else "act"
                p.dma(q, WD.ap[:, :, i * 128 + hh * 64:i * 128 + hh * 64 + 64], wv[:, :, off:off + 64], [wb], [],
                      "WD_" + q, pwrites=[WD.buf])
        X = [self.tile("X", [KC, TT], F32) for _ in range(2)]
        H = [self.tile("H", [KC, TT], BF16) for _ in range(2)]
        SQ = [self.tile("SQ", [TT], BF16) for _ in range(2)]
        RS = self.tile("RS", [TT], F32)
        SQX = [self.tile("SQX", [TT], BF16) for _ in range(2)]
        RB = [self.tile("RB", [TT], F32) for _ in range(2)]
        OF = [self.tile("OF", [TT], BF16) for _ in range(3)]
        OT = [self.tile("OT", [TT], BF16) for _ in range(2)]
        ps_st = self.psum[0]
        ps_z = [self.psum[1], self.psum[2]]
        ps_b = [self.psum[3], self.psum[4]]
        ps_t = [self.psum[5], self.psum[6]]
        hT_b = self.db("hT", NCH)
        hT_v = self.hT.rearrange("(kc p) t -> p kc t", p=128)

        def load_x(c):
            t = X[c % 2]
            p.dma("sp", t.ap, xin_v[:, :, c * TT:(c + 1) * TT], [xin_b[c]], [t.buf], "X%d" % (c % 2))

        specs = []
        for j in range(4):
            specs.append((self.daq, j * 128, (WIN, j * 128), "gdq", True))
        for j in range(4):
            specs.append((self.dak, j * 128, (WIN, 512 + j * 128), "gdk", False))
        for j in range(4):
            specs.append((self.nqT, j * 128, (WIN, 1536 + j * 128), "gnq", True))
        for g in range(2):
            specs.append((self.kselT, g * 128, (WD, g * 128), "gks", False))
        for g in range(2):
            specs.append((self.kwinT, g * 128, (WD, (2 + g) * 128), "gkw", False))
        cnt = [0, 0, 0]
        ps_z = [self.psum[1], self.psum[2], self.psum[3]]
        ps_b = [self.psum[4], self.psum[5]]
        ps_t = [self.psum[6], self.psum[7]]
        load_x(0)
        self.xnorm(X[0], H[0], gain, SQX, RS, ps_st)
        p.dma("act", hT_v[:, :, 0:TT], H[0].ap, [H[0].buf], [hT_b[0]], "HO0")
        for c in range(NCH):
            x = X[c % 2]
            h = H[c % 2]
            if c + 1 < NCH:
                load_x(c + 1)
            tsl = slice(c * TT, (c + 1) * TT)
            st_ = {}

            def stage1(i, h=h):
                (dst, r0, (wt, c0), gname, qt) = specs[i]
                pz = ps_z[cnt[0] % 3]
                sq = SQ[cnt[0] % 2]
                st_[i] = (pz, sq, cnt[0])
                cnt[0] += 1
                for kc in range(KC):
                    self.mm(pz, pz.ap, wt.ap[:, kc, c0:c0 + 128], h.ap[:, kc, :], kc == 0, kc == KC - 1,
                            [wt.buf, h.buf])
                p.act(lambda e: e.activation(out=sq.ap, in_=pz.ap, func=AF.Square), [pz.buf], [sq.buf])

            def stage2(i, tsl=tsl):
                (dst, r0, (wt, c0), gname, qt) = specs[i]
                pz, sq, k = st_.pop(i)
                pb = ps_b[k % 2]
                rb = RB[k % 2]
                of = OF[k % 3]
                self.mm(pb, pb.ap, self.BO.ap, sq.ap, True, True, [self.BO.buf, sq.buf])
                if qt:
                    p.act(lambda e: e.activation(out=rb.ap, in_=pb.ap, func=AF.Ln, bias=self.eps64.ap, scale=1.0),
                          [pb.buf, self.eps64.buf], [rb.buf])
                else:
                    p.act(lambda e: e.activation(out=rb.ap, in_=pb.ap, func=AF.Ln, bias=self.epsc.ap,
                                                 scale=1.0 / 64), [pb.buf, self.epsc.buf], [rb.buf])
                p.act(lambda e: e.activation(out=rb.ap, in_=rb.ap, func=AF.Exp, scale=-0.5), [rb.buf], [rb.buf])
                gcol = self.pk("%s_l%d" % (gname, l))
                p.dve(lambda e: e.scalar_tensor_tensor(out=of.ap, in0=pz.ap, scalar=gcol[:, 0:1], in1=rb.ap,
                                                       op0=ALU.mult, op1=ALU.mult),
                      [pz.buf, rb.buf, self.pkt.buf], [of.buf])
                p.dma("sp", dst[r0:r0 + 128, tsl], of.ap, [of.buf], [], "OF%d" % (k % 3),
                      pwrites=[self.db(dst.name)[0]])

            ns = len(specs)
            for i in range(ns + 1):
                if i < ns:
                    stage1(i)
                if i >= 1:
                    stage2(i - 1)
                if i == 8 and c + 1 < NCH:
                    self.xnorm(X[(c + 1) % 2], H[(c + 1) % 2], gain, SQX, RS, ps_st)
                    p.dma("act", hT_v[:, :, (c + 1) * TT:(c + 2) * TT], H[(c + 1) % 2].ap, [H[(c + 1) % 2].buf],
                          [hT_b[c + 1]], "HO%d" % ((c + 1) % 2))
            for dst, c0 in ((self.kcT, 2048), (self.vcT, 2176)):
                pz = ps_z[cnt[0] % 2]
                of = OF[cnt[0] % 3]
                cnt[0] += 1
                for kc in range(KC):
                    self.mm(pz, pz.ap, WIN.ap[:, kc, c0:c0 + 128], h.ap[:, kc, :], kc == 0, kc == KC - 1,
                            [WIN.buf, h.buf])
                p.act(lambda e, of=of, pz=pz: e.activation(out=of.ap, in_=pz.ap, func=AF.Copy),
                      [pz.buf], [of.buf])
                p.dma("sp", dst[:, tsl], of.ap, [of.buf], [], "OF%d" % ((cnt[0] - 1) % 3),
                      pwrites=[self.db(dst.name)[0]])
            pz = ps_z[cnt[0] % 2]
            of = OF[cnt[0] % 3]
            cnt[0] += 1
            for kc in range(KC):
                self.mm(pz, pz.ap[0:24, :], WIN.ap[:, kc, 2816:2840], h.ap[:, kc, :], kc == 0, kc == KC - 1,
                        [WIN.buf, h.buf])
            p.act(lambda e, of=of, pz=pz: e.activation(out=of.ap[0:24], in_=pz.ap[0:24, :], func=AF.Sigmoid),
                  [pz.buf], [of.buf])
            p.dma("sp", self.ngT[:, tsl], of.ap[0:24], [of.buf], [], "OF%d" % ((cnt[0] - 1) % 3),
                  pwrites=[self.db("ngT")[0]])
            for ts in range(4):
                pt = ps_t[cnt[1] % 2]
                ot = OT[cnt[1] % 2]
                cnt[1] += 1
                for kc in range(KC):
                    self.mm(pt, pt.ap, h.ap[:, kc, ts * 128:(ts + 1) * 128], WIN.ap[:, kc, 1024:1536],
                            kc == 0, kc == KC - 1, [WIN.buf, h.buf])
                p.dve(lambda e, ot=ot, pt=pt: e.tensor_copy(out=ot.ap, in_=pt.ap), [pt.buf], [ot.buf])
                r0 = c * TT + ts * 128
                p.dma("act", self.dav[r0:r0 + 128, :], ot.ap, [ot.buf], [], "OT%d" % ((cnt[1] - 1) % 2),
                      pwrites=[self.db("dav")[0]])
                pt = ps_t[cnt[1] % 2]
                ot = OT[cnt[1] % 2]
                cnt[1] += 1
                for i, c0 in enumerate((2432, 2688)):
                    for kc in range(KC):
                        self.mm(pt, pt.ap[:, i * 128:(i + 1) * 128], h.ap[:, kc, ts * 128:(ts + 1) * 128],
                                WIN.ap[:, kc, c0:c0 + 128], kc == 0, kc == KC - 1, [WIN.buf, h.buf])
                p.dve(lambda e, ot=ot, pt=pt: e.tensor_copy(out=ot.ap[:, 0:256], in_=pt.ap[:, 0:256]),
                      [pt.buf], [ot.buf])
                p.dma("act", self.vsw[r0:r0 + 128, :], ot.ap[:, 0:256], [ot.buf], [],
                      "OT%d" % ((cnt[1] - 1) % 2), pwrites=[self.db("vsw")[0]])
        self.phase_end()

    def cmp_phase(self, l):
        p = self.p
        self.cw = {}
        for n in ("cmp_w1_k", "cmp_w2_k", "cmp_w1_v", "cmp_w2_v"):
            if n not in self.dram:
                shp = [DEPTH, 2048, 128] if "w1" in n else [DEPTH, 128, 64]
                self.dt(n, shp, F32, kind="ExternalInput")
        NC_ = 255
        ps_b, ps_h, ps_o, ps_s = self.psum[0], self.psum[1], self.psum[2], self.psum[3]
        for t in ("k", "v"):
            w1 = self.dram["cmp_w1_" + t][l].rearrange("(l d) h -> d l h", d=64)
            w2 = self.dram["cmp_w2_" + t][l]
            W1c = self.tile("W1c", [32, 128], BF16)
            W2c = self.tile("W2c", [128], BF16)
            for hh in range(2):
                p.dma("pool", W1c.ap[hh * 64:(hh + 1) * 64], w1, [], [], "W1c", pwrites=[W1c.buf])
                p.dma("pool", W2c.ap[:, hh * 64:(hh + 1) * 64], w2, [], [], "W2c", pwrites=[W2c.buf])
            peb = self.tile("peb", [32], BF16)
            pe32 = self.pk("pe%s_l%d" % (t, l))
            p.dve(lambda e, peb=peb, pe32=pe32: e.tensor_copy(out=peb.ap, in_=pe32), [self.pkt.buf], [peb.buf])
            KR = self.tile("KR", [S], BF16)
            src = self.kcT if t == "k" else self.vcT
            p.dma("sp", KR.ap, src, [self.db(src.name)[0]], [KR.buf], "KR" + t)
            for g in range(2):
                rows = slice(g * 64, (g + 1) * 64)
                bcol = self.tile("bcol", [1], F32)
                for li in range(32):
                    self.mm(ps_b, ps_b.ap[:, 0:1], W1c.ap[rows, li, :], peb.ap[rows, li:li + 1], li == 0, li == 31,
                            [W1c.buf, peb.buf])
                p.dve(lambda e, bcol=bcol: e.tensor_copy(out=bcol.ap, in_=ps_b.ap[:, 0:1]), [ps_b.buf], [bcol.buf])
                for li in range(32):
                    self.mm(ps_h, ps_h.ap[:, 0:NC_], W1c.ap[rows, li, :], KR.ap[rows, li:li + 16 * 254 + 1:16],
                            li == 0, li == 31, [W1c.buf, KR.buf])
                TS = self.tile("TS", [256], F32)
                T2 = self.tile("T2", [256], F32)
                GL = self.tile("GL", [256], BF16)
                p.act(lambda e, TS=TS, bcol=bcol: e.activation(out=TS.ap[:, 0:NC_], in_=ps_h.ap[:, 0:NC_],
                                                               func=AF.Identity, bias=bcol.ap, scale=1.0),
                      [ps_h.buf, bcol.buf], [TS.buf])
                p.dve(lambda e, TS=TS, T2=T2: e.tensor_tensor(out=T2.ap[:, 0:NC_], in0=TS.ap[:, 0:NC_],
                                                              in1=TS.ap[:, 0:NC_], op=ALU.mult), [TS.buf], [T2.buf])
                p.dve(lambda e, T2=T2: e.tensor_scalar(out=T2.ap[:, 0:NC_], in0=T2.ap[:, 0:NC_], scalar1=0.044715,
                                                       scalar2=1.0, op0=ALU.mult, op1=ALU.add), [T2.buf], [T2.buf])
                p.dve(lambda e, TS=TS, T2=T2: e.tensor_tensor(out=T2.ap[:, 0:NC_], in0=T2.ap[:, 0:NC_],
                                                              in1=TS.ap[:, 0:NC_], op=ALU.mult), [TS.buf, T2.buf],
                      [T2.buf])
                p.act(lambda e, T2=T2: e.activation(out=T2.ap[:, 0:NC_], in_=T2.ap[:, 0:NC_], func=AF.Sigmoid,
                                                    scale=1.5957691216057308), [T2.buf], [T2.buf])
                p.dve(lambda e, TS=TS, T2=T2, GL=GL: e.tensor_tensor(out=GL.ap[:, 0:NC_], in0=T2.ap[:, 0:NC_],
                                                                     in1=TS.ap[:, 0:NC_], op=ALU.mult),
                      [TS.buf, T2.buf], [GL.buf])
                if t == "k":
                    self.mm(ps_o, ps_o.ap[:, 0:NC_], W2c.ap, GL.ap[:, 0:NC_], True, True, [W2c.buf, GL.buf])
                    SQ = self.tile("SQ", [256], BF16)
                    RB = self.tile("RB", [256], F32)
                    KO = self.tile("KO", [256], BF16)
                    p.pool(lambda e, KO=KO: e.memset(KO.ap, 0.0), [], [KO.buf])
                    p.act(lambda e, SQ=SQ: e.activation(out=SQ.ap[:, 0:NC_], in_=ps_o.ap[:, 0:NC_], func=AF.Square),
                          [ps_o.buf], [SQ.buf])
                    self.mm(ps_s, ps_s.ap[:, 0:NC_], self.BO.ap, SQ.ap[:, 0:NC_], True, True, [self.BO.buf, SQ.buf])
                    p.act(lambda e, RB=RB: e.activation(out=RB.ap[:, 0:NC_], in_=ps_s.ap[:, 0:NC_], func=AF.Sqrt,
                                                        bias=self.epsc.ap, scale=1.0 / 64),
                          [ps_s.buf, self.epsc.buf], [RB.buf])
                    p.dve(lambda e, RB=RB: e.reciprocal(out=RB.ap[:, 0:NC_], in_=RB.ap[:, 0:NC_]), [RB.buf], [RB.buf])
                    gcol = self.pk("gkc_l%d" % l)
                    p.dve(lambda e, KO=KO, RB=RB, gcol=gcol: e.scalar_tensor_tensor(
                        out=KO.ap[:, 0:NC_], in0=ps_o.ap[:, 0:NC_], scalar=gcol[:, 0:1], in1=RB.ap[:, 0:NC_],
                        op0=ALU.mult, op1=ALU.mult), [ps_o.buf, RB.buf, self.pkt.buf, KO.buf], [KO.buf])
                    p.dma("sp", self.kcn[g], KO.ap, [KO.buf], [], "KO", pwrites=[self.db("kcn")[0]])
                else:
                    for ct in range(2):
                        n = 128 if ct == 0 else 127
                        VZ = self.tile("VZ", [192], BF16)
                        p.pool(lambda e, VZ=VZ: e.memset(VZ.ap, 0.0), [], [VZ.buf])
                        self.mm(ps_o, ps_o.ap[0:n, 0:64], GL.ap[:, ct * 128:ct * 128 + n], W2c.ap[:, 0:64], True, True,
                                [W2c.buf, GL.buf])
                        p.dve(lambda e, VZ=VZ, n=n: e.tensor_copy(out=VZ.ap[0:n, 64:128], in_=ps_o.ap[0:n, 0:64]),
                              [ps_o.buf, VZ.buf], [VZ.buf])
                        p.dma("sp", self.vcz[g, ct], VZ.ap, [VZ.buf], [], "VZ", pwrites=[self.db("vcz")[0]])
        self.phase_end()

    def da_phase(self, l):
        p = self.p
        lam_init = 0.8 - 0.6 * float(np.exp(-0.3 * l))
        LB = self.pk("lam_l%d" % l)
        junk = self.tile("junk", [64], F32)
        d12 = self.tile("d12", [2], F32)
        lamneg = self.tile("lamneg", [1], F32)
        sbias = self.tile("sbias", [1], F32)
        p.pool(lambda e: e.memset(sbias.ap, EPS / (1.0 - lam_init) ** 2), [], [sbias.buf])
        for i in range(2):
            p.dve(lambda e, i=i: e.tensor_tensor(out=junk.ap, in0=LB[:, i * 128:i * 128 + 64],
                                                 in1=LB[:, i * 128 + 64:i * 128 + 128], op=ALU.mult),
                  [self.pkt.buf], [junk.buf])
            p.dve(lambda e, i=i: e.tensor_reduce(out=d12.ap[:, i:i + 1], in_=junk.ap, axis=AX.X, op=ALU.add),
                  [junk.buf], [d12.buf])
        p.act(lambda e: e.activation(out=d12.ap, in_=d12.ap, func=AF.Exp), [d12.buf], [d12.buf])
        p.dve(lambda e: e.tensor_tensor(out=lamneg.ap, in0=d12.ap[:, 1:2], in1=d12.ap[:, 0:1], op=ALU.subtract),
              [d12.buf], [lamneg.buf])
        p.dve(lambda e: e.tensor_scalar_add(out=lamneg.ap, in0=lamneg.ap, scalar1=-lam_init), [lamneg.buf],
              [lamneg.buf])
        sscale = 1.0 / (128.0 * (1.0 - lam_init) ** 2)
        gsub = self.pk("gsub_l%d" % l)
        tb31 = self.pk("tb31")
        QT = [self.tile("QT", [S], BF16) for _ in range(2)]
        KT = [[self.tile("KT", [S], BF16) for _ in range(2)] for _ in range(2)]
        for i2 in range(2):
            for comp in range(2):
                p.pool(lambda e, t=KT[i2][comp]: e.memset(t.ap, 0.0), [], [KT[i2][comp].buf])
        VV = [self.tile("VV", [32, 128], BF16) for _ in range(2)]
        ST = [self.tile("ST", [1792], BF16) for _ in range(2)]
        STR = self.tile("STR", [1792], BF16)
        PT = [self.tile("PT", [512], BF16) for _ in range(8)]
        PT0 = [self.tile("PT0", [512], BF16) for _ in range(4)]
        R = [self.tile("R", [512], F32) for _ in range(2)]
        A = [self.tile("A", [512], F32) for _ in range(2)]
        Y = self.tile("Y", [512], F32)
        SQ = self.tile("SQ", [512], BF16)
        RS = self.tile("RS", [512], F32)
        YO = [self.tile("YO", [512], BF16) for _ in range(2)]
        psS = self.psum[0:4]
        psO = self.psum[4:6]
        psL = self.psum[6:8]
        qb, kb, vb, fb = self.db("daq")[0], self.db("dak")[0], self.db("dav")[0], self.db("Fd")[0]
        ya_b = self.db("yaT")[0]
        OC = [self.tile("OC", [512], F32) for _ in range(2)]
        LC = [self.tile("LC", [512], F32) for _ in range(2)]
        psS = self.psum[0:4]
        tiles = {}
        cnt = {"s": 0, "pt": 0, "p0": 0}

        def load_head(h):
            i2 = h % 2
            qt, vv, st = QT[i2], VV[i2], ST[i2]
            p.dma("sp", qt.ap, self.daq[h * 128:(h + 1) * 128, :], [qb], [qt.buf], "QT%d" % i2)
            for comp in range(2):
                kt_ = KT[i2][comp]
                rs_ = slice(comp * 64, comp * 64 + 64)
                p.dma("act", kt_.ap[rs_], self.dak[h * 128 + comp * 64:h * 128 + comp * 64 + 64, :], [kb], [kt_.buf],
                      "KT%d%d" % (i2, comp))
            p.dma("sp", vv.ap, self.dav[:, h * 128:(h + 1) * 128].rearrange("(kt p) e -> p kt e", p=128), [vb],
                  [vv.buf], "VV%d" % i2)
            p.dma("act", STR.ap, bass.AP(self.Fd.tensor, h * LF + 3585, [[1, 128], [1, 1792]]), [fb], [STR.buf],
                  "STR")
            for c4 in range(4):
                ps = psS[cnt["s"] % 4]
                cnt["s"] += 1
                cs_ = slice(c4 * 448, (c4 + 1) * 448)
                self.mm(ps, ps.ap[:, 0:448], self.J.ap, STR.ap[:, cs_], True, True, [self.J.buf, STR.buf])
                p.act(lambda e, ps=ps, cs_=cs_: e.activation(out=st.ap[:, cs_], in_=ps.ap[:, 0:448], func=AF.Exp),
                      [ps.buf], [st.buf])

        jobs = []
        for h in range(4):
            for qc in range(8):
                nk = 4 * qc + 4
                for kt in range(nk):
                    for comp in range(2):
                        jobs.append((h, qc, kt, comp, nk))
        def front(j):
            h, qc, kt, comp, nk = jobs[j]
            i2 = h % 2
            qt, kt_, st = QT[i2], KT[i2][comp], ST[i2]
            qs = slice(qc * 512, (qc + 1) * 512)
            delta = qc * 512 - kt * 128
            near = delta <= 896
            ps = psS[cnt["s"] % 4]
            cnt["s"] += 1
            self.mm(ps, ps.ap, kt_.ap[:, kt * 128:(kt + 1) * 128], qt.ap[:, qs], True, True, [kt_.buf, qt.buf])
            pt = PT[cnt["pt"] % 8]
            cnt["pt"] += 1
            if near:
                p0 = PT0[cnt["p0"] % 4]
                cnt["p0"] += 1
                p.act(lambda e: e.activation(out=p0.ap, in_=ps.ap, func=AF.Exp), [ps.buf], [p0.buf])
                p.dve(lambda e: e.tensor_tensor(out=pt.ap, in0=p0.ap, in1=st.ap[:, delta + 384:delta + 896],
                                                op=ALU.mult), [p0.buf, st.buf], [pt.buf])
            else:
                p.act(lambda e: e.activation(out=pt.ap, in_=ps.ap, func=AF.Exp, bias=tb31[:, h:h + 1], scale=1.0),
                      [ps.buf, self.pkt.buf], [pt.buf])
            tiles[j] = pt

        steps = []

        def epi(h, qc):
            qs = slice(qc * 512, (qc + 1) * 512)
            for comp in range(2):
                p.act(lambda e, comp=comp: e.activation(out=LC[comp].ap, in_=psL[comp].ap, func=AF.Ln),
                      [psL[comp].buf], [LC[comp].buf])
                p.act(lambda e, comp=comp: e.activation(out=R[comp].ap, in_=LC[comp].ap, func=AF.Exp, scale=-1.0),
                      [LC[comp].buf], [R[comp].buf])
                p.dve(lambda e, comp=comp: e.tensor_tensor(out=A[comp].ap, in0=psO[comp].ap, in1=R[comp].ap,
                                                           op=ALU.mult), [psO[comp].buf, R[comp].buf], [A[comp].buf])
            steps.append(lambda: p.dve(lambda e: e.scalar_tensor_tensor(
                out=Y.ap, in0=A[1].ap, scalar=lamneg.ap[:, 0:1], in1=A[0].ap, op0=ALU.mult, op1=ALU.add),
                [A[0].buf, A[1].buf, lamneg.buf], [Y.buf]))
            steps.append(lambda: p.act(lambda e: e.activation(out=SQ.ap, in_=Y.ap, func=AF.Square), [Y.buf], [SQ.buf]))
            steps.append(None)
            steps.append(None)

            def stat():
                ps = psS[cnt["s"] % 4]
                cnt["s"] += 1
                self.mm(ps, ps.ap, self.ones.ap, SQ.ap, True, True, [self.ones.buf, SQ.buf])
                p.act(lambda e: e.activation(out=RS.ap, in_=ps.ap, func=AF.Ln, bias=sbias.ap, scale=sscale),
                      [ps.buf, sbias.buf], [RS.buf])
                p.act(lambda e: e.activation(out=RS.ap, in_=RS.ap, func=AF.Exp, scale=-0.5), [RS.buf], [RS.buf])
            steps.append(stat)
            steps.append(None)
            yo = YO[qc % 2]

            def fin():
                p.dve(lambda e: e.scalar_tensor_tensor(out=yo.ap, in0=Y.ap, scalar=gsub[:, 0:1], in1=RS.ap,
                                                       op0=ALU.mult, op1=ALU.mult),
                      [Y.buf, RS.buf, self.pkt.buf], [yo.buf])
                p.dma("sp", self.yaT[h * 128:(h + 1) * 128, qs], yo.ap, [yo.buf], [], "YO%d" % (qc % 2),
                      pwrites=[ya_b])
            steps.append(fin)

        def drain(n):
            while n > 0 and steps:
                f = steps.pop(0)
                if f is not None:
                    f()
                n -= 1

        def back(j):
            h, qc, kt, comp, nk = jobs[j]
            vv = VV[h % 2]
            pt = tiles.pop(j)
            self.mm(psO[comp], psO[comp].ap, vv.ap[:, kt, :], pt.ap, kt == 0, kt == nk - 1, [vv.buf, pt.buf])
            self.mm(psL[comp], psL[comp].ap, self.ones.ap, pt.ap, kt == 0, kt == nk - 1, [self.ones.buf, pt.buf])
            if kt == nk - 1 and comp == 1:
                drain(len(steps))
                epi(h, qc)
                if qc == 7 and h + 2 < 4:
                    load_head(h + 2)

        LA = 3
        load_head(0)
        load_head(1)
        n = len(jobs)
        for i in range(n + LA):
            if i < n:
                front(i)
            if i >= LA:
                back(i - LA)
            drain(2)
        drain(len(steps))
        self.phase_end()

    def nsa_phase(self, l):
        p = self.p
        for n, shp in (("ebig", [128, S]), ("gs", [24, 1536]), ("cadd", [S, 64])):
            if n not in self.dram:
                self.dt(n, shp, F32, kind="ExternalInput")
        EB = self.castload("EB", [S], self.dram["ebig"])
        GS = self.castload("GS", [1536], self.dram["gs"], parts=24)
        OV = self.castload("OV", [2, 65], self.cst_in[:, C_OV:C_OV + 130].rearrange("p (a b) -> p a b", a=2))
        CA = self.tile("CA", [32, 64], F32)
        p.dma("sp", CA.ap, self.dram["cadd"].rearrange("(qt p) j -> p qt j", p=128), [], [CA.buf], "CA")
        NG = self.tile("NG", [S], BF16)
        p.dma("act", NG.ap[0:24], self.ngT, [self.db("ngT")[0]], [NG.buf], "NG")
        tb31 = self.pk("tb31")
        fb = self.db("Fd")[0]
        tiny = 1e-30
        psS = self.psum[0:4]
        psO, psL, psG, psI = self.psum[4], self.psum[5], self.psum[6], self.psum[7]
        psTb = psI.ap.bitcast(BF16)
        PT = [self.tile("PT", [512], BF16) for _ in range(6)]
        Rt = self.tile("Rt", [512], F32)
        T1 = self.tile("T1", [512], F32)
        T2 = [self.tile("T2", [512], F32) for _ in range(2)]
        YO = [self.tile("YO", [512], BF16) for _ in range(2)]
        cnt = {"s": 0, "pt": 0, "t2": 0, "bt": 0, "yo": 0}

        def sbank():
            cnt["s"] += 1
            return psS[cnt["s"] % 3]

        psI2 = [self.psum[3], self.psum[7]]

        def ptile():
            cnt["pt"] += 1
            return PT[cnt["pt"] % 6]

        def epilogue(gi, pair, b, qs, YBt, first, guard=False):
            if guard:
                p.dve(lambda e: e.tensor_scalar_max(out=Rt.ap, in0=psL.ap, scalar1=tiny), [psL.buf], [Rt.buf])
                p.act(lambda e: e.activation(out=Rt.ap, in_=Rt.ap, func=AF.Ln), [Rt.buf], [Rt.buf])
                p.act(lambda e: e.activation(out=Rt.ap, in_=Rt.ap, func=AF.Exp, scale=-1.0), [Rt.buf], [Rt.buf])
            else:
                p.dve(lambda e: e.reciprocal(out=Rt.ap, in_=psL.ap), [psL.buf], [Rt.buf])
            p.dve(lambda e: e.tensor_tensor(out=T1.ap, in0=psO.ap, in1=Rt.ap, op=ALU.mult), [psO.buf, Rt.buf],
                  [T1.buf])
            c0 = ((gi * 2 + pair) * 3 + b) * 128
            self.mm(psG, psG.ap, GS.ap[0:24, c0:c0 + 128], NG.ap[0:24, qs], True, True, [GS.buf, NG.buf])
            if first:
                p.dve(lambda e: e.tensor_tensor(out=YBt.ap[:, qs], in0=T1.ap, in1=psG.ap, op=ALU.mult),
                      [T1.buf, psG.buf], [YBt.buf])
            else:
                cnt["t2"] += 1
                t2 = T2[cnt["t2"] % 2]
                p.dve(lambda e: e.tensor_tensor(out=t2.ap, in0=T1.ap, in1=psG.ap, op=ALU.mult),
                      [T1.buf, psG.buf], [t2.buf])
                p.pool(lambda e: e.tensor_tensor(out=YBt.ap[:, qs], in0=YBt.ap[:, qs], in1=t2.ap, op=ALU.add),
                       [t2.buf, YBt.buf], [YBt.buf])

        for g in range(2):
            base = self.aoff
            KS = [self.tile("KS", [S], BF16) for _ in range(2)]
            KW = [self.tile("KW", [S], BF16) for _ in range(2)]
            for hh in range(2):
                rs_ = slice(hh * 64, hh * 64 + 64)
                for kk, src, nm in ((KS, self.kselT, "kselT"), (KW, self.kwinT, "kwinT")):
                    p.pool(lambda e, t=kk[hh]: e.memset(t.ap, 0.0), [], [kk[hh].buf])
                    p.dma("sp" if hh == 0 else "act", kk[hh].ap[rs_], src[g * 128 + hh * 64:g * 128 + hh * 64 + 64, :],
                          [self.db(nm)[0]], [kk[hh].buf], "K%s%d" % (nm[1], hh))
            VS = self.tile("VS", [32, 192], BF16)
            VW = self.tile("VW", [32, 192], BF16)
            for i, vt in enumerate((VS, VW)):
                p.pool(lambda e, vt=vt: e.memset(vt.ap, 0.0), [], [vt.buf])
                c0 = i * 128 + g * 64
                p.dma("sp" if i == 0 else "act", vt.ap[:, :, 64:128],
                      self.vsw[:, c0:c0 + 64].rearrange("(kt p) d -> p kt d", p=128), [self.db("vsw")[0]],
                      [vt.buf], "VSW%d" % i)
            KC = [self.tile("KC", [256], BF16) for _ in range(2)]
            for hh in range(2):
                rs_ = slice(hh * 64, hh * 64 + 64)
                p.pool(lambda e, t=KC[hh]: e.memset(t.ap, 0.0), [], [KC[hh].buf])
                p.dma("sp", KC[hh].ap[rs_], self.kcn[g, hh * 64:hh * 64 + 64, :], [self.db("kcn")[0]], [KC[hh].buf],
                      "KC%d" % hh)
            VC = self.tile("VC", [2, 192], BF16)
            p.dma("act", VC.ap, self.vcz[g].rearrange("ct p c -> p ct c"), [self.db("vcz")[0]], [VC.buf], "VC")
            SELB = self.tile("SELB", [S], BF16)
            p.pool(lambda e, SELB=SELB: e.memset(SELB.ap, 0.0), [], [SELB.buf])
            YB = [self.tile("YB", [S], F32) for _ in range(2)]
            QP = [self.tile("QP", [S], BF16) for _ in range(2)]
            for pair in range(2):
                r0 = (g * 2 + pair) * 128
                p.dma("sp" if pair == 0 else "act", QP[pair].ap, self.nqT[r0:r0 + 128, :], [self.db("nqT")[0]],
                      [QP[pair].buf], "QP%d" % pair)
            PC = [[[self.tile("PC", [512], BF16) for _ in range(2)] for _ in range(2)] for _ in range(2)]
            BT = [self.tile("BT", [512], BF16) for _ in range(4)]
            LS = self.tile("LS", [4], F32)
            IMP = self.tile("IMP", [64], F32)
            M8 = self.tile("M8", [8], F32)
            SB = self.tile("SB", [64], BF16)
            stages = getattr(self, "nsa_stages", "ciws")
            for qc in range(8 if "c" in stages else 0):
                q0 = qc * 512
                qs = slice(q0, q0 + 512)
                ncts = 1 if qc < 4 else 2
                for pair in range(2):
                    for hh in range(2):
                        rows = slice(hh * 64, hh * 64 + 64)
                        head = g * 4 + pair * 2 + hh
                        for ct in range(ncts):
                            n = 128 if ct == 0 else 127
                            bt = BT[cnt["bt"] % 4]
                            q = "sp" if cnt["bt"] % 2 == 0 else "act"
                            off = (4 + head) * LF + 2033 + q0 - 16 * ct * 128
                            p.dma(q, bt.ap, bass.AP(self.Fd.tensor, off, [[16, 128], [1, 512]]), [fb], [bt.buf],
                                  "BT%d" % (cnt["bt"] % 4))
                            cnt["bt"] += 1
                            ps = sbank()
                            self.mm(ps, ps.ap[0:n, :], KC[hh].ap[:, ct * 128:ct * 128 + n], QP[pair].ap[:, qs],
                                    True, False, [KC[hh].buf, QP[pair].buf])
                            self.mm(ps, ps.ap[0:n, :], self.J.ap[:, 0:n], bt.ap, False, True, [self.J.buf, bt.buf])
                            pc = PC[pair][hh][ct]
                            p.act(lambda e, pc=pc, ps=ps, n=n: e.activation(out=pc.ap[0:n], in_=ps.ap[0:n, :],
                                                                             func=AF.Exp), [ps.buf], [pc.buf])
                    first = True
                    for hh in range(2):
                        for ct in range(ncts):
                            n = 128 if ct == 0 else 127
                            last = (hh == 1 and ct == ncts - 1)
                            pc = PC[pair][hh][ct]
                            cs = slice(64, 192) if hh == 0 else slice(0, 128)
                            self.mm(psO, psO.ap, VC.ap[0:n, ct, cs], pc.ap[0:n], first, last, [VC.buf, pc.buf])
                            self.mm(psL, psL.ap, self.OZ.ap[0:n, cs], pc.ap[0:n], first, last,
                                    [self.OZ.buf, pc.buf])
                            first = False
                    epilogue(g, pair, 0, qs, YB[pair], True, guard=True)
                for qsub in range(4 if "i" in stages else 0):
                    qt = qc * 4 + qsub
                    psI = psI2[qt % 2]
                    psTb = psI.ap.bitcast(BF16)
                    for h4 in range(4):
                        pair, hh = h4 // 2, h4 % 2
                        for ct in range(ncts):
                            n = 128 if ct == 0 else 127
                            pc = PC[pair][hh][ct]
                            self.mm(psI, psI.ap[:, h4 * 65:(h4 + 1) * 65], pc.ap[0:n, qsub * 128:(qsub + 1) * 128],
                                    OV.ap[0:n, ct, :], ct == 0, ct == ncts - 1, [pc.buf, OV.buf])
                    p.dve(lambda e, psI=psI: e.tensor_scalar_max(out=LS.ap, in0=psI.ap[:, 64:260:65], scalar1=tiny),
                          [psI.buf], [LS.buf])
                    p.dve(lambda e: e.reciprocal(out=LS.ap, in_=LS.ap), [LS.buf], [LS.buf])
                    p.dve(lambda e, psI=psI: e.tensor_scalar(out=IMP.ap, in0=psI.ap[:, 0:64], scalar1=LS.ap[:, 0:1],
                                                    scalar2=None, op0=ALU.mult), [psI.buf, LS.buf], [IMP.buf])
                    for h4 in range(1, 4):
                        p.dve(lambda e, h4=h4, psI=psI: e.scalar_tensor_tensor(
                            out=IMP.ap, in0=psI.ap[:, h4 * 65:h4 * 65 + 64], scalar=LS.ap[:, h4:h4 + 1], in1=IMP.ap,
                            op0=ALU.mult, op1=ALU.add), [psI.buf, LS.buf, IMP.buf], [IMP.buf])
                    p.dve(lambda e, qt=qt: e.tensor_tensor(out=IMP.ap, in0=IMP.ap, in1=CA.ap[:, qt, :], op=ALU.add),
                          [IMP.buf, CA.buf], [IMP.buf])
                    p.dve(lambda e: e.max(out=M8.ap, in_=IMP.ap), [IMP.buf], [M8.buf])
                    p.dve(lambda e: e.tensor_scalar(out=SB.ap, in0=IMP.ap, scalar1=M8.ap[:, 7:8], scalar2=NEG,
                                                    op0=ALU.is_lt, op1=ALU.mult), [IMP.buf, M8.buf], [SB.buf])
                    p.pe(lambda e, psTb=psTb: e.transpose(out=psTb[0:64, 640:768], in_=SB.ap, identity=self.identb.ap),
                         [SB.buf, self.identb.buf], [psI.buf])
                    p.act(lambda e, qt=qt, psTb=psTb: e.activation(out=SELB.ap[0:64, qt * 128:(qt + 1) * 128],
                                                        in_=psTb[0:64, 640:768], func=AF.Copy),
                          [psI.buf, SELB.buf], [SELB.buf])
            psS4 = [self.psum[0], self.psum[1], self.psum[2], self.psum[7]]
            psO2 = [self.psum[3], self.psum[4]]
            psL2 = [self.psum[5], self.psum[6]]
            noE = getattr(self, "nsa_noE", False)
            for pair in range(2):
                base2 = self.aoff
                STc = [self.tile("STc", [1792], BF16) for _ in range(2)]
                STw = [self.tile("STw", [1408], BF16) for _ in range(2)]
                STR = self.tile("STR", [1792], BF16)
                PT0 = [self.tile("PT0", [512], BF16) for _ in range(4)]
                for hh in range(2):
                    head = g * 4 + pair * 2 + hh
                    for (dst, off, wid, ch) in ((STc[hh], 3585, 1792, 448), (STw[hh], 8193, 1408, 352)):
                        p.dma("sp", STR.ap[:, 0:wid], bass.AP(self.Fd.tensor, (4 + head) * LF + off, [[1, 128], [1, wid]]),
                              [fb], [STR.buf], "STRn")
                        for c4 in range(4):
                            cnt["s"] += 1
                            ps = psS4[cnt["s"] % 4]
                            cs_ = slice(c4 * ch, (c4 + 1) * ch)
                            self.mm(ps, ps.ap[:, 0:ch], self.J.ap, STR.ap[:, cs_], True, True, [self.J.buf, STR.buf])
                            p.act(lambda e, ps=ps, cs_=cs_, dst=dst, ch=ch: e.activation(
                                out=dst.ap[:, cs_], in_=ps.ap[:, 0:ch], func=AF.Exp), [ps.buf], [dst.buf])
                qp = QP[pair]
                YBt = YB[pair]
                jobs = []
                ngrp = 0
                for qc in range(getattr(self, "nsa_maxqc", 8)):
                    for br in (1, 2):
                        if (br == 1 and "s" not in stages) or (br == 2 and "w" not in stages):
                            continue
                        if br == 1:
                            kts = list(range(4 * qc + 4))
                        else:
                            kts = [kt for kt in range(4 * qc - 4, 4 * qc + 4) if kt >= 0]
                        for ki, kt in enumerate(kts):
                            for hh in range(2):
                                jobs.append((qc, br, kt, hh, ki == 0 and hh == 0, ki == len(kts) - 1 and hh == 1,
                                             ngrp % 2))
                        ngrp += 1
                tl = {}

                def front(j):
                    qc, br, kt, hh, fst, lst, gp = jobs[j]
                    q0 = qc * 512
                    qs = slice(q0, q0 + 512)
                    KK, STs = (KS, STc) if br == 1 else (KW, STw)
                    delta = q0 - kt * 128
                    near = (delta <= 896) or br == 2
                    useE = (br == 1 and not noE)
                    rows = slice(hh * 64, hh * 64 + 64)
                    head = g * 4 + pair * 2 + hh
                    cnt["s"] += 1
                    ps = psS4[cnt["s"] % 4]
                    self.mm(ps, ps.ap, KK[hh].ap[:, kt * 128:(kt + 1) * 128], qp.ap[:, qs], True, not useE,
                            [KK[hh].buf, qp.buf])
                    if useE:
                        self.mm(ps, ps.ap, EB.ap[:, kt * 128:(kt + 1) * 128], SELB.ap[:, qs], False, True,
                                [EB.buf, SELB.buf])
                    pt = ptile()
                    if near:
                        cnt["p0"] = cnt.get("p0", 0) + 1
                        p0 = PT0[cnt["p0"] % 4]
                        p.act(lambda e: e.activation(out=p0.ap, in_=ps.ap, func=AF.Exp), [ps.buf], [p0.buf])
                        p.dve(lambda e: e.tensor_tensor(out=pt.ap, in0=p0.ap, in1=STs[hh].ap[:, delta + 384:delta + 896],
                                                        op=ALU.mult), [p0.buf, STs[hh].buf], [pt.buf])
                    else:
                        p.act(lambda e: e.activation(out=pt.ap, in_=ps.ap, func=AF.Exp,
                                                     bias=tb31[:, 4 + head:5 + head], scale=1.0),
                              [ps.buf, self.pkt.buf], [pt.buf])
                    tl[j] = pt

                def back(j):
                    qc, br, kt, hh, fst, lst, gp = jobs[j]
                    qs = slice(qc * 512, qc * 512 + 512)
                    VVt = VS if br == 1 else VW
                    pt = tl.pop(j)
                    ybt = YBt
                    po, pl = psO2[gp], psL2[gp]
                    cs = slice(64, 192) if hh == 0 else slice(0, 128)
                    self.mm(po, po.ap, VVt.ap[:, kt, cs], pt.ap, fst, lst, [VVt.buf, pt.buf])
                    self.mm(pl, pl.ap, self.OZ.ap[:, cs], pt.ap, fst, lst, [self.OZ.buf, pt.buf])
                    if not lst:
                        return
                    c0 = ((g * 2 + pair) * 3 + br) * 128
                    cnt["s"] += 1
                    psG2 = psS4[cnt["s"] % 4]
                    self.mm(psG2, psG2.ap, GS.ap[0:24, c0:c0 + 128], NG.ap[0:24, qs], True, True, [GS.buf, NG.buf])
                    p.act(lambda e: e.activation(out=Rt.ap, in_=pl.ap, func=AF.Ln), [pl.buf], [Rt.buf])
                    p.act(lambda e: e.activation(out=Rt.ap, in_=Rt.ap, func=AF.Exp, scale=-1.0), [Rt.buf], [Rt.buf])
                    p.dve(lambda e: e.tensor_tensor(out=T1.ap, in0=po.ap, in1=Rt.ap, op=ALU.mult), [po.buf, Rt.buf],
                          [T1.buf])
                    cnt["t2"] += 1
                    t2 = T2[cnt["t2"] % 2]
                    p.dve(lambda e: e.tensor_tensor(out=t2.ap, in0=T1.ap, in1=psG2.ap, op=ALU.mult),
                          [T1.buf, psG2.buf], [t2.buf])
                    p.pool(lambda e: e.tensor_tensor(out=ybt.ap[:, qs], in0=ybt.ap[:, qs], in1=t2.ap, op=ALU.add),
                           [t2.buf, ybt.buf], [ybt.buf])
                    if br == 2 or "w" not in stages:
                        yo = YO[cnt["yo"] % 2]
                        p.pool(lambda e: e.tensor_copy(out=yo.ap, in_=ybt.ap[:, qs]), [ybt.buf], [yo.buf])
                        r0 = (g * 2 + pair) * 128
                        p.dma("sp", self.ybT[r0:r0 + 128, qs], yo.ap, [yo.buf], [], "YBO%d" % (cnt["yo"] % 2),
                              pwrites=[self.db("ybT")[0]])
                        cnt["yo"] += 1

                LA = 3
                n = len(jobs)
                for i in range(n + LA):
                    if i < n:
                        front(i)
                    if i >= LA:
                        back(i - LA)
                self.phase_end()
                self.aoff = base2
            self.aoff = base
        self.aoff = self.abase

    def merge_phase(self, l, xin, xout):
        p = self.p
        TT, NCH, KC = 512, 8, 8
        wv = self.w_bf["w_in"][l].rearrange("(kc p) c -> p kc c", p=128)
        wb = self.db("w_in_bf", DEPTH)[l]
        WMG = self.tile("WMG", [KC, 2048], BF16)
        for i in range(4):
            q = "sp" if i % 2 == 0 else "act"
            p.dma(q, WMG.ap[:, :, i * 512:(i + 1) * 512], wv[:, :, 2840 + i * 512:2840 + (i + 1) * 512], [wb], [],
                  "WMG_" + q, pwrites=[WMG.buf])
        WA = self.tile("WA", [4, D], BF16)
        WB = self.tile("WB", [4, D], BF16)
        WO = self.tile("WO", [KC, D], BF16)
        p.dma("sp", WA.ap, self.w_bf["w_branch_a"][l].rearrange("(kc p) c -> p kc c", p=128),
              [self.db("w_branch_a_bf", DEPTH)[l]], [WA.buf], "WA")
        p.dma("act", WB.ap, self.w_bf["w_branch_b"][l].rearrange("(kc p) c -> p kc c", p=128),
              [self.db("w_branch_b_bf", DEPTH)[l]], [WB.buf], "WB")
        p.dma("sp", WO.ap, self.w_bf["w_out"][l].rearrange("(kc p) c -> p kc c", p=128),
              [self.db("w_out_bf", DEPTH)[l]], [WO.buf], "WO")
        xin_b = self.db(xin.name, NCH)
        xout_b = self.db(xout.name, NCH)
        xin_v = xin.rearrange("(kc p) t -> p kc t", p=128)
        xout_v = xout.rearrange("(kc p) t -> p kc t", p=128)
        hT_v = self.hT.rearrange("(kc p) t -> p kc t", p=128)
        ya_v = self.yaT.rearrange("(kc p) t -> p kc t", p=128)
        yb_v = self.ybT.rearrange("(kc p) t -> p kc t", p=128)
        X = [self.tile("X", [KC, TT], F32) for _ in range(2)]
        H = [self.tile("H", [KC, TT], BF16) for _ in range(2)]
        YA = [self.tile("YA", [4, TT], BF16) for _ in range(2)]
        YBt = [self.tile("YBt", [4, TT], BF16) for _ in range(2)]
        MER = self.tile("MER", [KC, TT], BF16)
        SG = [self.tile("SG", [TT], F32) for _ in range(2)]
        TA = self.tile("TA", [TT], F32)
        TB = self.tile("TB", [TT], F32)
        psA, psB, psMA, psMB = self.psum[0], self.psum[1], self.psum[2], self.psum[3]
        psY = [self.psum[4], self.psum[5]]
        hb, yab, ybb = self.db("hT", NCH), self.db("yaT")[0], self.db("ybT")[0]

        def load(c):
            i = c % 2
            ts = slice(c * TT, (c + 1) * TT)
            p.dma("sp", X[i].ap, xin_v[:, :, ts], [xin_b[c]], [X[i].buf], "X%d" % i)
            p.dma("act", H[i].ap, hT_v[:, :, ts], [hb[c]], [H[i].buf], "H%d" % i)
            p.dma("sp", YA[i].ap, ya_v[:, :, ts], [yab], [YA[i].buf], "YA%d" % i)
            p.dma("act", YBt[i].ap, yb_v[:, :, ts], [ybb], [YBt[i].buf], "YB%d" % i)

        load(0)
        for c in range(NCH):
            i = c % 2
            x, h, ya, yb = X[i], H[i], YA[i], YBt[i]
            if c + 1 < NCH:
                load(c + 1)
            for dc in range(KC):
                ds_ = slice(dc * 128, (dc + 1) * 128)
                for kc in range(KC):
                    self.mm(psMA, psMA.ap, WMG.ap[:, kc, dc * 128:(dc + 1) * 128], h.ap[:, kc, :], kc == 0,
                            kc == KC - 1, [WMG.buf, h.buf])
                p.act(lambda e: e.activation(out=SG[0].ap, in_=psMA.ap, func=AF.Sigmoid), [psMA.buf], [SG[0].buf])
                for kc in range(KC):
                    self.mm(psMB, psMB.ap, WMG.ap[:, kc, 1024 + dc * 128:1024 + (dc + 1) * 128], h.ap[:, kc, :],
                            kc == 0, kc == KC - 1, [WMG.buf, h.buf])
                p.act(lambda e: e.activation(out=SG[1].ap, in_=psMB.ap, func=AF.Sigmoid), [psMB.buf], [SG[1].buf])
                for k in range(4):
                    self.mm(psA, psA.ap, WA.ap[:, k, ds_], ya.ap[:, k, :], k == 0, k == 3, [WA.buf, ya.buf])
                for k in range(4):
                    self.mm(psB, psB.ap, WB.ap[:, k, ds_], yb.ap[:, k, :], k == 0, k == 3, [WB.buf, yb.buf])
                p.dve(lambda e: e.tensor_tensor(out=TA.ap, in0=SG[0].ap, in1=psA.ap, op=ALU.mult),
                      [SG[0].buf, psA.buf], [TA.buf])
                p.dve(lambda e: e.tensor_tensor(out=TB.ap, in0=SG[1].ap, in1=psB.ap, op=ALU.mult),
                      [SG[1].buf, psB.buf], [TB.buf])
                p.pool(lambda e, dc=dc: e.tensor_tensor(out=MER.ap[:, dc, :], in0=TA.ap, in1=TB.ap, op=ALU.add),
                       [TA.buf, TB.buf], [MER.buf])
            for dc in range(KC):
                py = psY[dc % 2]
                for kc in range(KC):
                    self.mm(py, py.ap, WO.ap[:, kc, dc * 128:(dc + 1) * 128], MER.ap[:, kc, :], kc == 0, kc == KC - 1,
                            [WO.buf, MER.buf])
                p.dve(lambda e, py=py, dc=dc, x=x: e.tensor_tensor(out=x.ap[:, dc, :], in0=py.ap, in1=x.ap[:, dc, :],
                                                                   op=ALU.add), [py.buf, x.buf], [x.buf])
            p.dma("sp", xout_v[:, :, c * TT:(c + 1) * TT], x.ap, [x.buf], [xout_b[c]], "XO%d" % i)
        self.phase_end()

C_ID, C_J, C_BO, C_OZ, C_OV = 0, 128, 256, 384, 576
NCST = 576 + 130
LF = 8192 + 2048


WSHAPES = {
    "w_in": (D, NIN),
    "w_branch_a": (512, D),
    "w_branch_b": (512, D),
    "w_out": (D, D),
    "ffn1_w1": (D, DFF),
    "ffn1_w3": (D, DFF),
    "ffn1_w2": (DFF, D),
    "ffn2_w1": (D, DFF),
    "ffn2_w3": (D, DFF),
    "ffn2_w2": (DFF, D),
}

for _l in range(DEPTH):
    for _n in ("ffn1", "mix", "ffn2"):
        Builder.pk_add("g_%s_l%d" % (_n, _l), 8)
    for _n in ("gdq", "gdk", "gnq", "gks", "gkw", "gkc", "gsub"):
        Builder.pk_add("%s_l%d" % (_n, _l), 1)
    Builder.pk_add("lam_l%d" % _l, 256)
    Builder.pk_add("pek_l%d" % _l, 32)
    Builder.pk_add("pev_l%d" % _l, 32)
Builder.pk_add("tb31", 12)

BUCKET_STARTS = [0, 1, 2, 3, 4, 5, 6, 7, 8, 9, 10, 11, 12, 13, 14, 15, 16, 21, 27, 35, 46, 59, 77, 99, 128, 166,
                 216, 280, 363, 470, 609, 790]


def pack_params(inp):
    f = lambda a: np.asarray(a, np.float32)
    cols = []
    bc = lambda v: np.broadcast_to(f(v).reshape(1, -1), (128, f(v).size))
    for l in range(DEPTH):
        for n in ("norm_ffn1", "norm_mix", "norm_ffn2"):
            cols.append(f(inp[n][l]).reshape(8, 128).T)
        t2 = lambda v: np.tile(f(v), 2).reshape(128, 1)
        cols.append(t2(inp["da_q_gain"][l]))
        cols.append(t2(inp["da_k_gain"][l]))
        cols.append(t2(inp["nsa_q_gain"][l]))
        cols.append(t2(inp["nsa_k_gain"][l, 1]))
        cols.append(t2(inp["nsa_k_gain"][l, 2]))
        cols.append(t2(inp["nsa_k_gain"][l, 0]))
        cols.append(f(inp["da_subln_gain"][l]).reshape(128, 1))
        for n in ("da_lambda_q1", "da_lambda_k1", "da_lambda_q2", "da_lambda_k2"):
            cols.append(bc(inp[n][l]))
        for n in ("cmp_pe_k", "cmp_pe_v"):
            cols.append(np.tile(f(inp[n][l]).T, (2, 1)))
    cols.append(bc(inp["rel_bias_table"][31]))
    pk = np.concatenate(cols, axis=1)
    assert pk.shape == (128, Builder.NPK), pk.shape
    return np.ascontiguousarray(pk)


def make_consts(inp):
    cst = np.zeros((128, NCST), np.float32)
    idx = np.arange(128)
    cst[idx, C_ID + idx] = 1.0
    cst[idx, C_J + 127 - idx] = 1.0
    cst[:64, C_BO:C_BO + 64] = 1.0
    cst[64:, C_BO + 64:C_BO + 128] = 1.0
    cst[:, C_OZ + 64:C_OZ + 128] = 1.0
    for ct in range(2):
        for pp in range(128):
            c = ct * 128 + pp
            if c >= 255:
                continue
            for j in range(64):
                if 16 * c < 64 * (j + 1) and 16 * c + 31 >= 64 * j:
                    cst[pp, C_OV + ct * 65 + j] = 1.0
            cst[pp, C_OV + ct * 65 + 64] = 1.0
    bs = np.asarray(BUCKET_STARTS)
    oh = np.zeros((33, LF), np.float32)
    i = np.arange(8192)
    dl = i - 4096
    bk = np.searchsorted(bs, np.maximum(dl, 0), side="right") - 1
    oh[np.where(dl < 0, 32, bk), i] = 1.0
    i = np.arange(2048)
    dl = i - 512
    ok = (dl >= 0) & (dl < 512)
    bk = np.searchsorted(bs, np.clip(dl, 0, None), side="right") - 1
    oh[np.where(ok, bk, 32), 8192 + i] = 1.0
    tblx = np.concatenate([np.asarray(inp["rel_bias_table"], np.float32), np.full((1, 12), NEG, np.float32)], 0)
    ebig = np.zeros((128, S), np.float32)
    ebig[np.arange(S) // 64, np.arange(S)] = 1.0
    gs = np.zeros((24, 1536), np.float32)
    for j in range(4):
        for b in range(3):
            for m in range(128):
                gs[(2 * j + m // 64) * 3 + b, (j * 3 + b) * 128 + m] = 1.0
    q = np.arange(S)
    cur = (q // 64)[:, None]
    jj = np.arange(64)[None, :]
    cadd = np.zeros((S, 64), np.float32)
    cadd[(jj == 0) | (jj == cur) | (jj == cur - 1)] = 1e4
    cadd[jj > cur] = -1e4
    return {"cst": cst, "oh": oh, "tblx": np.ascontiguousarray(tblx), "ebig": ebig, "gs": gs, "cadd": cadd}


_CACHE = {}


def build_full():
    b = Builder()
    b.prologue(list(WSHAPES))
    b.scratch()
    xT = [b.dt("xT%d" % i, [D, S], F32) for i in range(3)]
    out = b.dt("out", [S, D], F32, kind="ExternalOutput")
    b.transpose_in(xT[0])
    for l in range(DEPTH):
        b.ffn_phase(l, 1, xT[0], xT[1])
        b.proj_phase(l, xT[1])
        b.cmp_phase(l)
        b.da_phase(l)
        b.nsa_phase(l)
        b.merge_phase(l, xT[1], xT[2])
        b.ffn_phase(l, 2, xT[2], xT[0])
    b.transpose_out(xT[0], out)
    b.p.emit()
    return b


def kernel(**inputs):
    inp = {k: np.asarray(v) for k, v in inputs.items()}
    if "b" not in _CACHE:
        _CACHE["b"] = build_full()
    b = _CACHE["b"]
    shared = {"pk": pack_params(inp)}
    shared.update(make_consts(inp))
    for n in WSHAPES:
        shared[n] = np.ascontiguousarray(inp[n], dtype=np.float32)
    for n in ("cmp_w1_k", "cmp_w2_k", "cmp_w1_v", "cmp_w2_v"):
        shared[n] = np.ascontiguousarray(inp[n], dtype=np.float32)
    x = np.asarray(inp["x"], np.float32)
    maps = []
    for c in range(NCORES):
        m = dict(shared)
        m["x"] = np.ascontiguousarray(x[c])
        maps.append(m)
    res = run_bass_kernel_spmd(b.nc, maps, core_ids=list(range(NCORES)))
    return np.stack([np.asarray(res.results[c]["out"], np.float32) for c in range(NCORES)], 0)
```
